# Optimizing a Trainium2 kernel written in Bass

```python
import math
import jax, jax.numpy as jnp
from jax import lax
import numpy as np

D_MODEL = 1024
BATCH = 4
SEQ = 4096
DEPTH = 1

CHUNK = 64
EPS = 1e-6

SSM_EXPAND = 2
SSM_INNER = SSM_EXPAND * D_MODEL
SSM_HEAD_DIM = 64
SSM_HEADS = SSM_INNER // SSM_HEAD_DIM
SSM_STATE = 128
SSM_GROUPS = 4
CONV_WIDTH = 4
SSM_XBC = SSM_INNER + 2 * SSM_GROUPS * SSM_STATE

RWKV_DIM = D_MODEL
RWKV_HEAD_DIM = 64
RWKV_HEADS = RWKV_DIM // RWKV_HEAD_DIM
DECAY_LORA = 64
ICLR_LORA = 64
GN_EPS = RWKV_HEAD_DIM * 1e-5
RWKV_COLS = 4 * RWKV_DIM + DECAY_LORA + ICLR_LORA

N_BRANCH = 2
IN_COLS = SSM_INNER + SSM_XBC + SSM_HEADS + RWKV_COLS + N_BRANCH * D_MODEL

kernel_name = "hybrid_ssd_rwkv7_gated_merge"


def rms_norm(x, gain, eps=EPS):
    xf = x.astype(jnp.float32)
    y = xf * lax.rsqrt(jnp.mean(xf * xf, axis=-1, keepdims=True) + eps)
    return (y * gain.astype(jnp.float32)).astype(x.dtype)


def group_rms_norm(y, gain, groups, eps=EPS):
    b, s, c = y.shape
    yg = y.reshape(b, s, groups, c // groups)
    yg = yg * lax.rsqrt(jnp.mean(yg * yg, axis=-1, keepdims=True) + eps)
    return yg.reshape(b, s, c) * gain.astype(jnp.float32)


def causal_depthwise_conv(u, w, bias):
    c = u.shape[-1]
    out = lax.conv_general_dilated(
        u, w[:, None, :].astype(u.dtype), window_strides=(1,),
        padding=[(CONV_WIDTH - 1, 0)], dimension_numbers=("NWC", "WIO", "NWC"),
        feature_group_count=c)
    return out + bias.astype(u.dtype)


def ssd_chunked(xdt, a, bm, cm):
    b, s, h, p = xdt.shape
    g, n = bm.shape[2], bm.shape[3]
    j = h // g
    c = s // CHUNK
    l = CHUNK
    X = xdt.reshape(b, c, l, g, j, p)
    Bq = bm.reshape(b, c, l, g, n)
    Cq = cm.reshape(b, c, l, g, n)
    a_cum = jnp.cumsum(a.reshape(b, c, l, g, j), axis=2)

    causal = jnp.tril(jnp.ones((l, l), dtype=bool))
    seg = a_cum[:, :, :, None] - a_cum[:, :, None, :]
    decay = jnp.exp(jnp.where(causal[None, None, :, :, None, None], seg, -jnp.inf))
    scores = jnp.einsum("bclgn,bcsgn->bclsg", Cq, Bq)
    mix = scores[..., None] * decay
    y_diag = jnp.einsum("bclsgj,bcsgjp->bclgjp", mix, X)

    decay_states = jnp.exp(a_cum[:, :, -1:] - a_cum)
    states = jnp.einsum("bclgn,bclgjp->bcgjpn", Bq, X * decay_states[..., None])
    chunk_decay = jnp.exp(a_cum[:, :, -1])

    def chunk_step(carry, inp):
        st, dec = inp
        return carry * dec[..., None, None] + st, carry

    init = jnp.zeros((b, g, j, p, n), jnp.float32)
    _, prev = lax.scan(chunk_step, init,
                       (jnp.moveaxis(states, 1, 0), jnp.moveaxis(chunk_decay, 1, 0)))
    prev = jnp.moveaxis(prev, 0, 1)

    y_off = jnp.einsum("bclgn,bcgjpn->bclgjp", Cq, prev) * jnp.exp(a_cum)[..., None]
    return (y_diag + y_off).reshape(b, s, h, p)


def mamba2_branch(z, xbc, dt_raw, conv_w, conv_b, dt_bias, a_log, d_skip, norm_gain):
    b, s, _ = z.shape
    xbc = jax.nn.silu(causal_depthwise_conv(xbc, conv_w, conv_b)).astype(jnp.float32)
    xs, bm, cm = jnp.split(xbc, [SSM_INNER, SSM_INNER + SSM_GROUPS * SSM_STATE], axis=-1)
    xs = xs.reshape(b, s, SSM_HEADS, SSM_HEAD_DIM)
    bm = bm.reshape(b, s, SSM_GROUPS, SSM_STATE)
    cm = cm.reshape(b, s, SSM_GROUPS, SSM_STATE)
    dt = jax.nn.softplus(dt_raw.astype(jnp.float32) + dt_bias.astype(jnp.float32))
    A = -jnp.exp(a_log.astype(jnp.float32))
    y = ssd_chunked(xs * dt[..., None], dt * A, bm, cm)
    y = y + d_skip.astype(jnp.float32)[:, None] * xs
    y = y.reshape(b, s, SSM_INNER) * jax.nn.silu(z.astype(jnp.float32))
    return group_rms_norm(y, norm_gain, SSM_GROUPS)


def wkv7_scan(r, w, k, v, kk, a):
    b, s, h, d = r.shape

    def step(S, inp):
        r_t, w_t, k_t, v_t, kk_t, a_t = inp
        s_kk = jnp.einsum("bhvk,bhk->bhv", S, kk_t)
        S = (S * w_t[:, :, None, :]
             - s_kk[..., None] * (kk_t * a_t)[:, :, None, :]
             + v_t[..., None] * k_t[:, :, None, :])
        return S, jnp.einsum("bhvk,bhk->bhv", S, r_t)

    xs = tuple(jnp.moveaxis(t, 1, 0) for t in (r, w, k, v, kk, a))
    S0 = jnp.zeros((b, h, d, d), jnp.float32)
    _, y = lax.scan(step, S0, xs)
    return jnp.moveaxis(y, 0, 1)


def rwkv7_branch(rw, mu, w0, w2, a0, a2, k_k, k_a, r_k, gn_gain, gn_bias):
    b, s, _ = rw.shape
    rw = rw.astype(jnp.float32)
    prev = jnp.pad(rw, ((0, 0), (1, 0), (0, 0)))[:, :-1]
    rw = rw + mu.astype(jnp.float32) * (prev - rw)
    r, k, v, g, wd, ad = jnp.split(
        rw, [RWKV_DIM, 2 * RWKV_DIM, 3 * RWKV_DIM, 4 * RWKV_DIM, 4 * RWKV_DIM + DECAY_LORA], axis=-1)
    wlog = -jax.nn.softplus(-(w0.astype(jnp.float32) + jnp.tanh(wd) @ w2.astype(jnp.float32))) - 0.5
    decay = jnp.exp(-jnp.exp(wlog))
    a = jax.nn.sigmoid(a0.astype(jnp.float32) + ad @ a2.astype(jnp.float32))
    hs = (b, s, RWKV_HEADS, RWKV_HEAD_DIM)
    kk = (k * k_k.astype(jnp.float32)).reshape(hs)
    kk = kk / jnp.maximum(jnp.linalg.norm(kk, axis=-1, keepdims=True), 1e-12)
    k = k * (1.0 + (a - 1.0) * k_a.astype(jnp.float32))
    r, k, v, decay, a = (t.reshape(hs) for t in (r, k, v, decay, a))
    y = wkv7_scan(r, decay, k, v, kk, a)
    mean = jnp.mean(y, axis=-1, keepdims=True)
    var = jnp.mean(jnp.square(y - mean), axis=-1, keepdims=True)
    y = (y - mean) * lax.rsqrt(var + GN_EPS)
    y = y.reshape(b, s, RWKV_DIM) * gn_gain.astype(jnp.float32) + gn_bias.astype(jnp.float32)
    bonus = jnp.sum(r * k * r_k.astype(jnp.float32), axis=-1, keepdims=True) * v
    y = y + bonus.reshape(b, s, RWKV_DIM)
    return y * jax.nn.silu(g)


def hybrid_layer(x, pre_gain, w_in, b_gate, conv_w, conv_b, dt_bias, a_log, d_skip,
                 ssm_norm_gain, rwkv_mu, decay_w0, decay_w2, iclr_a0, iclr_a2, k_k, k_a,
                 r_k, gn_gain, gn_bias, w_branch_ssm, w_branch_rwkv, w_out, post_gain):
    h = rms_norm(x, pre_gain)
    p = jnp.einsum("bsd,de->bse", h, w_in)
    o1 = SSM_INNER
    o2 = o1 + SSM_XBC
    o3 = o2 + SSM_HEADS
    o4 = o3 + RWKV_COLS
    z, xbc, dt_raw, rw, gates = jnp.split(p, [o1, o2, o3, o4], axis=-1)

    y_ssm = mamba2_branch(z, xbc, dt_raw, conv_w, conv_b, dt_bias, a_log, d_skip, ssm_norm_gain)
    y_rwkv = rwkv7_branch(rw, rwkv_mu, decay_w0, decay_w2, iclr_a0, iclr_a2, k_k, k_a,
                          r_k, gn_gain, gn_bias)

    gates = jax.nn.sigmoid((gates + b_gate).astype(jnp.float32))
    g_ssm, g_rwkv = jnp.split(gates, 2, axis=-1)
    merged = (g_ssm * jnp.einsum("bsc,cd->bsd", y_ssm.astype(h.dtype), w_branch_ssm)
              + g_rwkv * jnp.einsum("bsc,cd->bsd", y_rwkv.astype(h.dtype), w_branch_rwkv))
    out = jnp.einsum("bsd,de->bse", merged.astype(h.dtype), w_out)
    return x + rms_norm(out, post_gain).astype(x.dtype)


def setup_inputs(seed: int = 0) -> dict:
    key = jax.random.key(seed)
    ks = jax.random.split(key, 25)
    L = DEPTH
    f32 = jnp.float32

    def nrm(k, shape, scale):
        return jax.random.normal(k, shape, f32) * scale

    def unif(k, shape, lo, hi):
        return jax.random.uniform(k, shape, f32, lo, hi)

    x = nrm(ks[0], (BATCH, SEQ, D_MODEL), 1.0)
    pre_gain = 1.0 + nrm(ks[1], (L, D_MODEL), 0.02)
    w_in = nrm(ks[2], (L, D_MODEL, IN_COLS), D_MODEL ** -0.5)
    b_gate = nrm(ks[3], (L, N_BRANCH * D_MODEL), 0.02)
    conv_w = nrm(ks[4], (L, CONV_WIDTH, SSM_XBC), CONV_WIDTH ** -0.5)
    conv_b = nrm(ks[5], (L, SSM_XBC), 0.02)
    dt0 = jnp.exp(unif(ks[6], (L, SSM_HEADS), math.log(1e-3), math.log(1e-1)))
    dt_bias = dt0 + jnp.log(-jnp.expm1(-dt0))
    a_log = jnp.log(unif(ks[7], (L, SSM_HEADS), 1.0, 16.0))
    d_skip = 1.0 + nrm(ks[8], (L, SSM_HEADS), 0.02)
    ssm_norm_gain = 1.0 + nrm(ks[9], (L, SSM_INNER), 0.02)
    rwkv_mu = unif(ks[10], (L, RWKV_COLS), 0.0, 1.0)
    decay_w0 = unif(ks[11], (L, RWKV_DIM), -6.0, 0.0)
    decay_w2 = nrm(ks[12], (L, DECAY_LORA, RWKV_DIM), 0.1 * DECAY_LORA ** -0.5)
    iclr_a0 = nrm(ks[13], (L, RWKV_DIM), 0.1)
    iclr_a2 = nrm(ks[14], (L, ICLR_LORA, RWKV_DIM), 0.5 * ICLR_LORA ** -0.5)
    k_k = 0.85 + nrm(ks[15], (L, RWKV_DIM), 0.02)
    k_a = 1.0 + nrm(ks[16], (L, RWKV_DIM), 0.02)
    r_k = nrm(ks[17], (L, RWKV_HEADS, RWKV_HEAD_DIM), 0.1)
    gn_gain = 1.0 + nrm(ks[18], (L, RWKV_DIM), 0.02)
    gn_bias = nrm(ks[19], (L, RWKV_DIM), 0.02)
    w_branch_ssm = nrm(ks[20], (L, SSM_INNER, D_MODEL), SSM_INNER ** -0.5)
    w_branch_rwkv = nrm(ks[21], (L, RWKV_DIM, D_MODEL), RWKV_DIM ** -0.5)
    w_out = nrm(ks[22], (L, D_MODEL, D_MODEL), D_MODEL ** -0.5)
    post_gain = 1.0 + nrm(ks[23], (L, D_MODEL), 0.02)
    return {"x": x, "pre_gain": pre_gain, "w_in": w_in, "b_gate": b_gate,
            "conv_w": conv_w, "conv_b": conv_b, "dt_bias": dt_bias, "a_log": a_log,
            "d_skip": d_skip, "ssm_norm_gain": ssm_norm_gain, "rwkv_mu": rwkv_mu,
            "decay_w0": decay_w0, "decay_w2": decay_w2, "iclr_a0": iclr_a0,
            "iclr_a2": iclr_a2, "k_k": k_k, "k_a": k_a, "r_k": r_k,
            "gn_gain": gn_gain, "gn_bias": gn_bias, "w_branch_ssm": w_branch_ssm,
            "w_branch_rwkv": w_branch_rwkv, "w_out": w_out, "post_gain": post_gain}


def reference(x, pre_gain, w_in, b_gate, conv_w, conv_b, dt_bias, a_log, d_skip,
              ssm_norm_gain, rwkv_mu, decay_w0, decay_w2, iclr_a0, iclr_a2, k_k, k_a,
              r_k, gn_gain, gn_bias, w_branch_ssm, w_branch_rwkv, w_out, post_gain):
    for layer in range(DEPTH):
        x = hybrid_layer(
            x, pre_gain[layer], w_in[layer], b_gate[layer], conv_w[layer], conv_b[layer],
            dt_bias[layer], a_log[layer], d_skip[layer], ssm_norm_gain[layer],
            rwkv_mu[layer], decay_w0[layer], decay_w2[layer], iclr_a0[layer],
            iclr_a2[layer], k_k[layer], k_a[layer], r_k[layer], gn_gain[layer],
            gn_bias[layer], w_branch_ssm[layer], w_branch_rwkv[layer], w_out[layer],
            post_gain[layer])
    return x
```

```python
import contextlib
import numpy as np
import concourse.bass as bass
import concourse.mybir as mybir
from concourse.alu_op_type import AluOpType as ALU
from concourse.bass_utils import run_bass_kernel_spmd

F32 = mybir.dt.float32
BF16 = mybir.dt.bfloat16
AF = mybir.ActivationFunctionType

CH = 30000


class Prog:
    def __init__(self, nc):
        self.nc = nc
        self.ops = []
        self.last_w = {}
        self.readers = {}
        self.stack = contextlib.ExitStack()

    def sb(self, name, shape, dtype=F32):
        return self.stack.enter_context(self.nc.sbuf_tensor("s_" + name, list(shape), dtype))

    def ps(self, name, shape, dtype=F32):
        return self.stack.enter_context(self.nc.psum_tensor("p_" + name, list(shape), dtype))

    def add(self, eng, fn, reads=(), writes=(), dma=False, semkey=None):
        i = len(self.ops)
        deps = set()
        for k in reads:
            if k in self.last_w:
                deps.add(self.last_w[k])
        for k in writes:
            if k in self.last_w:
                deps.add(self.last_w[k])
            for r in self.readers.get(k, ()):
                deps.add(r)
        deps.discard(i)
        self.ops.append(dict(eng=eng, fn=fn, deps=deps, dma=dma, semkey=semkey))
        for k in reads:
            self.readers.setdefault(k, []).append(i)
        for k in writes:
            self.last_w[k] = i
            self.readers[k] = []
        return i

    def dma(self, eng, fn, reads=(), writes=(), semkey=None):
        return self.add(eng, fn, reads, writes, dma=True, semkey=semkey)

    def emit(self, final_wait_ops=()):
        nc = self.nc
        ops = self.ops
        n = len(ops)
        waited_on = [False] * n
        for i, o in enumerate(ops):
            nd = set()
            for d in o['deps']:
                od = ops[d]
                if (not od['dma']) and (not o['dma']) and od['eng'] == 'pe' and o['eng'] == 'pe':
                    continue
                nd.add(d)
            o['deps'] = nd
            for d in nd:
                waited_on[d] = True
        for i in final_wait_ops:
            waited_on[i] = True
        cnt = {}
        for i, o in enumerate(ops):
            if o['dma']:
                sid = ('dma', o['semkey'])
                cnt[sid] = cnt.get(sid, 0) + 16
                o['ev'] = (sid, cnt[sid])
            elif waited_on[i]:
                e = o['eng']
                m = cnt.get(('m', e), 0)
                cnt[('m', e)] = m + 1
                o['ev'] = (('eng', e, m // CH), (m % CH) + 1)
            else:
                o['ev'] = None
        semids = []
        seen = set()
        for o in ops:
            if o['ev'] is not None and o['ev'][0] not in seen:
                seen.add(o['ev'][0])
                semids.append(o['ev'][0])
        sems = {}
        for k, sid in enumerate(semids):
            sems[sid] = self.stack.enter_context(nc.semaphore("s%d" % k))
        self.n_sems = len(semids)
        per_eng = {e: [] for e in ['pe', 'act', 'dve', 'pool', 'sp']}
        for i, o in enumerate(ops):
            per_eng[o['eng']].append(i)

        def run_engine(ename, eobj):
            waited = {}
            for i in per_eng[ename]:
                o = ops[i]
                need = {}
                for d in o['deps']:
                    sid, v = ops[d]['ev']
                    if sid[0] == 'eng':
                        key = ('eng', sid[1])
                        gv = sid[2] * CH + v
                    else:
                        key = sid
                        gv = v
                    if waited.get(key, 0) >= gv:
                        continue
                    if need.get(key, (None, 0))[1] < gv:
                        need[key] = (sid, gv)
                for key, (sid, gv) in need.items():
                    if sid[0] == 'eng':
                        eobj.wait_ge(sems[sid], gv - sid[2] * CH)
                    else:
                        eobj.wait_ge(sems[sid], gv)
                    waited[key] = gv
                ins = o['fn'](eobj)
                if o['ev'] is not None:
                    ins.then_inc(sems[o['ev'][0]], 16 if o['dma'] else 1)
            if ename == 'sp':
                for i in final_wait_ops:
                    sid, v = ops[i]['ev']
                    eobj.wait_ge(sems[sid], v)

        with nc.Block() as block:
            @block.tensor
            def _(e):
                run_engine('pe', e)

            @block.scalar
            def _(e):
                run_engine('act', e)

            @block.vector
            def _(e):
                run_engine('dve', e)

            @block.gpsimd
            def _(e):
                run_engine('pool', e)

            @block.sync
            def _(e):
                run_engine('sp', e)
        self.stack.close()


D = 1024
T = 4096
TB = 128
NBLK = T // TB
INC = 11424
EPS = 1e-6
GN_EPS = 64 * 1e-5
LWS = -float(np.exp(-0.5))

O_Z, O_X, O_B, O_C, O_DT = 0, 2048, 4096, 4608, 5120
O_R, O_K, O_V, O_G, O_WA = 5152, 6176, 7200, 8224, 9248
O_GS, O_GR = 9376, 10400

PC = {}
_o = 0
for _n, _w in [('pre', 8), ('post', 8), ('cw', 96), ('cb', 24), ('dsk', 16), ('sng', 16), ('mu', 33),
               ('w0', 8), ('a0', 8), ('kk', 8), ('ka', 8), ('rk', 8), ('gng', 8), ('gnb', 8), ('bg', 16),
               ('dtb', 1), ('alog', 1)]:
    PC[_n] = _o
    _o += _w
NPAR = _o
DC = {}
_o = 0
for _n, _w in [('omm', 33), ('omka', 8), ('aneg', 1)]:
    DC[_n] = _o
    _o += _w
NDER = _o

CC = {}
_o = 0
for _n, _w in [('ident', 128), ('incl', 128), ('strict', 128), ('ones', 128), ('blk', 128), ('m4', 128),
               ('low', 64), ('rmask', 128)]:
    CC[_n] = _o
    _o += _w
NCST = _o


def make_consts():
    c = np.zeros((128, NCST), np.float32)
    i = np.arange(128)
    c[:, CC['ident']:CC['ident'] + 128] = np.eye(128)
    c[:, CC['incl']:CC['incl'] + 128] = (i[:, None] <= i[None, :])
    c[:, CC['strict']:CC['strict'] + 128] = (i[:, None] > i[None, :])
    c[:, CC['ones']:CC['ones'] + 128] = 1.0
    c[:, CC['blk']:CC['blk'] + 128] = ((i[:, None] // 64) == (i[None, :] // 64))
    s = i[:, None] % 64
    t = i[None, :] % 64
    u = i[None, :] // 64
    c[:, CC['m4']:CC['m4'] + 128] = np.where(u == 0, s < t, s <= t)
    j = np.arange(64)
    c[:64, CC['low']:CC['low'] + 64] = (j[None, :] < j[:, None])
    c[64:, CC['low']:CC['low'] + 64] = (j[None, :] < j[:, None])
    c[:, CC['rmask']:CC['rmask'] + 128] = (np.arange(128)[None, :] % 64 != 0)
    return c


def build(nblk=NBLK, dbg=None, stage=9):
    nc = bass.Bass("TRN2", target_bir_lowering=False)
    Tn = nblk * TB
    x_d = nc.dram_tensor("x", [Tn, D], F32, kind="ExternalInput").ap()
    win_d = nc.dram_tensor("w_in", [D, INC], F32, kind="ExternalInput").ap()
    par_d = nc.dram_tensor("par", [128, NPAR], F32, kind="ExternalInput").ap()
    cst_d = nc.dram_tensor("cst", [128, NCST], F32, kind="ExternalInput").ap()
    w2a2_d = nc.dram_tensor("w2a2", [128, D], F32, kind="ExternalInput").ap()
    wssm_d = nc.dram_tensor("w_ssm", [2048, D], F32, kind="ExternalInput").ap()
    wrw_d = nc.dram_tensor("w_rwkv", [D, D], F32, kind="ExternalInput").ap()
    wout_d = nc.dram_tensor("w_out", [D, D], F32, kind="ExternalInput").ap()
    out_d = nc.dram_tensor("out", [Tn, D], F32, kind="ExternalOutput").ap()
    NCHK = 90
    win_bf = nc.dram_tensor("win_bf", [NCHK, 128, 8, 128], BF16, kind="Internal").ap()
    wssm_bf = nc.dram_tensor("wssm_bf", [8, 128, 16, 128], BF16, kind="Internal").ap()
    wrw_bf = nc.dram_tensor("wrw_bf", [8, 128, 8, 128], BF16, kind="Internal").ap()
    wout_bf = nc.dram_tensor("wout_bf", [8, 128, 8, 128], BF16, kind="Internal").ap()
    dbg_d = None
    if dbg is not None:
        dbg_d = nc.dram_tensor("dbg", [nblk, 128, dbg], F32, kind="ExternalOutput").ap()

    P = Prog(nc)
    sb = P.sb

    def act(out, in_, func, r, w, **kw):
        P.add('act', lambda e: e.activation(out=out, in_=in_, func=func, **kw), r, w)

    def tt(eng, out, in0, in1, op, r, w):
        P.add(eng, lambda e: e.tensor_tensor(out=out, in0=in0, in1=in1, op=op), r, w)

    def ts(eng, out, in0, s1, s2, op0, op1, r, w):
        if op1 is None:
            P.add(eng, lambda e: e.tensor_scalar(out=out, in0=in0, scalar1=s1, scalar2=None, op0=op0), r, w)
        else:
            P.add(eng, lambda e: e.tensor_scalar(out=out, in0=in0, scalar1=s1, scalar2=s2, op0=op0, op1=op1), r, w)

    def stt(out, in0, scalar, in1, op0, op1, r, w):
        P.add('dve', lambda e: e.scalar_tensor_tensor(out=out, in0=in0, scalar=scalar, in1=in1, op0=op0, op1=op1), r, w)

    def mm(out, lhsT, rhs, start, stop, r, w):
        P.add('pe', lambda e: e.matmul(out, lhsT=lhsT, rhs=rhs, start=start, stop=stop), r, w)

    def tr(out, in_, idn, r, w):
        P.add('pe', lambda e: e.transpose(out=out, in_=in_, identity=idn), r, w)

    def cp(eng, out, in_, r, w):
        if eng == 'act':
            P.add(eng, lambda e: e.activation(out=out, in_=in_, func=AF.Copy), r, w)
        else:
            P.add(eng, lambda e: e.tensor_copy(out=out, in_=in_), r, w)

    def recip(out, in_, r, w):
        P.add('dve', lambda e: e.reciprocal(out=out, in_=in_), r, w)

    def ld(out, in_, key, reads=()):
        return P.dma('sp', lambda e: e.dma_start(out=out, in_=in_), reads=list(reads), writes=[key], semkey=key)

    par = sb("par", [128, NPAR])
    der = sb("der", [128, NDER])
    cst = sb("cst", [128, NCST])
    w2a2 = sb("w2a2", [128, D])
    ld(par[:], par_d, 'par')
    ld(cst[:], cst_d, 'cst')
    ld(w2a2[:], w2a2_d, 'w2a2')

    def pcol(n, i=0, p0=0, p1=128):
        return par[p0:p1, PC[n] + i:PC[n] + i + 1]

    def dcol(n, i=0, p0=0, p1=128):
        return der[p0:p1, DC[n] + i:DC[n] + i + 1]

    def cm(n, w=128, p0=0, p1=128):
        return cst[p0:p1, CC[n]:CC[n] + w]

    ts('dve', der[:, DC['omm']:DC['omm'] + 33], par[:, PC['mu']:PC['mu'] + 33], -1.0, 1.0, ALU.mult, ALU.add, ['par'], ['der'])
    ts('dve', der[:, DC['omka']:DC['omka'] + 8], par[:, PC['ka']:PC['ka'] + 8], -1.0, 1.0, ALU.mult, ALU.add, ['par'], ['der'])
    act(dcol('aneg', 0, 0, 32), pcol('alog', 0, 0, 32), AF.Exp, ['par'], ['der'])
    ts('dve', dcol('aneg', 0, 0, 32), dcol('aneg', 0, 0, 32), -1.0, None, ALU.mult, None, ['der'], ['der'])

    PB = [P.ps("pb%d" % b, [128, 512]) for b in range(8)]

    def bk(b, qs=None, halves=(0, 1)):
        return ['B%dh%d' % (b, h) for h in halves]

    def pq(b, q, p0=0, p1=128, w=128):
        return PB[b][p0:p1, q * 128:q * 128 + w]

    hist_c = sb("hist_c", [128, 24, 3])
    hist_r = sb("hist_r", [128, 33, 1])
    Sst = sb("Sst", [128, 2048])
    SR = sb("SR", [128, 8, 64])
    P.add('pool', lambda e: e.memset(hist_c[:], 0.0), [], ['hc%d' % c for c in range(24)])
    P.add('pool', lambda e: e.memset(hist_r[:], 0.0), [], ['hr%d' % c for c in range(33)])
    P.add('pool', lambda e: e.memset(Sst[:], 0.0), [], ['S%d' % g for g in range(4)])
    P.add('pool', lambda e: e.memset(SR[:], 0.0), [], ['SR%d_%d' % (c, h) for c in range(8) for h in range(2)])

    arena = sb("arena", [128, 20480])
    _ao = [0]

    def carve(n, pat=None, **kw):
        v = arena[:, _ao[0]:_ao[0] + n]
        _ao[0] += n
        assert _ao[0] <= 20480
        return v.rearrange(pat, **kw) if pat else v

    xt = [sb("xt%d" % i, [128, D]) for i in range(2)]
    xn = sb("xn", [128, D])
    ssq = sb("ssq", [128, 1])
    rstd = sb("rstd", [128, 1])
    hT = sb("hT", [128, 8, TB], BF16)
    NW = 6
    wt = [sb("wt%d" % i, [128, 8, 128], BF16) for i in range(NW)]
    _ao[0] = 0
    xbc = carve(3072, "p (c t) -> p c t", c=24)
    zs = carve(2048, "p (c t) -> p c t", c=16)
    raw = [sb("raw%d" % i, [128, 3 + TB]) for i in range(8)]
    acc = [sb("acc%d" % i, [128, TB]) for i in range(4)]
    dtT = sb("dtT", [32, TB])
    aT = sb("aT", [32, TB])
    dta = sb("dta", [128, 64])
    Xdt = carve(2048)
    Xds = carve(2048)
    rhsa_f = carve(4096)
    rhsa = rhsa_f.bitcast(BF16)[:, 0:4096].rearrange("p (h l) -> p h l", h=32)
    dstok = sb("dstok", [128, 32])
    dec = [carve(512, "p (h l) -> p h l", h=4) for i in range(2)]
    mix = [carve(512, "p (h l) -> p h l", h=4) for i in range(2)]
    Eg = [carve(512, "p (h l) -> p h l", h=4) for i in range(2)]
    Cdec = [carve(512, "p (h l) -> p h l", h=4) for i in range(2)]
    t1 = [sb("t1_%d" % i, [128, TB]) for i in range(2)]
    yz = carve(2048, "p (c t) -> p c t", c=16)
    Btok = carve(512, "p (g n) -> p g n", g=4)
    scm = carve(512, "p (g n) -> p g n", g=4)
    sqb = [sb("sqb%d" % i, [128, TB], BF16) for i in range(2)]
    rsb = sb("rsb", [128, TB])
    yn = sb("yn", [128, 16, TB], BF16)
    rawr = [sb("rawr%d" % i, [128, 1 + TB]) for i in range(8)]
    tmpr = [sb("tmpr%d" % i, [128, TB]) for i in range(4)]
    _ao[0] = 0
    sh = carve(33 * 128, "p (c t) -> p c t", c=33)
    lw = carve(1024, "p (c t) -> p c t", c=8)
    av = carve(1024, "p (c t) -> p c t", c=8)
    kkb = carve(1024, "p (c t) -> p c t", c=8)
    kp = carve(1024, "p (c t) -> p c t", c=8)
    bon = carve(1024, "p (c t) -> p c t", c=8)
    EP = carve(1024, "p (c t) -> p c t", c=8)
    NTMP = 10
    tmpb = [[sb("tm%d_%d" % (k, i), [128, TB]) for i in range(2)] for k in range(NTMP)]
    RA = carve(2048, "p (c j t) -> p c j t", c=8, j=2)
    BKt = carve(3072, "p (c j t) -> p c j t", c=8, j=2)
    BKh = carve(3072, "p (c j t) -> p c j t", c=8, j=2)
    VV = sb("VV", [128, 8, 2, 192])
    NAM = 4
    BKtok = [sb("BKtok%d" % i, [128, 64]) for i in range(NAM)]
    UV = [sb("UV%d" % i, [128, 64]) for i in range(NAM)]
    AM = [sb("AM%d" % i, [128, 128]) for i in range(NAM)]
    Pm = [[sb("Pm%d_%d" % (i, k), [128, 64]) for k in range(2)] for i in range(NAM)]
    Ptm = [[sb("Ptm%d_%d" % (i, k), [128, 64]) for k in range(2)] for i in range(NAM)]
    Rt = [[sb("Rt%d_%d" % (i, k), [128, 64]) for k in range(2)] for i in range(NAM)]
    X1s = [sb("X1s%d" % i, [128, 64]) for i in range(NAM)]
    ybuf = carve(1024, "p (c t) -> p c t", c=8)
    yr = sb("yr", [128, 8, TB], BF16)
    _ao[0] = 13312
    wob = sb("wob", [128, 4, 8, 128], BF16)
    mT = sb("mT", [128, 8, TB], BF16)
    oT = carve(1024, "p (c t) -> p c t", c=8)
    on = carve(1024, "p (c t) -> p c t", c=8)
    G = carve(2048, "p (c t) -> p c t", c=16)
    wrwS = sb("wrwS", [128, 8, 8, 128], BF16)
    bar = sb("bar", [128, 1])
    K_SSD = (['xbc%d' % c for c in range(24)] + ['zs%d' % c for c in range(16)] + ['Xdt%d' % c for c in range(4)]
             + ['Xds%d' % c for c in range(4)] + ['rhsa', 'Btok', 'scm'] + ['yz%d' % c for c in range(16)]
             + ['%s%d' % (n, i) for n in ('dec', 'mix', 'Eg', 'Cdec') for i in range(2)])
    K_RW = (['sh%d' % c for c in range(33)] + ['%s%d' % (n, c) for n in ('lw', 'av', 'kkb', 'kp', 'bon', 'EP', 'RA', 'BKt', 'BKh', 'ybuf')
                                                for c in range(8)])
    K_OUT = (['%s%d' % (n, c) for n in ('oT', 'on') for c in range(8)] + ['G%d' % c for c in range(16)])

    K_SSD_E = ['xbc%d' % c for c in range(24)] + ['zs%d' % c for c in range(16)] + ['rhsa']
    K_SSD_R = [k for k in K_SSD if k not in K_SSD_E]

    def barrier(old, new):
        P.add('pool', lambda e: e.memset(bar[:], 0.0), [], ['bar'] + old + new)

    P.add('pool', lambda e: e.memset(VV[:], 0.0), [], ['VV%d' % c for c in range(8)])

    ident = cm('ident')
    cstb = sb("cstb", [128, 384], BF16)
    cp('dve', cstb[:, 0:128], cm('strict'), ['cst'], ['cstb'])
    cp('dve', cstb[:, 128:256], cm('ones'), ['cst'], ['cstb'])
    cp('dve', cstb[:, 256:384], cm('blk'), ['cst'], ['cstb'])
    cnt = {'raw': 0, 'acc': 0, 'rawr': 0, 'wo': 0, 'am': 0, 'tmpr': 0}
    final_ops = []

    def roundrobin(gens):
        gens = list(gens)
        while gens:
            nxt = []
            for g in gens:
                try:
                    next(g)
                    nxt.append(g)
                except StopIteration:
                    pass
            gens = nxt

    def consumer_z(c):
        def f(pp, pk):
            act(zs[:, c, :], pp, AF.Silu, pk, ['zs%d' % c])
            yield
        return f

    def consumer_xbc(c):
        def f(pp, pk):
            ri = cnt['raw'] % 8
            cnt['raw'] += 1
            r_ = raw[ri]
            rk_ = 'raw%d' % ri
            act(r_[:, 3:3 + TB], pp, AF.Copy, pk, [rk_])
            cp('pool', r_[:, 0:3], hist_c[:, c, :], ['hc%d' % c], [rk_])
            yield
            ai = cnt['acc'] % 4
            cnt['acc'] += 1
            a_ = acc[ai]
            ak_ = 'acc%d' % ai
            wc = PC['cw'] + c * 4
            ts('dve', a_[:], r_[:, 0:TB], par[:, wc:wc + 1], pcol('cb', c), ALU.mult, ALU.add, [rk_, 'par'], [ak_])
            yield
            for k in range(1, 4):
                stt(a_[:], r_[:, k:k + TB], par[:, wc + k:wc + k + 1], a_[:], ALU.mult, ALU.add, [rk_, ak_, 'par'], [ak_])
                yield
            cp('pool', hist_c[:, c, :], r_[:, TB:TB + 3], [rk_], ['hc%d' % c])
            act(xbc[:, c, :], a_[:], AF.Silu, [ak_], ['xbc%d' % c])
            yield
        return f

    def consumer_dt(pp, pk):
        act(dtT[:], pp, AF.Exp, pk + ['par'], ['dtT'], bias=pcol('dtb', 0, 0, 32))
        act(dtT[:], dtT[:], AF.Ln, ['dtT'], ['dtT'], bias=1.0)
        ts('dve', aT[:], dtT[:], dcol('aneg', 0, 0, 32), None, ALU.mult, None, ['dtT', 'der'], ['aT'])
        id32 = cst[0:32, CC['ident']:CC['ident'] + 32]
        tr(PB[2][:, 0:32], dtT[:], id32, ['dtT', 'cst'], bk(2, [0]))
        tr(PB[2][:, 32:64], aT[:], id32, ['aT', 'cst'], bk(2, [0]))
        cp('dve', dta[:], PB[2][:, 0:64], bk(2, [0]), ['dta'])
        yield
        mm(PB[3][:, 0:32], cm('strict'), dta[:, 32:64], True, True, ['cst', 'dta'], bk(3, [0]))
        act(dstok[:], PB[3][:, 0:32], AF.Exp, bk(3, [0]), ['dstok'])
        tt('pool', rhsa[:], dta[:, 32:64].unsqueeze(2).to_broadcast([128, 32, 128]),
           cm('incl').unsqueeze(1).to_broadcast([128, 32, 128]), ALU.mult, ['dta', 'cst'], ['rhsa'])
        yield

    def consumer_rw(idx):
        def f(pp, pk):
            ri = cnt['rawr'] % 8
            cnt['rawr'] += 1
            r_ = rawr[ri]
            rk_ = 'rawr%d' % ri
            act(r_[:, 1:1 + TB], pp, AF.Copy, pk, [rk_])
            cp('pool', r_[:, 0:1], hist_r[:, idx, :], ['hr%d' % idx], [rk_])
            yield
            ti_ = cnt['tmpr'] % 4
            cnt['tmpr'] += 1
            t_ = tmpr[ti_]
            tk_ = 'tmpr%d' % ti_
            ts('dve', t_[:], r_[:, 1:1 + TB], dcol('omm', idx), None, ALU.mult, None, [rk_, 'der'], [tk_])
            yield
            stt(sh[:, idx, :], r_[:, 0:TB], pcol('mu', idx), t_[:], ALU.mult, ALU.add, [rk_, tk_, 'par'], ['sh%d' % idx])
            cp('pool', hist_r[:, idx, :], r_[:, TB:TB + 1], [rk_], ['hr%d' % idx])
            yield
        return f

    def consumer_gate(i):
        def f(pp, pk):
            act(G[:, i, :], pp, AF.Sigmoid, pk + ['par'], ['G%d' % i], bias=pcol('bg', i))
            yield
        return f

    glist = []
    glist.append((O_DT, 32, consumer_dt))
    for c in range(16):
        glist.append((O_X + c * 128, 128, consumer_xbc(c)))
    for g in range(4):
        glist.append((O_B + g * 128, 128, consumer_xbc(16 + g)))
    for g in range(4):
        glist.append((O_C + g * 128, 128, consumer_xbc(20 + g)))
    for c in range(16):
        glist.append((O_Z + c * 128, 128, consumer_z(c)))
    n_ssm_groups = len(glist)
    glist.append((O_WA, 128, consumer_rw(32)))
    for c in range(8):
        for i, o in enumerate([O_R, O_K, O_V, O_G]):
            glist.append((o + c * 128, 128, consumer_rw(i * 8 + c)))
    n_rw_groups = len(glist)
    for i in range(16):
        glist.append((O_GS + i * 128, 128, consumer_gate(i)))
    NG = len(glist)
    if stage <= 1:
        g_end = n_ssm_groups
    elif stage == 2:
        g_end = n_rw_groups
    else:
        g_end = NG
    sched = [(b, g) for b in range(nblk) for g in range(g_end)]
    pos = {bg: i for i, bg in enumerate(sched)}

    def issue_w(si):
        if si >= len(sched):
            return
        col0, M, _ = glist[sched[si][1]]
        s = si % NW
        ch = chunk_of(col0)
        if M == 128:
            ld(wt[s][:, :, :], win_bf[ch], 'wt%d' % s, reads=['wbf'])
        else:
            ld(wt[s][:, :, 0:M], win_bf[ch, :, :, 0:M], 'wt%d' % s, reads=['wbf'])

    def chunk_of(col0):
        if col0 < O_DT:
            return col0 // 128
        if col0 == O_DT:
            return 40
        return 41 + (col0 - O_R) // 128

    GL = 4
    INTERLEAVE = False

    def run_groups(blk, g0, g1):
        for _ in run_groups_gen(blk, g0, g1):
            pass

    def run_groups_gen(blk, g0, g1):
        g = g0
        pending = []
        while g < g1:
            gens = []
            for gg in range(g, min(g + GL, g1)):
                si = pos[(blk, gg)]
                issue_w(si + NW - 1)
                col0, M, cons = glist[gg]
                s = si % NW
                q = [0, 1, 4, 5][si % 4]
                pp = PB[q][0:M, 0:TB]
                pk = bk(q)
                for kc in range(8):
                    mm(pp, wt[s][:, kc, 0:M], hT[:, kc, :], kc == 0, kc == 7, ['wt%d' % s, 'hT%d' % kc], pk)
                gens.append(cons(pp, pk))
            live = []
            for gn in gens:
                try:
                    next(gn)
                    live.append(gn)
                except StopIteration:
                    pass
            roundrobin(pending)
            pending = live
            g += GL
            yield
        roundrobin(pending)
        yield

    NB = 4
    stg = [arena[:, i * 2048:(i + 1) * 2048] for i in range(NB)]
    stb = [arena[:, 8192 + i * 1024:8192 + (i + 1) * 1024].bitcast(BF16) for i in range(NB)]
    wst_keys = []
    slabs = []

    def convert(src, nrc, segs, dst):
        for rc in range(nrc):
            for (c0, w, ch0) in segs:
                slabs.append((src, rc, c0, w, ch0, dst))

    segs_in = [(0, 2048, 0), (2048, 2048, 16), (4096, 1024, 32), (O_DT, 32, 40), (O_R, 2048, 41), (O_R + 2048, 2048, 57),
               (O_R + 4096, 2048, 73), (O_R + 6144, 128, 89)]
    convert(win_d, 8, segs_in, win_bf)
    convert(wssm_d, 16, [(0, 1024, 0)], wssm_bf)
    convert(wrw_d, 8, [(0, 1024, 0)], wrw_bf)
    convert(wout_d, 8, [(0, 1024, 0)], wout_bf)

    def slab_ld(n):
        src, rc, c0, w, ch0, dst = slabs[n]
        i = n % NB
        P.dma('sp', lambda e: e.dma_start(out=stg[i][:, 0:w], in_=src[rc * 128:(rc + 1) * 128, c0:c0 + w]),
              writes=['stg%d' % i], semkey='stg%d' % i)

    def slab_cast_st(n):
        src, rc, c0, w, ch0, dst = slabs[n]
        i = n % NB
        eng = ['act', 'dve', 'pool'][n % 3]
        cp(eng, stb[i][:, 0:w], stg[i][:, 0:w], ['stg%d' % i], ['stb%d' % i])
        if w >= 128:
            nch = w // 128
            d_ap = dst[ch0:ch0 + nch, :, rc, :].rearrange("c p m -> p c m")
            s_ap = stb[i][:, 0:w].rearrange("p (c m) -> p c m", m=128)
        else:
            d_ap = dst[ch0, :, rc, 0:w]
            s_ap = stb[i][:, 0:w]
        k = 'wst%d' % n
        wst_keys.append(k)
        P.dma('sp', lambda e: e.dma_start(out=d_ap, in_=s_ap), reads=['stb%d' % i], writes=[k], semkey='wost%d' % i)

    for n in range(min(NB, len(slabs))):
        slab_ld(n)
    for n in range(len(slabs)):
        slab_cast_st(n)
        if n + NB < len(slabs):
            slab_ld(n + NB)
    P.add('pool', lambda e: e.memset(bar[:], 0.0), wst_keys, ['wbf', 'bar'])
    barrier(['stg%d' % i for i in range(NB)] + ['stb%d' % i for i in range(NB)], K_SSD)
    ld(wrwS[:], wrw_bf.rearrange("o p c m -> p o c m"), 'wrwS', reads=['wbf'])

    for si in range(NW - 1):
        issue_w(si)

    def dump(blk, off, ap, keys, width):
        i = P.dma('sp', lambda e: e.dma_start(out=dbg_d[blk, 0:ap.shape[0], off:off + width], in_=ap), reads=keys,
                  semkey='dbg%d_%d' % (blk, off))
        final_ops.append(i)

    def z_group(blk, g):
        si = pos[(blk, g)]
        issue_w(si + NW - 1)
        col0, M, cons = glist[g]
        s_ = si % NW
        pp = PB[3][0:M, 0:TB]
        pk = bk(3)
        for kc in range(8):
            mm(pp, wt[s_][:, kc, 0:M], hT[:, kc, :], kc == 0, kc == 7, ['wt%d' % s_, 'hT%d' % kc], pk)
        for _ in cons(pp, pk):
            pass

    def ssd_norm_gen(g):
        bn = [2, 3, 6, 7][g]
        sq = t1[g // 2][:].bitcast(BF16)[:, (g % 2) * TB:(g % 2 + 1) * TB]
        sqk = 't1_%d_%d' % (g // 2, g % 2)
        rs = tmpb[8 + g // 2][g % 2]
        rsk = 'tm%d_%d' % (8 + g // 2, g % 2)
        for q in range(4):
            c = g * 4 + q
            tt('pool', yz[:, c, :], yz[:, c, :], zs[:, c, :], ALU.mult, ['yz%d' % c, 'zs%d' % c], ['yz%d' % c])
            yield
        for q in range(4):
            c = g * 4 + q
            act(sq, yz[:, c, :], AF.Square, ['yz%d' % c], [sqk])
            mm(pq(bn, 0), cstb[:, 128:256], sq, q == 0, q == 3, [sqk, 'cstb'], bk(bn))
            yield
        act(rs[:], pq(bn, 0), AF.Ln, bk(bn), [rsk], scale=1.0 / 512, bias=EPS)
        yield
        act(rs[:], rs[:], AF.Exp, [rsk], [rsk], scale=-0.5)
        yield
        for q in range(4):
            c = g * 4 + q
            stt(yn[:, c, :], yz[:, c, :], pcol('sng', c), rs[:], ALU.mult, ALU.mult, ['yz%d' % c, rsk, 'par'], ['yn%d' % c])
            yield

    def ssd_core(blk):
        for c4 in range(4):
            for q in range(4):
                c = c4 * 4 + q
                tr(pq(2, q), xbc[:, c, :], ident, ['xbc%d' % c, 'cst'], bk(2, [q]))
            tt('dve', Xdt[:, c4 * 512:(c4 + 1) * 512].rearrange("p (h d) -> p h d", h=8),
               PB[2][:, :].rearrange("p (h d) -> p h d", h=8),
               dta[:, c4 * 8:(c4 + 1) * 8].unsqueeze(2).to_broadcast([128, 8, 64]), ALU.mult, bk(2) + ['dta'], ['Xdt%d' % c4])
        for g in range(4):
            tr(pq(2, g), xbc[:, 16 + g, :], ident, ['xbc%d' % (16 + g), 'cst'], bk(2, [g]))
        act(Btok[:].rearrange("p g n -> p (g n)"), PB[2][:, :], AF.Copy, bk(2), ['Btok'])
        for c4 in range(4):
            tt('dve', Xds[:, c4 * 512:(c4 + 1) * 512].rearrange("p (h d) -> p h d", h=8),
               Xdt[:, c4 * 512:(c4 + 1) * 512].rearrange("p (h d) -> p h d", h=8),
               dstok[:, c4 * 8:(c4 + 1) * 8].unsqueeze(2).to_broadcast([128, 8, 64]), ALU.mult, ['Xdt%d' % c4, 'dstok'], ['Xds%d' % c4])
        for g in range(4):
            mm(pq(3, g), xbc[:, 16 + g, :], xbc[:, 20 + g, :], True, True, ['xbc%d' % (16 + g), 'xbc%d' % (20 + g)], bk(3, [g]))
        tt('dve', scm[:], PB[3][:, :].rearrange("p (g l) -> p g l", g=4), cm('incl').unsqueeze(1).to_broadcast([128, 4, 128]),
           ALU.mult, bk(3) + ['cst'], ['scm'])
        halves = [(g, hf) for g in range(4) for hf in range(2)]

        def stage1(i):
            g, hf = halves[i]
            u = i % 2
            h0 = g * 8 + hf * 4
            bs, be = (4, 5) if u == 0 else (0, 1)
            rv = rhsa[:, h0:h0 + 4, :].rearrange("p h l -> p (h l)")
            mm(PB[bs][:, :], cstb[:, 0:128], rv, True, True, ['cstb', 'rhsa'], bk(bs))
            mm(PB[be][:, :], cstb[:, 128:256], rv, True, True, ['cstb', 'rhsa'], bk(be))
            act(dec[u][:].rearrange("p h l -> p (h l)"), PB[bs][:, :], AF.Exp, bk(bs), ['dec%d' % u])
            act(Eg[u][:].rearrange("p h l -> p (h l)"), PB[be][:, :], AF.Exp, bk(be), ['Eg%d' % u])
            tt('dve', mix[u][:], dec[u][:], scm[:, g, :].unsqueeze(1).to_broadcast([128, 4, 128]), ALU.mult,
               ['dec%d' % u, 'scm'], ['mix%d' % u])
            tt('pool', Cdec[u][:], Eg[u][:], xbc[:, 20 + g, :].unsqueeze(1).to_broadcast([128, 4, 128]), ALU.mult,
               ['Eg%d' % u, 'xbc%d' % (20 + g)], ['Cdec%d' % u])

        def stage2(i):
            g, hf = halves[i]
            u = i % 2
            h0 = g * 8 + hf * 4
            by = 6 if g % 2 == 0 else 2
            for j in range(4):
                h = h0 + j
                c = h // 2
                half = h % 2
                q = c % 4
                yo = pq(by, q, half * 64, (half + 1) * 64)
                mm(yo, Xdt[:, h * 64:(h + 1) * 64], mix[u][:, j, :], True, False, ['Xdt%d' % (h // 8), 'mix%d' % u], bk(by))
                mm(yo, Sst[:, h * 64:(h + 1) * 64], Cdec[u][:, j, :], False, True, ['S%d' % g, 'Cdec%d' % u], bk(by))
            mm(PB[7][:, 0:256], Btok[:, g, :], Xds[:, h0 * 64:(h0 + 4) * 64], True, True, ['Btok', 'Xds%d' % g], bk(7))
            sv = Sst[:, h0 * 64:(h0 + 4) * 64]
            tt('dve', sv.rearrange("p (h d) -> p h d", h=4), sv.rearrange("p (h d) -> p h d", h=4),
               Eg[u][:, :, 127:128].to_broadcast([128, 4, 64]), ALU.mult, ['S%d' % g, 'Eg%d' % u], ['S%d' % g])
            tt('dve', sv, sv, PB[7][:, 0:256], ALU.add, ['S%d' % g] + bk(7), ['S%d' % g])

        def stage3(g):
            by = 6 if g % 2 == 0 else 2
            for q in range(4):
                c = g * 4 + q
                stt(yz[:, c, :], xbc[:, c, :], pcol('dsk', c), pq(by, q), ALU.mult, ALU.add, ['xbc%d' % c, 'par'] + bk(by), ['yz%d' % c])

        stage1(0)
        pend3 = None
        zg = n_ssm_groups - 16
        for i in range(8):
            if i + 1 < 8:
                stage1(i + 1)
            stage2(i)
            z_group(blk, zg)
            z_group(blk, zg + 1)
            zg += 2
            if pend3 is not None:
                stage3(pend3)
                pend3 = None
            if halves[i][1] == 1:
                pend3 = halves[i][0]
        stage3(pend3)

    def front_p1():
        act(sh[0:64, 32, :], sh[0:64, 32, :], AF.Tanh, ['sh32'], ['sh32'])
        for c in range(8):
            T_ = [tmpb[k][c % 2] for k in range(NTMP)]
            TK = ['tm%d_%d' % (k, c % 2) for k in range(NTMP)]
            bw, ba = (2, 3) if c % 2 == 0 else (6, 7)
            mm(pq(bw, 0), w2a2[0:64, c * 128:(c + 1) * 128], sh[0:64, 32, :], True, True, ['w2a2', 'sh32'], bk(bw))
            mm(pq(ba, 0), w2a2[64:128, c * 128:(c + 1) * 128], sh[64:128, 32, :], True, True, ['w2a2', 'sh32'], bk(ba))
            act(T_[0][:], pq(bw, 0), AF.Sigmoid, bk(bw) + ['par'], [TK[0]], bias=pcol('w0', c))
            ts('dve', lw[:, c, :], T_[0][:], LWS, None, ALU.mult, None, [TK[0]], ['lw%d' % c])
            act(av[:, c, :], pq(ba, 0), AF.Sigmoid, bk(ba) + ['par'], ['av%d' % c], bias=pcol('a0', c))

    def front_silu():
        for c in range(8):
            act(sh[:, 24 + c, :], sh[:, 24 + c, :], AF.Silu, ['sh%d' % (24 + c)], ['sh%d' % (24 + c)])

    def front_c(c):
        if True:
            T_ = [tmpb[k][c % 2] for k in range(NTMP)]
            TK = ['tm%d_%d' % (k, c % 2) for k in range(NTMP)]
            r_, k_, v_, g_ = sh[:, c, :], sh[:, 8 + c, :], sh[:, 16 + c, :], sh[:, 24 + c, :]
            rk_, kk_, vk_ = 'sh%d' % c, 'sh%d' % (8 + c), 'sh%d' % (16 + c)
            ts('dve', T_[1][:], k_, pcol('kk', c), None, ALU.mult, None, [kk_, 'par'], [TK[1]])
            T2b = T_[2][:].bitcast(BF16)[:, 0:TB]
            act(T2b, T_[1][:], AF.Square, [TK[1]], [TK[2]])
            mm(pq(6, 0), cstb[:, 256:384], T2b, True, True, ['cstb', TK[2]], bk(6))
            ts('dve', T_[2][:], pq(6, 0), 1e-24, None, ALU.max, None, bk(6), [TK[2]])
            act(T_[2][:], T_[2][:], AF.Ln, [TK[2]], [TK[2]])
            act(T_[2][:], T_[2][:], AF.Exp, [TK[2]], [TK[2]], scale=-0.5)
            tt('dve', kkb[:, c, :], T_[1][:], T_[2][:], ALU.mult, [TK[1], TK[2]], ['kkb%d' % c])
            ts('dve', T_[3][:], av[:, c, :], pcol('ka', c), dcol('omka', c), ALU.mult, ALU.add, ['av%d' % c, 'par', 'der'], [TK[3]])
            tt('dve', kp[:, c, :], k_, T_[3][:], ALU.mult, [kk_, TK[3]], ['kp%d' % c])
            stt(T_[4][:], r_, pcol('rk', c), kp[:, c, :], ALU.mult, ALU.mult, [rk_, 'par', 'kp%d' % c], [TK[4]])
            mm(pq(7, 0), cm('blk'), T_[4][:], True, True, ['cst', TK[4]], bk(7))
            tt('dve', bon[:, c, :], pq(7, 0), v_, ALU.mult, bk(7) + [vk_], ['bon%d' % c])
            P.add('dve', lambda e, o=T_[5][:], m=cm('rmask'), l=lw[:, c, :]: e.tensor_tensor_scan(out=o, data0=m, data1=l, initial=0.0,
                                                                                                 op0=ALU.mult, op1=ALU.add),
                  ['cst', 'lw%d' % c], [TK[5]])
            cum = T_[5]
            act(EP[:, c, :], cum[:], AF.Exp, [TK[5]], ['EP%d' % c])
            act(T_[6][:], cum[:], AF.Exp, [TK[5]], [TK[6]], scale=-1.0)
            tt('dve', T_[7][:], cum[:], lw[:, c, :], ALU.subtract, [TK[5], 'lw%d' % c], [TK[7]])
            act(T_[7][:], T_[7][:], AF.Exp, [TK[7]], [TK[7]])
            c3 = cum[:].rearrange("p (j t) -> p j t", j=2)
            tt('dve', T_[8][:].rearrange("p (j t) -> p j t", j=2), c3[:, :, 63:64].to_broadcast([128, 2, 64]), c3, ALU.subtract,
               [TK[5]], [TK[8]])
            act(T_[8][:], T_[8][:], AF.Exp, [TK[8]], [TK[8]])

            def v3(ap):
                return ap.rearrange("p (j t) -> p j t", j=2)
            tt('dve', RA[:, c, :, 64:128], v3(r_), v3(EP[:, c, :]), ALU.mult, [rk_, 'EP%d' % c], ['RA%d' % c])
            stt(RA[:, c, :, 0:64], v3(kkb[:, c, :]), -1.0, v3(T_[7][:]), ALU.mult, ALU.mult, ['kkb%d' % c, TK[7]], ['RA%d' % c])
            tt('dve', T_[9][:], kkb[:, c, :], av[:, c, :], ALU.mult, ['kkb%d' % c, 'av%d' % c], [TK[9]])
            tt('dve', BKt[:, c, :, 0:64], v3(T_[9][:]), v3(T_[6][:]), ALU.mult, [TK[9], TK[6]], ['BKt%d' % c])
            tt('pool', BKt[:, c, :, 128:192], v3(T_[9][:]), v3(T_[6][:]), ALU.mult, [TK[9], TK[6]], ['BKt%d' % c])
            tt('dve', BKt[:, c, :, 64:128], v3(kp[:, c, :]), v3(T_[6][:]), ALU.mult, ['kp%d' % c, TK[6]], ['BKt%d' % c])
            tt('pool', BKh[:, c, :, 0:64], v3(T_[9][:]), v3(T_[8][:]), ALU.mult, [TK[9], TK[8]], ['BKh%d' % c])
            tt('pool', BKh[:, c, :, 128:192], v3(T_[9][:]), v3(T_[8][:]), ALU.mult, [TK[9], TK[8]], ['BKh%d' % c])
            tt('pool', BKh[:, c, :, 64:128], v3(kp[:, c, :]), v3(T_[8][:]), ALU.mult, ['kp%d' % c, TK[8]], ['BKh%d' % c])
            cp('pool', VV[:, c, :, 64:128], v3(v_), [vk_], ['VV%d' % c])

    def rwkv_core(blk):
        for c2 in range(4):
            for hh in range(2):
                pr = hh * 64
                po = 64 - pr
                ph = po // 64
                off = 0 if hh == 1 else 64
                sl = slice(po, po + 64)
                idh = cst[pr:pr + 64, CC['ident'] + pr:CC['ident'] + pr + 64]
                ido = cst[po:po + 64, CC['ident'] + po:CC['ident'] + po + 64]
                RES = {}

                def inv_gen(cc, j):
                    c = 2 * c2 + cc
                    ui = cc * 2 + j
                    am = AM[ui]
                    amk = 'AM%d' % ui
                    bkt, uvh = BKtok[ui], UV[ui]
                    bktk, uvk_v = 'BKtok%d' % ui, 'UVv%d' % ui
                    cs = ui * 64

                    def reg(bn):
                        return PB[bn][po:po + 64, cs:cs + 64], bk(bn, halves=[ph])
                    tr(PB[0][:, ui * 128:ui * 128 + 64], BKh[pr:pr + 64, c, j, off:off + 128], idh, ['BKh%d' % c, 'cst'], bk(0))
                    yield
                    tr(PB[0][:, ui * 128 + 64:ui * 128 + 128], VV[pr:pr + 64, c, j, off:off + 128], idh, ['VV%d' % c, 'cst'], bk(0))
                    yield
                    mm(pq(1, ui), BKt[pr:pr + 64, c, j, off:off + 128], RA[pr:pr + 64, c, j, :], True, True, ['BKt%d' % c, 'RA%d' % c], bk(1))
                    yield
                    r0, k0 = reg(6)
                    mm(r0, RA[pr:pr + 64, c, j, 0:64], BKt[pr:pr + 64, c, j, 0:64], True, True, ['BKt%d' % c, 'RA%d' % c], k0)
                    yield
                    cp('act', bkt[:], PB[0][:, ui * 128:ui * 128 + 64], bk(0), [bktk])
                    yield
                    cp('act', uvh[pr:pr + 64, :], PB[0][pr:pr + 64, ui * 128 + 64:ui * 128 + 128], bk(0), [uvk_v])
                    yield
                    tt('dve', am[:], pq(1, ui), cm('m4'), ALU.mult, bk(1) + ['cst'], [amk])
                    yield
                    p0, pt0, rt = Pm[ui], Ptm[ui], Rt[ui]
                    pk0 = ['Pm%d_%d' % (ui, k) for k in range(2)]
                    ptk = ['Ptm%d_%d' % (ui, k) for k in range(2)]
                    rtk = ['Rt%d_%d' % (ui, k) for k in range(2)]
                    tt('dve', p0[0][sl, :], r0, cm('low', 64, po, po + 64), ALU.mult, k0 + ['cst'], [pk0[0]])
                    yield
                    def bfv(t_):
                        return t_[:].bitcast(BF16)[sl, 0:64]
                    tt('pool', bfv(rt[0]), am[sl, 0:64], ido, ALU.add, [amk, 'cst'], [rtk[0]])
                    yield
                    Pcur, Pck = p0[0][sl, :], pk0[0]
                    Ptcur, Ptck = am[sl, 0:64], amk
                    Rcur, Rck = bfv(rt[0]), rtk[0]
                    rP, kP = reg(2)
                    rPt, kPt = reg(3)
                    rR, kR = reg(4)
                    for lvl in range(1, 6):
                        nb = lvl % 2
                        mm(rP, Ptcur, Pcur, True, True, [Ptck, Pck], kP)
                        yield
                        if lvl < 5:
                            mm(rPt, Pcur, Ptcur, True, True, [Ptck, Pck], kPt)
                            yield
                        cp('act', bfv(p0[nb]), rP, kP, [pk0[nb]])
                        yield
                        if lvl < 5:
                            cp('act', bfv(pt0[nb]), rPt, kPt, [ptk[nb]])
                            yield
                        mm(rR, bfv(p0[nb]), Rcur, True, True, [pk0[nb], Rck], kR)
                        yield
                        rout = rt[nb][sl, :] if lvl == 5 else bfv(rt[nb])
                        tt('dve', rout, Rcur, rR, ALU.add, [Rck] + kR, [rtk[nb]])
                        yield
                        Pcur, Pck = bfv(p0[nb]), pk0[nb]
                        if lvl < 5:
                            Ptcur, Ptck = bfv(pt0[nb]), ptk[nb]
                        Rcur, Rck = rout, rtk[nb]
                    RES[(cc, j)] = (Rcur, Rck)

                def seq_gen(cc, j):
                    c = 2 * c2 + cc
                    ui = cc * 2 + j
                    am = AM[ui]
                    amk = 'AM%d' % ui
                    bkt, uvh, xs = BKtok[ui], UV[ui], X1s[ui]
                    bktk, uvk_v, uvk_u, xsk = 'BKtok%d' % ui, 'UVv%d' % ui, 'UVu%d' % ui, 'X1s%d' % ui
                    Rcur, Rck = RES[(cc, j)]
                    srk = 'SR%d_%d' % (c, hh)
                    srv = SR[pr:pr + 64, c, :]
                    cs = ui * 64
                    rX, kX = PB[5][po:po + 64, cs:cs + 64], bk(5, halves=[ph])
                    rU, kU = PB[6][po:po + 64, cs:cs + 64], bk(6, halves=[ph])
                    yk = bk(7, halves=[hh])
                    mm(rX, RA[pr:pr + 64, c, j, 0:64], srv, True, False, ['RA%d' % c, srk], kX)
                    mm(rX, am[pr:pr + 64, 0:64], uvh[pr:pr + 64, :], False, True, [amk, uvk_v], kX)
                    yield
                    cp('act', xs[sl, :], rX, kX, [xsk])
                    yield
                    mm(rU, Rcur, xs[sl, :], True, True, [Rck, xsk], kU)
                    yield
                    cp('dve', uvh[sl, :], rU, kU, [uvk_u])
                    yield
                    uvk = [uvk_v, uvk_u]
                    mm(PB[7][pr:pr + 64, cs:cs + 64], srv, RA[pr:pr + 64, c, j, 64:128], True, True, [srk, 'RA%d' % c], yk)
                    yield
                    mm(PB[7][pr:pr + 64, 256 + cs:256 + cs + 64], uvh[:, :], am[:, 64:128], True, True, uvk + [amk], yk)
                    yield
                    so = PB[5][pr:pr + 64, cs:cs + 64]
                    sok = bk(5, halves=[hh])
                    mm(so, bkt[:, :], uvh[:, :], True, True, [bktk] + uvk, sok)
                    yield
                    ts('pool', srv, srv, EP[pr:pr + 64, c, j * 64 + 63:j * 64 + 64], None, ALU.mult, None, [srk, 'EP%d' % c], [srk])
                    yield
                    tt('dve', srv, srv, so, ALU.add, [srk] + sok, [srk])
                    yield

                roundrobin([inv_gen(cc, j) for cc in range(2) for j in range(2)])
                for j in range(2):
                    roundrobin([seq_gen(cc, j) for cc in range(2)])
                hs = slice(pr, pr + 64)
                for cc in range(2):
                    c = 2 * c2 + cc
                    cp('act', ybuf[hs, c, :], PB[7][hs, cc * 128:cc * 128 + 128], bk(7, halves=[hh]), ['ybuf%d' % c])
                    tt('dve', ybuf[hs, c, :], ybuf[hs, c, :], PB[7][hs, 256 + cc * 128:256 + cc * 128 + 128], ALU.add,
                       ['ybuf%d' % c] + bk(7, halves=[hh]), ['ybuf%d' % c])

    def gn_gen(c):
        T0, T1 = tmpb[c][0], tmpb[c][1]
        K0, K1 = 'tm%d_0' % c, 'tm%d_1' % c
        bm, qm = 2 + c // 4, c % 4
        bv = 6 + c // 4
        mm(pq(bm, qm), cm('blk'), ybuf[:, c, :], True, True, ['cst', 'ybuf%d' % c], bk(bm))
        yield
        stt(T0[:], pq(bm, qm), -1.0 / 64, ybuf[:, c, :], ALU.mult, ALU.add, bk(bm) + ['ybuf%d' % c], [K0])
        yield
        T1b = T1[:].bitcast(BF16)[:, 0:TB]
        act(T1b, T0[:], AF.Square, [K0], [K1])
        yield
        mm(pq(bv, qm), cstb[:, 256:384], T1b, True, True, ['cstb', K1], bk(bv))
        yield
        act(T1[:], pq(bv, qm), AF.Ln, bk(bv), [K1], scale=1.0 / 64, bias=GN_EPS)
        yield
        act(T1[:], T1[:], AF.Exp, [K1], [K1], scale=-0.5)
        yield
        tt('dve', T0[:], T0[:], T1[:], ALU.mult, [K0, K1], [K0])
        yield
        ts('dve', T0[:], T0[:], pcol('gng', c), pcol('gnb', c), ALU.mult, ALU.add, [K0, 'par'], [K0])
        yield
        tt('dve', T0[:], T0[:], bon[:, c, :], ALU.add, [K0, 'bon%d' % c], [K0])
        yield
        tt('dve', yr[:, c, :], T0[:], sh[:, 24 + c, :], ALU.mult, [K0, 'sh%d' % (24 + c)], ['yr%d' % c])
        yield

    WT = []
    for o_ in range(8):
        WT.append(wssm_bf[o_, :, 0:8, :])
        WT.append(wssm_bf[o_, :, 8:16, :])
    for eo_ in range(8):
        WT.append(wout_bf[eo_])

    def wo_issue(t):
        if t >= 24:
            return
        ld(wob[:, t % 4], WT[t], 'wo%d' % (t % 4), reads=['wbf'])

    def out_prefetch():
        for t in range(3):
            wo_issue(t)

    def out_gen(blk, xs_, xk):
        for o in range(8):
            for hf in range(2):
                t = 2 * o + hf
                wo_issue(t + 3)
                for c8 in range(8):
                    c = hf * 8 + c8
                    mm(pq(2 + o % 2, 0), wob[:, t % 4, c8, :], yn[:, c, :], c == 0, c == 15, ['wo%d' % (t % 4), 'yn%d' % c], bk(2 + o % 2))
            tt('dve', tmpb[3][o % 2][:], pq(2 + o % 2, 0), G[:, o, :], ALU.mult, bk(2 + o % 2) + ['G%d' % o], ['tm3_%d' % (o % 2)])
            for c in range(8):
                mm(pq(6 + o % 2, 0), wrwS[:, o, c, :], yr[:, c, :], c == 0, c == 7, ['wrwS', 'yr%d' % c], bk(6 + o % 2))
            T_ = tmpb[0][o % 2]
            tk = 'tm0_%d' % (o % 2)
            tt('dve', T_[:], pq(6 + o % 2, 0), G[:, 8 + o, :], ALU.mult, bk(6 + o % 2) + ['G%d' % (8 + o)], [tk])
            tt('dve', mT[:, o, :], tmpb[3][o % 2][:], T_[:], ALU.add, ['tm3_%d' % (o % 2), tk], ['mT%d' % o])
            yield
        for eo in range(8):
            t = 16 + eo
            wo_issue(t + 3)
            for o in range(8):
                mm(pq(2 + eo % 2, 0), wob[:, t % 4, o, :], mT[:, o, :], o == 0, o == 7, ['wo%d' % (t % 4), 'mT%d' % o], bk(2 + eo % 2))
            cp('act', oT[:, eo, :], pq(2 + eo % 2, 0), bk(2 + eo % 2), ['oT%d' % eo])
            T_ = tmpb[1][eo % 2]
            tk = 'tm1_%d' % (eo % 2)
            Tb = T_[:].bitcast(BF16)[:, 0:TB]
            act(Tb, oT[:, eo, :], AF.Square, ['oT%d' % eo], [tk])
            mm(pq(6, 0), cstb[:, 128:256], Tb, eo == 0, eo == 7, ['cstb', tk], bk(6))
            yield
        T_ = tmpb[2][0]
        tk = 'tm2_0'
        act(T_[:], pq(6, 0), AF.Ln, bk(6), [tk], scale=1.0 / D, bias=EPS)
        act(T_[:], T_[:], AF.Exp, [tk], [tk], scale=-0.5)
        for eo in range(8):
            stt(on[:, eo, :], oT[:, eo, :], pcol('post', eo), T_[:], ALU.mult, ALU.mult, ['oT%d' % eo, 'par', tk], ['on%d' % eo])
        yield
        for hb in range(2):
            for q_ in range(4):
                eo = hb * 4 + q_
                tr(pq(7, q_), on[:, eo, :], ident, ['on%d' % eo, 'cst'], bk(7))
            fin = xn[:, hb * 512:(hb + 1) * 512]
            tt('dve', fin, PB[7][:, :], xs_[:, hb * 512:(hb + 1) * 512], ALU.add, bk(7) + [xk], ['xn'])
            i = P.dma('sp', lambda e, hb=hb, fin=fin: e.dma_start(out=out_d[blk * TB:(blk + 1) * TB, hb * 512:(hb + 1) * 512], in_=fin),
                      reads=['xn'], semkey='fin%d' % hb)
            final_ops.append(i)
            yield

    def ld_x(blk):
        ld(xt[blk % 2][:], x_d[blk * TB:(blk + 1) * TB, :], 'xt%d' % (blk % 2))

    def stageA_gen(blk, do_ld=True):
        xs_ = xt[blk % 2]
        xk = 'xt%d' % (blk % 2)
        if do_ld:
            ld_x(blk)
        act(xn[:], xs_[:], AF.Square, [xk], ['ssq', 'xn'], accum_out=ssq[:])
        act(rstd[:], ssq[:], AF.Ln, ['ssq'], ['rstd'], scale=1.0 / D, bias=EPS)
        act(rstd[:], rstd[:], AF.Exp, ['rstd'], ['rstd'], scale=-0.5)
        ts('dve', xn[:], xs_[:], rstd[:], None, ALU.mult, None, [xk, 'rstd'], ['xn'])
        yield
        for c in range(8):
            b_, q_ = c // 4, c % 4
            tr(pq(b_, q_), xn[:, c * 128:(c + 1) * 128], ident, ['xn', 'cst'], bk(b_))
        for c in range(8):
            b_, q_ = c // 4, c % 4
            act(hT[:, c, :], pq(b_, q_), AF.Copy, bk(b_) + ['par'], ['hT%d' % c], scale=pcol('pre', c))
        yield

    def front_gen(blk, do_a=True):
        if do_a:
            for _ in stageA_gen(blk):
                yield
        if blk > 0:
            barrier(K_RW, K_SSD_E)
        for _ in run_groups_gen(blk, 0, n_ssm_groups - 16):
            yield

    for _ in front_gen(0):
        pass
    for blk in range(nblk):
        xs_ = xt[blk % 2]
        xk = 'xt%d' % (blk % 2)
        if blk > 0:
            barrier(K_OUT + K_RW, K_SSD_R)
        ssd_core(blk)
        roundrobin([ssd_norm_gen(g_) for g_ in range(4)])
        barrier(K_SSD, K_RW)
        bi = 0
        nfc = 0
        for _ in run_groups_gen(blk, n_ssm_groups, n_rw_groups):
            if bi == 1:
                front_p1()
            elif bi >= 2 and nfc < 8 and 4 * nfc + 4 < 4 * (bi - 1):
                front_c(nfc)
                nfc += 1
            bi += 1
        while nfc < 8:
            front_c(nfc)
            nfc += 1
        front_silu()
        out_prefetch()
        if blk + 1 < nblk:
            ld_x(blk + 1)
        rwkv_core(blk)
        barrier(['BKt%d' % c_ for c_ in range(8)] + ['BKh%d' % c_ for c_ in range(8)], K_OUT)
        roundrobin([gn_gen(c_) for c_ in range(8)])
        run_groups(blk, n_rw_groups, NG)
        if INTERLEAVE:
            gens = [out_gen(blk, xs_, xk)]
            if blk + 1 < nblk:
                gens.append(front_gen(blk + 1))
            roundrobin(gens)
        else:
            gens = [out_gen(blk, xs_, xk)]
            if blk + 1 < nblk:
                gens.append(stageA_gen(blk + 1, do_ld=False))
            roundrobin(gens)
            if blk + 1 < nblk:
                roundrobin([front_gen(blk + 1, do_a=False)])

    P.emit(final_wait_ops=final_ops)
    return nc, P


def pack_params(inp):
    p = np.zeros((128, NPAR), np.float32)

    def cmaj(v):
        v = np.asarray(v, np.float32).reshape(-1, 128)
        return v.T

    def put(name, arr):
        p[:arr.shape[0], PC[name]:PC[name] + arr.shape[1]] = arr

    put('pre', cmaj(inp['pre_gain'][0]))
    put('post', cmaj(inp['post_gain'][0]))
    cw = np.asarray(inp['conv_w'][0], np.float32)
    cwp = np.zeros((128, 24, 4), np.float32)
    for k in range(4):
        cwp[:, :, k] = cmaj(cw[k])
    put('cw', cwp.reshape(128, 96))
    put('cb', cmaj(inp['conv_b'][0]))
    put('dsk', cmaj(np.repeat(np.asarray(inp['d_skip'][0], np.float32), 64)))
    put('sng', cmaj(inp['ssm_norm_gain'][0]))
    put('mu', cmaj(inp['rwkv_mu'][0]))
    put('w0', cmaj(inp['decay_w0'][0]))
    put('a0', cmaj(inp['iclr_a0'][0]))
    put('kk', cmaj(inp['k_k'][0]))
    put('ka', cmaj(inp['k_a'][0]))
    put('rk', cmaj(np.asarray(inp['r_k'][0], np.float32).reshape(-1)))
    put('gng', cmaj(inp['gn_gain'][0]))
    put('gnb', cmaj(inp['gn_bias'][0]))
    put('bg', cmaj(inp['b_gate'][0]))
    p[:32, PC['dtb']] = np.asarray(inp['dt_bias'][0], np.float32)
    p[:32, PC['alog']] = np.asarray(inp['a_log'][0], np.float32)
    return p


_CACHE = {}


def make_in_maps(inp, nblk=NBLK, ncores=4):
    par = pack_params(inp)
    cst = make_consts()
    w2a2 = np.concatenate([np.asarray(inp['decay_w2'][0], np.float32), np.asarray(inp['iclr_a2'][0], np.float32)], axis=0)
    shared = {
        "w_in": np.ascontiguousarray(np.asarray(inp['w_in'][0], np.float32)),
        "par": par, "cst": cst, "w2a2": np.ascontiguousarray(w2a2),
        "w_ssm": np.ascontiguousarray(np.asarray(inp['w_branch_ssm'][0], np.float32)),
        "w_rwkv": np.ascontiguousarray(np.asarray(inp['w_branch_rwkv'][0], np.float32)),
        "w_out": np.ascontiguousarray(np.asarray(inp['w_out'][0], np.float32)),
    }
    x = np.asarray(inp['x'], np.float32)
    maps = []
    for b in range(ncores):
        m = dict(shared)
        m["x"] = np.ascontiguousarray(x[b, :nblk * TB])
        maps.append(m)
    return maps


def kernel(**inputs):
    if 'nc' not in _CACHE:
        _CACHE['nc'] = build()[0]
    nc = _CACHE['nc']
    maps = make_in_maps(inputs)
    res = run_bass_kernel_spmd(nc, maps, core_ids=list(range(4)))
    out = np.stack([np.asarray(r["out"], np.float32) for r in res.results], axis=0)
    return out
```

```python
import contextlib
import numpy as np
import concourse.bass as bass
import concourse.mybir as mybir
from concourse.alu_op_type import AluOpType as ALU
from concourse.bass_utils import run_bass_kernel_spmd

F32 = mybir.dt.float32
BF16 = mybir.dt.bfloat16
AF = mybir.ActivationFunctionType

CH = 30000


class Prog:
    def __init__(self, nc):
        self.nc = nc
        self.ops = []
        self.last_w = {}
        self.readers = {}
        self.stack = contextlib.ExitStack()

    def sb(self, name, shape, dtype=F32):
        return self.stack.enter_context(self.nc.sbuf_tensor("s_" + name, list(shape), dtype))

    def ps(self, name, shape, dtype=F32):
        return self.stack.enter_context(self.nc.psum_tensor("p_" + name, list(shape), dtype))

    def add(self, eng, fn, reads=(), writes=(), dma=False, semkey=None):
        i = len(self.ops)
        deps = set()
        for k in reads:
            if k in self.last_w:
                deps.add(self.last_w[k])
        for k in writes:
            if k in self.last_w:
                deps.add(self.last_w[k])
            for r in self.readers.get(k, ()):
                deps.add(r)
        deps.discard(i)
        self.ops.append(dict(eng=eng, fn=fn, deps=deps, dma=dma, semkey=semkey))
        for k in reads:
            self.readers.setdefault(k, []).append(i)
        for k in writes:
            self.last_w[k] = i
            self.readers[k] = []
        return i

    def dma(self, eng, fn, reads=(), writes=(), semkey=None):
        return self.add(eng, fn, reads, writes, dma=True, semkey=semkey)

    def emit(self, final_wait_ops=()):
        nc = self.nc
        ops = self.ops
        n = len(ops)
        waited_on = [False] * n
        for i, o in enumerate(ops):
            nd = set()
            for d in o['deps']:
                od = ops[d]
                if (not od['dma']) and (not o['dma']) and od['eng'] == 'pe' and o['eng'] == 'pe':
                    continue
                nd.add(d)
            o['deps'] = nd
            for d in nd:
                waited_on[d] = True
        for i in final_wait_ops:
            waited_on[i] = True
        cnt = {}
        for i, o in enumerate(ops):
            if o['dma']:
                sid = ('dma', o['semkey'])
                cnt[sid] = cnt.get(sid, 0) + 16
                o['ev'] = (sid, cnt[sid])
            elif waited_on[i]:
                e = o['eng']
                m = cnt.get(('m', e), 0)
                cnt[('m', e)] = m + 1
                o['ev'] = (('eng', e, m // CH), (m % CH) + 1)
            else:
                o['ev'] = None
        semids = []
        seen = set()
        for o in ops:
            if o['ev'] is not None and o['ev'][0] not in seen:
                seen.add(o['ev'][0])
                semids.append(o['ev'][0])
        sems = {}
        for k, sid in enumerate(semids):
            sems[sid] = self.stack.enter_context(nc.semaphore("s%d" % k))
        self.n_sems = len(semids)
        per_eng = {e: [] for e in ['pe', 'act', 'dve', 'pool', 'sp']}
        for i, o in enumerate(ops):
            per_eng[o['eng']].append(i)

        def run_engine(ename, eobj):
            waited = {}
            for i in per_eng[ename]:
                o = ops[i]
                need = {}
                for d in o['deps']:
                    sid, v = ops[d]['ev']
                    if sid[0] == 'eng':
                        key = ('eng', sid[1])
                        gv = sid[2] * CH + v
                    else:
                        key = sid
                        gv = v
                    if waited.get(key, 0) >= gv:
                        continue
                    if need.get(key, (None, 0))[1] < gv:
                        need[key] = (sid, gv)
                for key, (sid, gv) in need.items():
                    if sid[0] == 'eng':
                        eobj.wait_ge(sems[sid], gv - sid[2] * CH)
                    else:
                        eobj.wait_ge(sems[sid], gv)
                    waited[key] = gv
                ins = o['fn'](eobj)
                if o['ev'] is not None:
                    ins.then_inc(sems[o['ev'][0]], 16 if o['dma'] else 1)
            if ename == 'sp':
                for i in final_wait_ops:
                    sid, v = ops[i]['ev']
                    eobj.wait_ge(sems[sid], v)

        with nc.Block() as block:
            @block.tensor
            def _(e):
                run_engine('pe', e)

            @block.scalar
            def _(e):
                run_engine('act', e)

            @block.vector
            def _(e):
                run_engine('dve', e)

            @block.gpsimd
            def _(e):
                run_engine('pool', e)

            @block.sync
            def _(e):
                run_engine('sp', e)
        self.stack.close()


D = 1024
T = 4096
TB = 128
NBLK = T // TB
INC = 11424
EPS = 1e-6
GN_EPS = 64 * 1e-5
LWS = -float(np.exp(-0.5))

O_Z, O_X, O_B, O_C, O_DT = 0, 2048, 4096, 4608, 5120
O_R, O_K, O_V, O_G, O_WA = 5152, 6176, 7200, 8224, 9248
O_GS, O_GR = 9376, 10400

PC = {}
_o = 0
for _n, _w in [('pre', 8), ('post', 8), ('cw', 96), ('cb', 24), ('dsk', 16), ('sng', 16), ('mu', 33),
               ('w0', 8), ('a0', 8), ('kk', 8), ('ka', 8), ('rk', 8), ('gng', 8), ('gnb', 8), ('bg', 16),
               ('dtb', 1), ('alog', 1)]:
    PC[_n] = _o
    _o += _w
NPAR = _o
DC = {}
_o = 0
for _n, _w in [('omm', 33), ('omka', 8), ('aneg', 1)]:
    DC[_n] = _o
    _o += _w
NDER = _o

CC = {}
_o = 0
for _n, _w in [('ident', 128), ('incl', 128), ('strict', 128), ('ones', 128), ('blk', 128), ('m4', 128),
               ('low', 64), ('rmask', 128)]:
    CC[_n] = _o
    _o += _w
NCST = _o


def make_consts():
    c = np.zeros((128, NCST), np.float32)
    i = np.arange(128)
    c[:, CC['ident']:CC['ident'] + 128] = np.eye(128)
    c[:, CC['incl']:CC['incl'] + 128] = (i[:, None] <= i[None, :])
    c[:, CC['strict']:CC['strict'] + 128] = (i[:, None] > i[None, :])
    c[:, CC['ones']:CC['ones'] + 128] = 1.0
    c[:, CC['blk']:CC['blk'] + 128] = ((i[:, None] // 64) == (i[None, :] // 64))
    s = i[:, None] % 64
    t = i[None, :] % 64
    u = i[None, :] // 64
    c[:, CC['m4']:CC['m4'] + 128] = np.where(u == 0, s < t, s <= t)
    j = np.arange(64)
    c[:64, CC['low']:CC['low'] + 64] = (j[None, :] < j[:, None])
    c[64:, CC['low']:CC['low'] + 64] = (j[None, :] < j[:, None])
    c[:, CC['rmask']:CC['rmask'] + 128] = (np.arange(128)[None, :] % 64 != 0)
    return c


def build(nblk=NBLK, dbg=None, stage=9):
    nc = bass.Bass("TRN2", target_bir_lowering=False)
    Tn = nblk * TB
    x_d = nc.dram_tensor("x", [Tn, D], F32, kind="ExternalInput").ap()
    win_d = nc.dram_tensor("w_in", [D, INC], F32, kind="ExternalInput").ap()
    par_d = nc.dram_tensor("par", [128, NPAR], F32, kind="ExternalInput").ap()
    cst_d = nc.dram_tensor("cst", [128, NCST], F32, kind="ExternalInput").ap()
    w2a2_d = nc.dram_tensor("w2a2", [128, D], F32, kind="ExternalInput").ap()
    wssm_d = nc.dram_tensor("w_ssm", [2048, D], F32, kind="ExternalInput").ap()
    wrw_d = nc.dram_tensor("w_rwkv", [D, D], F32, kind="ExternalInput").ap()
    wout_d = nc.dram_tensor("w_out", [D, D], F32, kind="ExternalInput").ap()
    out_d = nc.dram_tensor("out", [Tn, D], F32, kind="ExternalOutput").ap()
    NCHK = 90
    win_bf = nc.dram_tensor("win_bf", [NCHK, 128, 8, 128], BF16, kind="Internal").ap()
    wssm_bf = nc.dram_tensor("wssm_bf", [8, 128, 16, 128], BF16, kind="Internal").ap()
    wrw_bf = nc.dram_tensor("wrw_bf", [8, 128, 8, 128], BF16, kind="Internal").ap()
    wout_bf = nc.dram_tensor("wout_bf", [8, 128, 8, 128], BF16, kind="Internal").ap()
    dbg_d = None
    if dbg is not None:
        dbg_d = nc.dram_tensor("dbg", [nblk, 128, dbg], F32, kind="ExternalOutput").ap()

    P = Prog(nc)
    sb = P.sb

    def act(out, in_, func, r, w, **kw):
        P.add('act', lambda e: e.activation(out=out, in_=in_, func=func, **kw), r, w)

    def tt(eng, out, in0, in1, op, r, w):
        P.add(eng, lambda e: e.tensor_tensor(out=out, in0=in0, in1=in1, op=op), r, w)

    def ts(eng, out, in0, s1, s2, op0, op1, r, w):
        if op1 is None:
            P.add(eng, lambda e: e.tensor_scalar(out=out, in0=in0, scalar1=s1, scalar2=None, op0=op0), r, w)
        else:
            P.add(eng, lambda e: e.tensor_scalar(out=out, in0=in0, scalar1=s1, scalar2=s2, op0=op0, op1=op1), r, w)

    def stt(out, in0, scalar, in1, op0, op1, r, w):
        P.add('dve', lambda e: e.scalar_tensor_tensor(out=out, in0=in0, scalar=scalar, in1=in1, op0=op0, op1=op1), r, w)

    def mm(out, lhsT, rhs, start, stop, r, w):
        P.add('pe', lambda e: e.matmul(out, lhsT=lhsT, rhs=rhs, start=start, stop=stop), r, w)

    def tr(out, in_, idn, r, w):
        P.add('pe', lambda e: e.transpose(out=out, in_=in_, identity=idn), r, w)

    def cp(eng, out, in_, r, w):
        if eng == 'act':
            P.add(eng, lambda e: e.activation(out=out, in_=in_, func=AF.Copy), r, w)
        else:
            P.add(eng, lambda e: e.tensor_copy(out=out, in_=in_), r, w)

    def recip(out, in_, r, w):
        P.add('dve', lambda e: e.reciprocal(out=out, in_=in_), r, w)

    def ld(out, in_, key, reads=()):
        return P.dma('sp', lambda e: e.dma_start(out=out, in_=in_), reads=list(reads), writes=[key], semkey=key)

    par = sb("par", [128, NPAR])
    der = sb("der", [128, NDER])
    cst = sb("cst", [128, NCST])
    w2a2 = sb("w2a2", [128, D])
    ld(par[:], par_d, 'par')
    ld(cst[:], cst_d, 'cst')
    ld(w2a2[:], w2a2_d, 'w2a2')

    def pcol(n, i=0, p0=0, p1=128):
        return par[p0:p1, PC[n] + i:PC[n] + i + 1]

    def dcol(n, i=0, p0=0, p1=128):
        return der[p0:p1, DC[n] + i:DC[n] + i + 1]

    def cm(n, w=128, p0=0, p1=128):
        return cst[p0:p1, CC[n]:CC[n] + w]

    ts('dve', der[:, DC['omm']:DC['omm'] + 33], par[:, PC['mu']:PC['mu'] + 33], -1.0, 1.0, ALU.mult, ALU.add, ['par'], ['der'])
    ts('dve', der[:, DC['omka']:DC['omka'] + 8], par[:, PC['ka']:PC['ka'] + 8], -1.0, 1.0, ALU.mult, ALU.add, ['par'], ['der'])
    act(dcol('aneg', 0, 0, 32), pcol('alog', 0, 0, 32), AF.Exp, ['par'], ['der'])
    ts('dve', dcol('aneg', 0, 0, 32), dcol('aneg', 0, 0, 32), -1.0, None, ALU.mult, None, ['der'], ['der'])

    PB = [P.ps("pb%d" % b, [128, 512]) for b in range(8)]

    def bk(b, qs=None, halves=(0, 1)):
        return ['B%dh%d' % (b, h) for h in halves]

    def pq(b, q, p0=0, p1=128, w=128):
        return PB[b][p0:p1, q * 128:q * 128 + w]

    hist_c = sb("hist_c", [128, 24, 3])
    hist_r = sb("hist_r", [128, 33, 1])
    Sst = sb("Sst", [128, 2048])
    SR = sb("SR", [128, 8, 64])
    P.add('pool', lambda e: e.memset(hist_c[:], 0.0), [], ['hc%d' % c for c in range(24)])
    P.add('pool', lambda e: e.memset(hist_r[:], 0.0), [], ['hr%d' % c for c in range(33)])
    P.add('pool', lambda e: e.memset(Sst[:], 0.0), [], ['S%d' % g for g in range(4)])
    P.add('pool', lambda e: e.memset(SR[:], 0.0), [], ['SR%d_%d' % (c, h) for c in range(8) for h in range(2)])

    arena = sb("arena", [128, 20480])
    _ao = [0]

    def carve(n, pat=None, **kw):
        v = arena[:, _ao[0]:_ao[0] + n]
        _ao[0] += n
        assert _ao[0] <= 20480
        return v.rearrange(pat, **kw) if pat else v

    xt = [sb("xt%d" % i, [128, D]) for i in range(2)]
    xn = sb("xn", [128, D])
    ssq = sb("ssq", [128, 1])
    rstd = sb("rstd", [128, 1])
    hT = sb("hT", [128, 8, TB], BF16)
    NW = 6
    wt = [sb("wt%d" % i, [128, 8, 128], BF16) for i in range(NW)]
    _ao[0] = 0
    xbc = carve(3072, "p (c t) -> p c t", c=24)
    zs = carve(2048, "p (c t) -> p c t", c=16)
    raw = [sb("raw%d" % i, [128, 3 + TB]) for i in range(8)]
    acc = [sb("acc%d" % i, [128, TB]) for i in range(4)]
    dtT = sb("dtT", [32, TB])
    aT = sb("aT", [32, TB])
    dta = sb("dta", [128, 64])
    Xdt = carve(2048)
    Xds = carve(2048)
    rhsa_f = carve(4096)
    rhsa = rhsa_f.bitcast(BF16)[:, 0:4096].rearrange("p (h l) -> p h l", h=32)
    dstok = sb("dstok", [128, 32])
    dec = [carve(512, "p (h l) -> p h l", h=4) for i in range(2)]
    mix = [carve(512, "p (h l) -> p h l", h=4) for i in range(2)]
    Eg = [carve(512, "p (h l) -> p h l", h=4) for i in range(2)]
    Cdec = [carve(512, "p (h l) -> p h l", h=4) for i in range(2)]
    t1 = [sb("t1_%d" % i, [128, TB]) for i in range(2)]
    yz = carve(2048, "p (c t) -> p c t", c=16)
    Btok = carve(512, "p (g n) -> p g n", g=4)
    scm = carve(512, "p (g n) -> p g n", g=4)
    sqb = [sb("sqb%d" % i, [128, TB], BF16) for i in range(2)]
    rsb = sb("rsb", [128, TB])
    yn = sb("yn", [128, 16, TB], BF16)
    rawr = [sb("rawr%d" % i, [128, 1 + TB]) for i in range(8)]
    tmpr = [sb("tmpr%d" % i, [128, TB]) for i in range(4)]
    _ao[0] = 0
    sh = carve(33 * 128, "p (c t) -> p c t", c=33)
    lw = carve(1024, "p (c t) -> p c t", c=8)
    av = carve(1024, "p (c t) -> p c t", c=8)
    kkb = carve(1024, "p (c t) -> p c t", c=8)
    kp = carve(1024, "p (c t) -> p c t", c=8)
    bon = carve(1024, "p (c t) -> p c t", c=8)
    EP = carve(1024, "p (c t) -> p c t", c=8)
    NTMP = 10
    tmpb = [[sb("tm%d_%d" % (k, i), [128, TB]) for i in range(2)] for k in range(NTMP)]
    RA = carve(2048, "p (c j t) -> p c j t", c=8, j=2)
    BKt = carve(3072, "p (c j t) -> p c j t", c=8, j=2)
    BKh = carve(3072, "p (c j t) -> p c j t", c=8, j=2)
    VV = sb("VV", [128, 8, 2, 192])
    NAM = 4
    BKtok = [sb("BKtok%d" % i, [128, 64]) for i in range(NAM)]
    UV = [sb("UV%d" % i, [128, 64]) for i in range(NAM)]
    AM = [sb("AM%d" % i, [128, 128]) for i in range(NAM)]
    Pm = [[sb("Pm%d_%d" % (i, k), [128, 64]) for k in range(2)] for i in range(NAM)]
    Ptm = [[sb("Ptm%d_%d" % (i, k), [128, 64]) for k in range(2)] for i in range(NAM)]
    Rt = [[sb("Rt%d_%d" % (i, k), [128, 64]) for k in range(2)] for i in range(NAM)]
    X1s = [sb("X1s%d" % i, [128, 64]) for i in range(NAM)]
    ybuf = carve(1024, "p (c t) -> p c t", c=8)
    yr = sb("yr", [128, 8, TB], BF16)
    _ao[0] = 13312
    wob = sb("wob", [128, 4, 8, 128], BF16)
    mT = sb("mT", [128, 8, TB], BF16)
    oT = carve(1024, "p (c t) -> p c t", c=8)
    on = carve(1024, "p (c t) -> p c t", c=8)
    G = carve(2048, "p (c t) -> p c t", c=16)
    wrwS = sb("wrwS", [128, 8, 8, 128], BF16)
    bar = sb("bar", [128, 1])
    K_SSD = (['xbc%d' % c for c in range(24)] + ['zs%d' % c for c in range(16)] + ['Xdt%d' % c for c in range(4)]
             + ['Xds%d' % c for c in range(4)] + ['rhsa', 'Btok', 'scm'] + ['yz%d' % c for c in range(16)]
             + ['%s%d' % (n, i) for n in ('dec', 'mix', 'Eg', 'Cdec') for i in range(2)])
    K_RW = (['sh%d' % c for c in range(33)] + ['%s%d' % (n, c) for n in ('lw', 'av', 'kkb', 'kp', 'bon', 'EP', 'RA', 'BKt', 'BKh', 'ybuf')
                                                for c in range(8)])
    K_OUT = (['%s%d' % (n, c) for n in ('oT', 'on') for c in range(8)] + ['G%d' % c for c in range(16)])

    K_SSD_E = ['xbc%d' % c for c in range(24)] + ['zs%d' % c for c in range(16)] + ['rhsa']
    K_SSD_R = [k for k in K_SSD if k not in K_SSD_E]

    def barrier(old, new):
        P.add('pool', lambda e: e.memset(bar[:], 0.0), [], ['bar'] + old + new)

    P.add('pool', lambda e: e.memset(VV[:], 0.0), [], ['VV%d' % c for c in range(8)])

    ident = cm('ident')
    cstb = sb("cstb", [128, 384], BF16)
    cp('dve', cstb[:, 0:128], cm('strict'), ['cst'], ['cstb'])
    cp('dve', cstb[:, 128:256], cm('ones'), ['cst'], ['cstb'])
    cp('dve', cstb[:, 256:384], cm('blk'), ['cst'], ['cstb'])
    cnt = {'raw': 0, 'acc': 0, 'rawr': 0, 'wo': 0, 'am': 0, 'tmpr': 0}
    final_ops = []

    def roundrobin(gens):
        gens = list(gens)
        while gens:
            nxt = []
            for g in gens:
                try:
                    next(g)
                    nxt.append(g)
                except StopIteration:
                    pass
            gens = nxt

    def consumer_z(c):
        def f(pp, pk):
            act(zs[:, c, :], pp, AF.Silu, pk, ['zs%d' % c])
            yield
        return f

    def consumer_xbc(c):
        def f(pp, pk):
            ri = cnt['raw'] % 8
            cnt['raw'] += 1
            r_ = raw[ri]
            rk_ = 'raw%d' % ri
            act(r_[:, 3:3 + TB], pp, AF.Copy, pk, [rk_])
            cp('pool', r_[:, 0:3], hist_c[:, c, :], ['hc%d' % c], [rk_])
            yield
            ai = cnt['acc'] % 4
            cnt['acc'] += 1
            a_ = acc[ai]
            ak_ = 'acc%d' % ai
            wc = PC['cw'] + c * 4
            ts('dve', a_[:], r_[:, 0:TB], par[:, wc:wc + 1], pcol('cb', c), ALU.mult, ALU.add, [rk_, 'par'], [ak_])
            yield
            for k in range(1, 4):
                stt(a_[:], r_[:, k:k + TB], par[:, wc + k:wc + k + 1], a_[:], ALU.mult, ALU.add, [rk_, ak_, 'par'], [ak_])
                yield
            cp('pool', hist_c[:, c, :], r_[:, TB:TB + 3], [rk_], ['hc%d' % c])
            act(xbc[:, c, :], a_[:], AF.Silu, [ak_], ['xbc%d' % c])
            yield
        return f

    def consumer_dt(pp, pk):
        act(dtT[:], pp, AF.Exp, pk + ['par'], ['dtT'], bias=pcol('dtb', 0, 0, 32))
        act(dtT[:], dtT[:], AF.Ln, ['dtT'], ['dtT'], bias=1.0)
        ts('dve', aT[:], dtT[:], dcol('aneg', 0, 0, 32), None, ALU.mult, None, ['dtT', 'der'], ['aT'])
        id32 = cst[0:32, CC['ident']:CC['ident'] + 32]
        tr(PB[2][:, 0:32], dtT[:], id32, ['dtT', 'cst'], bk(2, [0]))
        tr(PB[2][:, 32:64], aT[:], id32, ['aT', 'cst'], bk(2, [0]))
        cp('dve', dta[:], PB[2][:, 0:64], bk(2, [0]), ['dta'])
        yield
        mm(PB[3][:, 0:32], cm('strict'), dta[:, 32:64], True, True, ['cst', 'dta'], bk(3, [0]))
        act(dstok[:], PB[3][:, 0:32], AF.Exp, bk(3, [0]), ['dstok'])
        tt('pool', rhsa[:], dta[:, 32:64].unsqueeze(2).to_broadcast([128, 32, 128]),
           cm('incl').unsqueeze(1).to_broadcast([128, 32, 128]), ALU.mult, ['dta', 'cst'], ['rhsa'])
        yield

    def consumer_rw(idx):
        def f(pp, pk):
            ri = cnt['rawr'] % 8
            cnt['rawr'] += 1
            r_ = rawr[ri]
            rk_ = 'rawr%d' % ri
            act(r_[:, 1:1 + TB], pp, AF.Copy, pk, [rk_])
            cp('pool', r_[:, 0:1], hist_r[:, idx, :], ['hr%d' % idx], [rk_])
            yield
            ti_ = cnt['tmpr'] % 4
            cnt['tmpr'] += 1
            t_ = tmpr[ti_]
            tk_ = 'tmpr%d' % ti_
            ts('dve', t_[:], r_[:, 1:1 + TB], dcol('omm', idx), None, ALU.mult, None, [rk_, 'der'], [tk_])
            yield
            stt(sh[:, idx, :], r_[:, 0:TB], pcol('mu', idx), t_[:], ALU.mult, ALU.add, [rk_, tk_, 'par'], ['sh%d' % idx])
            cp('pool', hist_r[:, idx, :], r_[:, TB:TB + 1], [rk_], ['hr%d' % idx])
            yield
        return f

    def consumer_gate(i):
        def f(pp, pk):
            act(G[:, i, :], pp, AF.Sigmoid, pk + ['par'], ['G%d' % i], bias=pcol('bg', i))
            yield
        return f

    glist = []
    glist.append((O_DT, 32, consumer_dt))
    for c in range(16):
        glist.append((O_X + c * 128, 128, consumer_xbc(c)))
    for g in range(4):
        glist.append((O_B + g * 128, 128, consumer_xbc(16 + g)))
    for g in range(4):
        glist.append((O_C + g * 128, 128, consumer_xbc(20 + g)))
    for c in range(16):
        glist.append((O_Z + c * 128, 128, consumer_z(c)))
    n_ssm_groups = len(glist)
    glist.append((O_WA, 128, consumer_rw(32)))
    for c in range(8):
        for i, o in enumerate([O_R, O_K, O_V, O_G]):
            glist.append((o + c * 128, 128, consumer_rw(i * 8 + c)))
    n_rw_groups = len(glist)
    for i in range(16):
        glist.append((O_GS + i * 128, 128, consumer_gate(i)))
    NG = len(glist)
    if stage <= 1:
        g_end = n_ssm_groups
    elif stage == 2:
        g_end = n_rw_groups
    else:
        g_end = NG
    sched = [(b, g) for b in range(nblk) for g in range(g_end)]
    pos = {bg: i for i, bg in enumerate(sched)}

    def issue_w(si):
        if si >= len(sched):
            return
        col0, M, _ = glist[sched[si][1]]
        s = si % NW
        ch = chunk_of(col0)
        if M == 128:
            ld(wt[s][:, :, :], win_bf[ch], 'wt%d' % s, reads=['wbf'])
        else:
            ld(wt[s][:, :, 0:M], win_bf[ch, :, :, 0:M], 'wt%d' % s, reads=['wbf'])

    def chunk_of(col0):
        if col0 < O_DT:
            return col0 // 128
        if col0 == O_DT:
            return 40
        return 41 + (col0 - O_R) // 128

    GL = 4
    INTERLEAVE = False

    def run_groups(blk, g0, g1):
        for _ in run_groups_gen(blk, g0, g1):
            pass

    def run_groups_gen(blk, g0, g1):
        g = g0
        pending = []
        while g < g1:
            gens = []
            for gg in range(g, min(g + GL, g1)):
                si = pos[(blk, gg)]
                issue_w(si + NW - 1)
                col0, M, cons = glist[gg]
                s = si % NW
                q = [0, 1, 4, 5][si % 4]
                pp = PB[q][0:M, 0:TB]
                pk = bk(q)
                for kc in range(8):
                    mm(pp, wt[s][:, kc, 0:M], hT[:, kc, :], kc == 0, kc == 7, ['wt%d' % s, 'hT%d' % kc], pk)
                gens.append(cons(pp, pk))
            live = []
            for gn in gens:
                try:
                    next(gn)
                    live.append(gn)
                except StopIteration:
                    pass
            roundrobin(pending)
            pending = live
            g += GL
            yield
        roundrobin(pending)
        yield

    NB = 4
    stg = [arena[:, i * 2048:(i + 1) * 2048] for i in range(NB)]
    stb = [arena[:, 8192 + i * 1024:8192 + (i + 1) * 1024].bitcast(BF16) for i in range(NB)]
    wst_keys = []
    slabs = []

    def convert(src, nrc, segs, dst):
        for rc in range(nrc):
            for (c0, w, ch0) in segs:
                slabs.append((src, rc, c0, w, ch0, dst))

    segs_in = [(0, 2048, 0), (2048, 2048, 16), (4096, 1024, 32), (O_DT, 32, 40), (O_R, 2048, 41), (O_R + 2048, 2048, 57),
               (O_R + 4096, 2048, 73), (O_R + 6144, 128, 89)]
    convert(win_d, 8, segs_in, win_bf)
    convert(wssm_d, 16, [(0, 1024, 0)], wssm_bf)
    convert(wrw_d, 8, [(0, 1024, 0)], wrw_bf)
    convert(wout_d, 8, [(0, 1024, 0)], wout_bf)

    def slab_ld(n):
        src, rc, c0, w, ch0, dst = slabs[n]
        i = n % NB
        P.dma('sp', lambda e: e.dma_start(out=stg[i][:, 0:w], in_=src[rc * 128:(rc + 1) * 128, c0:c0 + w]),
              writes=['stg%d' % i], semkey='stg%d' % i)

    def slab_cast_st(n):
        src, rc, c0, w, ch0, dst = slabs[n]
        i = n % NB
        eng = ['act', 'dve', 'pool'][n % 3]
        cp(eng, stb[i][:, 0:w], stg[i][:, 0:w], ['stg%d' % i], ['stb%d' % i])
        if w >= 128:
            nch = w // 128
            d_ap = dst[ch0:ch0 + nch, :, rc, :].rearrange("c p m -> p c m")
            s_ap = stb[i][:, 0:w].rearrange("p (c m) -> p c m", m=128)
        else:
            d_ap = dst[ch0, :, rc, 0:w]
            s_ap = stb[i][:, 0:w]
        k = 'wst%d' % n
        wst_keys.append(k)
        P.dma('sp', lambda e: e.dma_start(out=d_ap, in_=s_ap), reads=['stb%d' % i], writes=[k], semkey='wost%d' % i)

    for n in range(min(NB, len(slabs))):
        slab_ld(n)
    for n in range(len(slabs)):
        slab_cast_st(n)
        if n + NB < len(slabs):
            slab_ld(n + NB)
    P.add('pool', lambda e: e.memset(bar[:], 0.0), wst_keys, ['wbf', 'bar'])
    barrier(['stg%d' % i for i in range(NB)] + ['stb%d' % i for i in range(NB)], K_SSD)
    ld(wrwS[:], wrw_bf.rearrange("o p c m -> p o c m"), 'wrwS', reads=['wbf'])

    for si in range(NW - 1):
        issue_w(si)

    def dump(blk, off, ap, keys, width):
        i = P.dma('sp', lambda e: e.dma_start(out=dbg_d[blk, 0:ap.shape[0], off:off + width], in_=ap), reads=keys,
                  semkey='dbg%d_%d' % (blk, off))
        final_ops.append(i)

    def ssd_norm_gen(g):
        bn = [2, 3, 6, 7][g]
        sq = t1[g // 2][:].bitcast(BF16)[:, (g % 2) * TB:(g % 2 + 1) * TB]
        sqk = 't1_%d_%d' % (g // 2, g % 2)
        rs = tmpb[8 + g // 2][g % 2]
        rsk = 'tm%d_%d' % (8 + g // 2, g % 2)
        for q in range(4):
            c = g * 4 + q
            tt('pool', yz[:, c, :], yz[:, c, :], zs[:, c, :], ALU.mult, ['yz%d' % c, 'zs%d' % c], ['yz%d' % c])
            yield
        for q in range(4):
            c = g * 4 + q
            act(sq, yz[:, c, :], AF.Square, ['yz%d' % c], [sqk])
            mm(pq(bn, 0), cstb[:, 128:256], sq, q == 0, q == 3, [sqk, 'cstb'], bk(bn))
            yield
        act(rs[:], pq(bn, 0), AF.Ln, bk(bn), [rsk], scale=1.0 / 512, bias=EPS)
        yield
        act(rs[:], rs[:], AF.Exp, [rsk], [rsk], scale=-0.5)
        yield
        for q in range(4):
            c = g * 4 + q
            stt(yn[:, c, :], yz[:, c, :], pcol('sng', c), rs[:], ALU.mult, ALU.mult, ['yz%d' % c, rsk, 'par'], ['yn%d' % c])
            yield

    def ssd_core(blk):
        for c4 in range(4):
            for q in range(4):
                c = c4 * 4 + q
                tr(pq(2, q), xbc[:, c, :], ident, ['xbc%d' % c, 'cst'], bk(2, [q]))
            tt('dve', Xdt[:, c4 * 512:(c4 + 1) * 512].rearrange("p (h d) -> p h d", h=8),
               PB[2][:, :].rearrange("p (h d) -> p h d", h=8),
               dta[:, c4 * 8:(c4 + 1) * 8].unsqueeze(2).to_broadcast([128, 8, 64]), ALU.mult, bk(2) + ['dta'], ['Xdt%d' % c4])
        for g in range(4):
            tr(pq(2, g), xbc[:, 16 + g, :], ident, ['xbc%d' % (16 + g), 'cst'], bk(2, [g]))
        act(Btok[:].rearrange("p g n -> p (g n)"), PB[2][:, :], AF.Copy, bk(2), ['Btok'])
        for c4 in range(4):
            tt('dve', Xds[:, c4 * 512:(c4 + 1) * 512].rearrange("p (h d) -> p h d", h=8),
               Xdt[:, c4 * 512:(c4 + 1) * 512].rearrange("p (h d) -> p h d", h=8),
               dstok[:, c4 * 8:(c4 + 1) * 8].unsqueeze(2).to_broadcast([128, 8, 64]), ALU.mult, ['Xdt%d' % c4, 'dstok'], ['Xds%d' % c4])
        for g in range(4):
            mm(pq(3, g), xbc[:, 16 + g, :], xbc[:, 20 + g, :], True, True, ['xbc%d' % (16 + g), 'xbc%d' % (20 + g)], bk(3, [g]))
        tt('dve', scm[:], PB[3][:, :].rearrange("p (g l) -> p g l", g=4), cm('incl').unsqueeze(1).to_broadcast([128, 4, 128]),
           ALU.mult, bk(3) + ['cst'], ['scm'])
        halves = [(g, hf) for g in range(4) for hf in range(2)]

        def stage1(i):
            g, hf = halves[i]
            u = i % 2
            h0 = g * 8 + hf * 4
            bs, be = (4, 5) if u == 0 else (0, 1)
            rv = rhsa[:, h0:h0 + 4, :].rearrange("p h l -> p (h l)")
            mm(PB[bs][:, :], cstb[:, 0:128], rv, True, True, ['cstb', 'rhsa'], bk(bs))
            mm(PB[be][:, :], cstb[:, 128:256], rv, True, True, ['cstb', 'rhsa'], bk(be))
            act(dec[u][:].rearrange("p h l -> p (h l)"), PB[bs][:, :], AF.Exp, bk(bs), ['dec%d' % u])
            act(Eg[u][:].rearrange("p h l -> p (h l)"), PB[be][:, :], AF.Exp, bk(be), ['Eg%d' % u])
            tt('dve', mix[u][:], dec[u][:], scm[:, g, :].unsqueeze(1).to_broadcast([128, 4, 128]), ALU.mult,
               ['dec%d' % u, 'scm'], ['mix%d' % u])
            tt('pool', Cdec[u][:], Eg[u][:], xbc[:, 20 + g, :].unsqueeze(1).to_broadcast([128, 4, 128]), ALU.mult,
               ['Eg%d' % u, 'xbc%d' % (20 + g)], ['Cdec%d' % u])

        def stage2(i):
            g, hf = halves[i]
            u = i % 2
            h0 = g * 8 + hf * 4
            by = 6 if g % 2 == 0 else 2
            for j in range(4):
                h = h0 + j
                c = h // 2
                half = h % 2
                q = c % 4
                yo = pq(by, q, half * 64, (half + 1) * 64)
                mm(yo, Xdt[:, h * 64:(h + 1) * 64], mix[u][:, j, :], True, False, ['Xdt%d' % (h // 8), 'mix%d' % u], bk(by))
                mm(yo, Sst[:, h * 64:(h + 1) * 64], Cdec[u][:, j, :], False, True, ['S%d' % g, 'Cdec%d' % u], bk(by))
            mm(PB[7][:, 0:256], Btok[:, g, :], Xds[:, h0 * 64:(h0 + 4) * 64], True, True, ['Btok', 'Xds%d' % g], bk(7))
            sv = Sst[:, h0 * 64:(h0 + 4) * 64]
            tt('dve', sv.rearrange("p (h d) -> p h d", h=4), sv.rearrange("p (h d) -> p h d", h=4),
               Eg[u][:, :, 127:128].to_broadcast([128, 4, 64]), ALU.mult, ['S%d' % g, 'Eg%d' % u], ['S%d' % g])
            tt('dve', sv, sv, PB[7][:, 0:256], ALU.add, ['S%d' % g] + bk(7), ['S%d' % g])

        def stage3(g):
            by = 6 if g % 2 == 0 else 2
            for q in range(4):
                c = g * 4 + q
                stt(yz[:, c, :], xbc[:, c, :], pcol('dsk', c), pq(by, q), ALU.mult, ALU.add, ['xbc%d' % c, 'par'] + bk(by), ['yz%d' % c])

        stage1(0)
        pend3 = None
        for i in range(8):
            if i + 1 < 8:
                stage1(i + 1)
            stage2(i)
            if pend3 is not None:
                stage3(pend3)
                pend3 = None
            if halves[i][1] == 1:
                pend3 = halves[i][0]
        stage3(pend3)

    def front_p1():
        act(sh[0:64, 32, :], sh[0:64, 32, :], AF.Tanh, ['sh32'], ['sh32'])
        for c in range(8):
            T_ = [tmpb[k][c % 2] for k in range(NTMP)]
            TK = ['tm%d_%d' % (k, c % 2) for k in range(NTMP)]
            bw, ba = (2, 3) if c % 2 == 0 else (6, 7)
            mm(pq(bw, 0), w2a2[0:64, c * 128:(c + 1) * 128], sh[0:64, 32, :], True, True, ['w2a2', 'sh32'], bk(bw))
            mm(pq(ba, 0), w2a2[64:128, c * 128:(c + 1) * 128], sh[64:128, 32, :], True, True, ['w2a2', 'sh32'], bk(ba))
            act(T_[0][:], pq(bw, 0), AF.Sigmoid, bk(bw) + ['par'], [TK[0]], bias=pcol('w0', c))
            ts('dve', lw[:, c, :], T_[0][:], LWS, None, ALU.mult, None, [TK[0]], ['lw%d' % c])
            act(av[:, c, :], pq(ba, 0), AF.Sigmoid, bk(ba) + ['par'], ['av%d' % c], bias=pcol('a0', c))

    def front_silu():
        for c in range(8):
            act(sh[:, 24 + c, :], sh[:, 24 + c, :], AF.Silu, ['sh%d' % (24 + c)], ['sh%d' % (24 + c)])

    def front_c(c):
        if True:
            T_ = [tmpb[k][c % 2] for k in range(NTMP)]
            TK = ['tm%d_%d' % (k, c % 2) for k in range(NTMP)]
            r_, k_, v_, g_ = sh[:, c, :], sh[:, 8 + c, :], sh[:, 16 + c, :], sh[:, 24 + c, :]
            rk_, kk_, vk_ = 'sh%d' % c, 'sh%d' % (8 + c), 'sh%d' % (16 + c)
            ts('dve', T_[1][:], k_, pcol('kk', c), None, ALU.mult, None, [kk_, 'par'], [TK[1]])
            yield
            T2b = T_[2][:].bitcast(BF16)[:, 0:TB]
            act(T2b, T_[1][:], AF.Square, [TK[1]], [TK[2]])
            yield
            mm(pq(6, c % 2), cstb[:, 256:384], T2b, True, True, ['cstb', TK[2]], bk(6))
            yield
            ts('dve', T_[2][:], pq(6, c % 2), 1e-24, None, ALU.max, None, bk(6), [TK[2]])
            yield
            act(T_[2][:], T_[2][:], AF.Ln, [TK[2]], [TK[2]])
            yield
            act(T_[2][:], T_[2][:], AF.Exp, [TK[2]], [TK[2]], scale=-0.5)
            yield
            tt('dve', kkb[:, c, :], T_[1][:], T_[2][:], ALU.mult, [TK[1], TK[2]], ['kkb%d' % c])
            yield
            ts('dve', T_[3][:], av[:, c, :], pcol('ka', c), dcol('omka', c), ALU.mult, ALU.add, ['av%d' % c, 'par', 'der'], [TK[3]])
            yield
            tt('dve', kp[:, c, :], k_, T_[3][:], ALU.mult, [kk_, TK[3]], ['kp%d' % c])
            yield
            stt(T_[4][:], r_, pcol('rk', c), kp[:, c, :], ALU.mult, ALU.mult, [rk_, 'par', 'kp%d' % c], [TK[4]])
            yield
            mm(pq(7, c % 2), cm('blk'), T_[4][:], True, True, ['cst', TK[4]], bk(7))
            yield
            tt('dve', bon[:, c, :], pq(7, c % 2), v_, ALU.mult, bk(7) + [vk_], ['bon%d' % c])
            yield
            P.add('dve', lambda e, o=T_[5][:], m=cm('rmask'), l=lw[:, c, :]: e.tensor_tensor_scan(out=o, data0=m, data1=l, initial=0.0,
                                                                                                 op0=ALU.mult, op1=ALU.add),
                  ['cst', 'lw%d' % c], [TK[5]])
            yield
            cum = T_[5]
            act(EP[:, c, :], cum[:], AF.Exp, [TK[5]], ['EP%d' % c])
            yield
            act(T_[6][:], cum[:], AF.Exp, [TK[5]], [TK[6]], scale=-1.0)
            yield
            tt('dve', T_[7][:], cum[:], lw[:, c, :], ALU.subtract, [TK[5], 'lw%d' % c], [TK[7]])
            yield
            act(T_[7][:], T_[7][:], AF.Exp, [TK[7]], [TK[7]])
            yield
            c3 = cum[:].rearrange("p (j t) -> p j t", j=2)
            tt('dve', T_[8][:].rearrange("p (j t) -> p j t", j=2), c3[:, :, 63:64].to_broadcast([128, 2, 64]), c3, ALU.subtract,
               [TK[5]], [TK[8]])
            yield
            act(T_[8][:], T_[8][:], AF.Exp, [TK[8]], [TK[8]])
            yield

            def v3(ap):
                return ap.rearrange("p (j t) -> p j t", j=2)
            tt('dve', RA[:, c, :, 64:128], v3(r_), v3(EP[:, c, :]), ALU.mult, [rk_, 'EP%d' % c], ['RA%d' % c])
            yield
            stt(RA[:, c, :, 0:64], v3(kkb[:, c, :]), -1.0, v3(T_[7][:]), ALU.mult, ALU.mult, ['kkb%d' % c, TK[7]], ['RA%d' % c])
            yield
            tt('dve', T_[9][:], kkb[:, c, :], av[:, c, :], ALU.mult, ['kkb%d' % c, 'av%d' % c], [TK[9]])
            yield
            tt('dve', BKt[:, c, :, 0:64], v3(T_[9][:]), v3(T_[6][:]), ALU.mult, [TK[9], TK[6]], ['BKt%d' % c])
            yield
            tt('pool', BKt[:, c, :, 128:192], v3(T_[9][:]), v3(T_[6][:]), ALU.mult, [TK[9], TK[6]], ['BKt%d' % c])
            yield
            tt('dve', BKt[:, c, :, 64:128], v3(kp[:, c, :]), v3(T_[6][:]), ALU.mult, ['kp%d' % c, TK[6]], ['BKt%d' % c])
            yield
            tt('pool', BKh[:, c, :, 0:64], v3(T_[9][:]), v3(T_[8][:]), ALU.mult, [TK[9], TK[8]], ['BKh%d' % c])
            yield
            tt('pool', BKh[:, c, :, 128:192], v3(T_[9][:]), v3(T_[8][:]), ALU.mult, [TK[9], TK[8]], ['BKh%d' % c])
            yield
            tt('pool', BKh[:, c, :, 64:128], v3(kp[:, c, :]), v3(T_[8][:]), ALU.mult, ['kp%d' % c, TK[8]], ['BKh%d' % c])
            yield
            cp('pool', VV[:, c, :, 64:128], v3(v_), [vk_], ['VV%d' % c])
            yield

    def rwkv_core(blk):
        for c2 in range(4):
            for hh in range(2):
                pr = hh * 64
                po = 64 - pr
                ph = po // 64
                off = 0 if hh == 1 else 64
                sl = slice(po, po + 64)
                idh = cst[pr:pr + 64, CC['ident'] + pr:CC['ident'] + pr + 64]
                ido = cst[po:po + 64, CC['ident'] + po:CC['ident'] + po + 64]
                RES = {}

                def inv_gen(cc, j):
                    c = 2 * c2 + cc
                    ui = cc * 2 + j
                    am = AM[ui]
                    amk = 'AM%d' % ui
                    bkt, uvh = BKtok[ui], UV[ui]
                    bktk, uvk_v = 'BKtok%d' % ui, 'UVv%d' % ui
                    cs = ui * 64

                    def reg(bn):
                        return PB[bn][po:po + 64, cs:cs + 64], bk(bn, halves=[ph])
                    tr(PB[0][:, ui * 128:ui * 128 + 64], BKh[pr:pr + 64, c, j, off:off + 128], idh, ['BKh%d' % c, 'cst'], bk(0))
                    yield
                    tr(PB[0][:, ui * 128 + 64:ui * 128 + 128], VV[pr:pr + 64, c, j, off:off + 128], idh, ['VV%d' % c, 'cst'], bk(0))
                    yield
                    mm(pq(1, ui), BKt[pr:pr + 64, c, j, off:off + 128], RA[pr:pr + 64, c, j, :], True, True, ['BKt%d' % c, 'RA%d' % c], bk(1))
                    yield
                    r0, k0 = reg(6)
                    mm(r0, RA[pr:pr + 64, c, j, 0:64], BKt[pr:pr + 64, c, j, 0:64], True, True, ['BKt%d' % c, 'RA%d' % c], k0)
                    yield
                    cp('act', bkt[:], PB[0][:, ui * 128:ui * 128 + 64], bk(0), [bktk])
                    yield
                    cp('act', uvh[pr:pr + 64, :], PB[0][pr:pr + 64, ui * 128 + 64:ui * 128 + 128], bk(0), [uvk_v])
                    yield
                    tt('dve', am[:], pq(1, ui), cm('m4'), ALU.mult, bk(1) + ['cst'], [amk])
                    yield
                    p0, pt0, rt = Pm[ui], Ptm[ui], Rt[ui]
                    pk0 = ['Pm%d_%d' % (ui, k) for k in range(2)]
                    ptk = ['Ptm%d_%d' % (ui, k) for k in range(2)]
                    rtk = ['Rt%d_%d' % (ui, k) for k in range(2)]
                    tt('dve', p0[0][sl, :], r0, cm('low', 64, po, po + 64), ALU.mult, k0 + ['cst'], [pk0[0]])
                    yield
                    def bfv(t_):
                        return t_[:].bitcast(BF16)[sl, 0:64]
                    tt('pool', bfv(rt[0]), am[sl, 0:64], ido, ALU.add, [amk, 'cst'], [rtk[0]])
                    yield
                    Pcur, Pck = p0[0][sl, :], pk0[0]
                    Ptcur, Ptck = am[sl, 0:64], amk
                    Rcur, Rck = bfv(rt[0]), rtk[0]
                    rP, kP = reg(2)
                    rPt, kPt = reg(3)
                    rR, kR = reg(4)
                    for lvl in range(1, 6):
                        nb = lvl % 2
                        mm(rP, Ptcur, Pcur, True, True, [Ptck, Pck], kP)
                        yield
                        if lvl < 5:
                            mm(rPt, Pcur, Ptcur, True, True, [Ptck, Pck], kPt)
                            yield
                        cp('act', bfv(p0[nb]), rP, kP, [pk0[nb]])
                        yield
                        if lvl < 5:
                            cp('act', bfv(pt0[nb]), rPt, kPt, [ptk[nb]])
                            yield
                        mm(rR, bfv(p0[nb]), Rcur, True, True, [pk0[nb], Rck], kR)
                        yield
                        rout = rt[nb][sl, :] if lvl == 5 else bfv(rt[nb])
                        tt('dve', rout, Rcur, rR, ALU.add, [Rck] + kR, [rtk[nb]])
                        yield
                        Pcur, Pck = bfv(p0[nb]), pk0[nb]
                        if lvl < 5:
                            Ptcur, Ptck = bfv(pt0[nb]), ptk[nb]
                        Rcur, Rck = rout, rtk[nb]
                    RES[(cc, j)] = (Rcur, Rck)

                def seq_gen(cc, j):
                    c = 2 * c2 + cc
                    ui = cc * 2 + j
                    am = AM[ui]
                    amk = 'AM%d' % ui
                    bkt, uvh, xs = BKtok[ui], UV[ui], X1s[ui]
                    bktk, uvk_v, uvk_u, xsk = 'BKtok%d' % ui, 'UVv%d' % ui, 'UVu%d' % ui, 'X1s%d' % ui
                    Rcur, Rck = RES[(cc, j)]
                    srk = 'SR%d_%d' % (c, hh)
                    srv = SR[pr:pr + 64, c, :]
                    cs = ui * 64
                    rX, kX = PB[5][po:po + 64, cs:cs + 64], bk(5, halves=[ph])
                    rU, kU = PB[6][po:po + 64, cs:cs + 64], bk(6, halves=[ph])
                    yk = bk(7, halves=[hh])
                    mm(rX, RA[pr:pr + 64, c, j, 0:64], srv, True, False, ['RA%d' % c, srk], kX)
                    mm(rX, am[pr:pr + 64, 0:64], uvh[pr:pr + 64, :], False, True, [amk, uvk_v], kX)
                    yield
                    cp('act', xs[sl, :], rX, kX, [xsk])
                    yield
                    mm(rU, Rcur, xs[sl, :], True, True, [Rck, xsk], kU)
                    yield
                    cp('dve', uvh[sl, :], rU, kU, [uvk_u])
                    yield
                    uvk = [uvk_v, uvk_u]
                    mm(PB[7][pr:pr + 64, cs:cs + 64], srv, RA[pr:pr + 64, c, j, 64:128], True, True, [srk, 'RA%d' % c], yk)
                    yield
                    mm(PB[7][pr:pr + 64, 256 + cs:256 + cs + 64], uvh[:, :], am[:, 64:128], True, True, uvk + [amk], yk)
                    yield
                    so = PB[5][pr:pr + 64, cs:cs + 64]
                    sok = bk(5, halves=[hh])
                    mm(so, bkt[:, :], uvh[:, :], True, True, [bktk] + uvk, sok)
                    yield
                    ts('pool', srv, srv, EP[pr:pr + 64, c, j * 64 + 63:j * 64 + 64], None, ALU.mult, None, [srk, 'EP%d' % c], [srk])
                    yield
                    tt('dve', srv, srv, so, ALU.add, [srk] + sok, [srk])
                    yield

                roundrobin([inv_gen(cc, j) for cc in range(2) for j in range(2)])
                for j in range(2):
                    roundrobin([seq_gen(cc, j) for cc in range(2)])
                hs = slice(pr, pr + 64)
                for cc in range(2):
                    c = 2 * c2 + cc
                    cp('act', ybuf[hs, c, :], PB[7][hs, cc * 128:cc * 128 + 128], bk(7, halves=[hh]), ['ybuf%d' % c])
                    tt('dve', ybuf[hs, c, :], ybuf[hs, c, :], PB[7][hs, 256 + cc * 128:256 + cc * 128 + 128], ALU.add,
                       ['ybuf%d' % c] + bk(7, halves=[hh]), ['ybuf%d' % c])

    def gn_gen(c):
        T0, T1 = tmpb[c][0], tmpb[c][1]
        K0, K1 = 'tm%d_0' % c, 'tm%d_1' % c
        bm, qm = 2 + c // 4, c % 4
        bv = 6 + c // 4
        mm(pq(bm, qm), cm('blk'), ybuf[:, c, :], True, True, ['cst', 'ybuf%d' % c], bk(bm))
        yield
        stt(T0[:], pq(bm, qm), -1.0 / 64, ybuf[:, c, :], ALU.mult, ALU.add, bk(bm) + ['ybuf%d' % c], [K0])
        yield
        T1b = T1[:].bitcast(BF16)[:, 0:TB]
        act(T1b, T0[:], AF.Square, [K0], [K1])
        yield
        mm(pq(bv, qm), cstb[:, 256:384], T1b, True, True, ['cstb', K1], bk(bv))
        yield
        act(T1[:], pq(bv, qm), AF.Ln, bk(bv), [K1], scale=1.0 / 64, bias=GN_EPS)
        yield
        act(T1[:], T1[:], AF.Exp, [K1], [K1], scale=-0.5)
        yield
        tt('dve', T0[:], T0[:], T1[:], ALU.mult, [K0, K1], [K0])
        yield
        ts('dve', T0[:], T0[:], pcol('gng', c), pcol('gnb', c), ALU.mult, ALU.add, [K0, 'par'], [K0])
        yield
        tt('dve', T0[:], T0[:], bon[:, c, :], ALU.add, [K0, 'bon%d' % c], [K0])
        yield
        tt('dve', yr[:, c, :], T0[:], sh[:, 24 + c, :], ALU.mult, [K0, 'sh%d' % (24 + c)], ['yr%d' % c])
        yield

    WT = []
    for o_ in range(8):
        WT.append(wssm_bf[o_, :, 0:8, :])
        WT.append(wssm_bf[o_, :, 8:16, :])
    for eo_ in range(8):
        WT.append(wout_bf[eo_])

    def wo_issue(t):
        if t >= 24:
            return
        ld(wob[:, t % 4], WT[t], 'wo%d' % (t % 4), reads=['wbf'])

    def out_prefetch():
        for t in range(3):
            wo_issue(t)

    def out_gen(blk, xs_, xk):
        for o in range(8):
            for hf in range(2):
                t = 2 * o + hf
                wo_issue(t + 3)
                for c8 in range(8):
                    c = hf * 8 + c8
                    mm(pq(2 + o % 2, 0), wob[:, t % 4, c8, :], yn[:, c, :], c == 0, c == 15, ['wo%d' % (t % 4), 'yn%d' % c], bk(2 + o % 2))
            tt('dve', tmpb[3][o % 2][:], pq(2 + o % 2, 0), G[:, o, :], ALU.mult, bk(2 + o % 2) + ['G%d' % o], ['tm3_%d' % (o % 2)])
            for c in range(8):
                mm(pq(6 + o % 2, 0), wrwS[:, o, c, :], yr[:, c, :], c == 0, c == 7, ['wrwS', 'yr%d' % c], bk(6 + o % 2))
            T_ = tmpb[0][o % 2]
            tk = 'tm0_%d' % (o % 2)
            tt('dve', T_[:], pq(6 + o % 2, 0), G[:, 8 + o, :], ALU.mult, bk(6 + o % 2) + ['G%d' % (8 + o)], [tk])
            tt('dve', mT[:, o, :], tmpb[3][o % 2][:], T_[:], ALU.add, ['tm3_%d' % (o % 2), tk], ['mT%d' % o])
            yield
        for eo in range(8):
            t = 16 + eo
            wo_issue(t + 3)
            for o in range(8):
                mm(pq(2 + eo % 2, 0), wob[:, t % 4, o, :], mT[:, o, :], o == 0, o == 7, ['wo%d' % (t % 4), 'mT%d' % o], bk(2 + eo % 2))
            cp('act', oT[:, eo, :], pq(2 + eo % 2, 0), bk(2 + eo % 2), ['oT%d' % eo])
            T_ = tmpb[1][eo % 2]
            tk = 'tm1_%d' % (eo % 2)
            Tb = T_[:].bitcast(BF16)[:, 0:TB]
            act(Tb, oT[:, eo, :], AF.Square, ['oT%d' % eo], [tk])
            mm(pq(6, 0), cstb[:, 128:256], Tb, eo == 0, eo == 7, ['cstb', tk], bk(6))
            yield
        T_ = tmpb[2][0]
        tk = 'tm2_0'
        act(T_[:], pq(6, 0), AF.Ln, bk(6), [tk], scale=1.0 / D, bias=EPS)
        act(T_[:], T_[:], AF.Exp, [tk], [tk], scale=-0.5)
        for eo in range(8):
            stt(on[:, eo, :], oT[:, eo, :], pcol('post', eo), T_[:], ALU.mult, ALU.mult, ['oT%d' % eo, 'par', tk], ['on%d' % eo])
        yield
        for hb in range(2):
            for q_ in range(4):
                eo = hb * 4 + q_
                tr(pq(7, q_), on[:, eo, :], ident, ['on%d' % eo, 'cst'], bk(7))
            fin = xn[:, hb * 512:(hb + 1) * 512]
            tt('dve', fin, PB[7][:, :], xs_[:, hb * 512:(hb + 1) * 512], ALU.add, bk(7) + [xk], ['xn'])
            i = P.dma('sp', lambda e, hb=hb, fin=fin: e.dma_start(out=out_d[blk * TB:(blk + 1) * TB, hb * 512:(hb + 1) * 512], in_=fin),
                      reads=['xn'], semkey='fin%d' % hb)
            final_ops.append(i)
            yield

    def ld_x(blk):
        ld(xt[blk % 2][:], x_d[blk * TB:(blk + 1) * TB, :], 'xt%d' % (blk % 2))

    def stageA_gen(blk, do_ld=True):
        xs_ = xt[blk % 2]
        xk = 'xt%d' % (blk % 2)
        if do_ld:
            ld_x(blk)
        act(xn[:], xs_[:], AF.Square, [xk], ['ssq', 'xn'], accum_out=ssq[:])
        act(rstd[:], ssq[:], AF.Ln, ['ssq'], ['rstd'], scale=1.0 / D, bias=EPS)
        act(rstd[:], rstd[:], AF.Exp, ['rstd'], ['rstd'], scale=-0.5)
        ts('dve', xn[:], xs_[:], rstd[:], None, ALU.mult, None, [xk, 'rstd'], ['xn'])
        yield
        for c in range(8):
            b_, q_ = c // 4, c % 4
            tr(pq(b_, q_), xn[:, c * 128:(c + 1) * 128], ident, ['xn', 'cst'], bk(b_))
        for c in range(8):
            b_, q_ = c // 4, c % 4
            act(hT[:, c, :], pq(b_, q_), AF.Copy, bk(b_) + ['par'], ['hT%d' % c], scale=pcol('pre', c))
        yield

    def front_gen(blk, do_a=True):
        if do_a:
            for _ in stageA_gen(blk):
                yield
        if blk > 0:
            barrier(K_RW, K_SSD_E)
        for _ in run_groups_gen(blk, 0, n_ssm_groups):
            yield

    for _ in front_gen(0):
        pass
    for blk in range(nblk):
        xs_ = xt[blk % 2]
        xk = 'xt%d' % (blk % 2)
        if blk > 0:
            barrier(K_OUT + K_RW, K_SSD_R)
        ssd_core(blk)
        roundrobin([ssd_norm_gen(g_) for g_ in range(4)])
        barrier(K_SSD, K_RW)
        bi = 0
        nfc = 0
        for _ in run_groups_gen(blk, n_ssm_groups, n_rw_groups):
            if bi == 1:
                front_p1()
            elif bi >= 2 and nfc < 8 and 4 * (nfc + 1) + 4 < 4 * (bi - 1):
                roundrobin([front_c(nfc), front_c(nfc + 1)])
                nfc += 2
            bi += 1
        while nfc < 8:
            roundrobin([front_c(nfc), front_c(nfc + 1)])
            nfc += 2
        front_silu()
        out_prefetch()
        if blk + 1 < nblk:
            ld_x(blk + 1)
        rwkv_core(blk)
        barrier(['BKt%d' % c_ for c_ in range(8)] + ['BKh%d' % c_ for c_ in range(8)], K_OUT)
        roundrobin([gn_gen(c_) for c_ in range(8)])
        run_groups(blk, n_rw_groups, NG)
        if INTERLEAVE:
            gens = [out_gen(blk, xs_, xk)]
            if blk + 1 < nblk:
                gens.append(front_gen(blk + 1))
            roundrobin(gens)
        else:
            gens = [out_gen(blk, xs_, xk)]
            if blk + 1 < nblk:
                gens.append(stageA_gen(blk + 1, do_ld=False))
            roundrobin(gens)
            if blk + 1 < nblk:
                roundrobin([front_gen(blk + 1, do_a=False)])

    P.emit(final_wait_ops=final_ops)
    return nc, P


def pack_params(inp):
    p = np.zeros((128, NPAR), np.float32)

    def cmaj(v):
        v = np.asarray(v, np.float32).reshape(-1, 128)
        return v.T

    def put(name, arr):
        p[:arr.shape[0], PC[name]:PC[name] + arr.shape[1]] = arr

    put('pre', cmaj(inp['pre_gain'][0]))
    put('post', cmaj(inp['post_gain'][0]))
    cw = np.asarray(inp['conv_w'][0], np.float32)
    cwp = np.zeros((128, 24, 4), np.float32)
    for k in range(4):
        cwp[:, :, k] = cmaj(cw[k])
    put('cw', cwp.reshape(128, 96))
    put('cb', cmaj(inp['conv_b'][0]))
    put('dsk', cmaj(np.repeat(np.asarray(inp['d_skip'][0], np.float32), 64)))
    put('sng', cmaj(inp['ssm_norm_gain'][0]))
    put('mu', cmaj(inp['rwkv_mu'][0]))
    put('w0', cmaj(inp['decay_w0'][0]))
    put('a0', cmaj(inp['iclr_a0'][0]))
    put('kk', cmaj(inp['k_k'][0]))
    put('ka', cmaj(inp['k_a'][0]))
    put('rk', cmaj(np.asarray(inp['r_k'][0], np.float32).reshape(-1)))
    put('gng', cmaj(inp['gn_gain'][0]))
    put('gnb', cmaj(inp['gn_bias'][0]))
    put('bg', cmaj(inp['b_gate'][0]))
    p[:32, PC['dtb']] = np.asarray(inp['dt_bias'][0], np.float32)
    p[:32, PC['alog']] = np.asarray(inp['a_log'][0], np.float32)
    return p


_CACHE = {}


def make_in_maps(inp, nblk=NBLK, ncores=4):
    par = pack_params(inp)
    cst = make_consts()
    w2a2 = np.concatenate([np.asarray(inp['decay_w2'][0], np.float32), np.asarray(inp['iclr_a2'][0], np.float32)], axis=0)
    shared = {
        "w_in": np.ascontiguousarray(np.asarray(inp['w_in'][0], np.float32)),
        "par": par, "cst": cst, "w2a2": np.ascontiguousarray(w2a2),
        "w_ssm": np.ascontiguousarray(np.asarray(inp['w_branch_ssm'][0], np.float32)),
        "w_rwkv": np.ascontiguousarray(np.asarray(inp['w_branch_rwkv'][0], np.float32)),
        "w_out": np.ascontiguousarray(np.asarray(inp['w_out'][0], np.float32)),
    }
    x = np.asarray(inp['x'], np.float32)
    maps = []
    for b in range(ncores):
        m = dict(shared)
        m["x"] = np.ascontiguousarray(x[b, :nblk * TB])
        maps.append(m)
    return maps


def kernel(**inputs):
    if 'nc' not in _CACHE:
        _CACHE['nc'] = build()[0]
    nc = _CACHE['nc']
    maps = make_in_maps(inputs)
    res = run_bass_kernel_spmd(nc, maps, core_ids=list(range(4)))
    out = np.stack([np.asarray(r["out"], np.float32) for r in res.results], axis=0)
    return out
```

```python
import contextlib
import numpy as np
import concourse.bass as bass
import concourse.mybir as mybir
from concourse.alu_op_type import AluOpType as ALU
from concourse.bass_utils import run_bass_kernel_spmd

F32 = mybir.dt.float32
BF16 = mybir.dt.bfloat16
AF = mybir.ActivationFunctionType

CH = 30000


class Prog:
    def __init__(self, nc):
        self.nc = nc
        self.ops = []
        self.last_w = {}
        self.readers = {}
        self.stack = contextlib.ExitStack()

    def sb(self, name, shape, dtype=F32):
        return self.stack.enter_context(self.nc.sbuf_tensor("s_" + name, list(shape), dtype))

    def ps(self, name, shape, dtype=F32):
        return self.stack.enter_context(self.nc.psum_tensor("p_" + name, list(shape), dtype))

    def add(self, eng, fn, reads=(), writes=(), dma=False, semkey=None):
        i = len(self.ops)
        deps = set()
        for k in reads:
            if k in self.last_w:
                deps.add(self.last_w[k])
        for k in writes:
            if k in self.last_w:
                deps.add(self.last_w[k])
            for r in self.readers.get(k, ()):
                deps.add(r)
        deps.discard(i)
        self.ops.append(dict(eng=eng, fn=fn, deps=deps, dma=dma, semkey=semkey))
        for k in reads:
            self.readers.setdefault(k, []).append(i)
        for k in writes:
            self.last_w[k] = i
            self.readers[k] = []
        return i

    def dma(self, eng, fn, reads=(), writes=(), semkey=None):
        return self.add(eng, fn, reads, writes, dma=True, semkey=semkey)

    def emit(self, final_wait_ops=()):
        nc = self.nc
        ops = self.ops
        n = len(ops)
        waited_on = [False] * n
        for i, o in enumerate(ops):
            nd = set()
            for d in o['deps']:
                od = ops[d]
                if (not od['dma']) and (not o['dma']) and od['eng'] == 'pe' and o['eng'] == 'pe':
                    continue
                nd.add(d)
            o['deps'] = nd
            for d in nd:
                waited_on[d] = True
        for i in final_wait_ops:
            waited_on[i] = True
        cnt = {}
        for i, o in enumerate(ops):
            if o['dma']:
                sid = ('dma', o['semkey'])
                cnt[sid] = cnt.get(sid, 0) + 16
                o['ev'] = (sid, cnt[sid])
            elif waited_on[i]:
                e = o['eng']
                m = cnt.get(('m', e), 0)
                cnt[('m', e)] = m + 1
                o['ev'] = (('eng', e, m // CH), (m % CH) + 1)
            else:
                o['ev'] = None
        semids = []
        seen = set()
        for o in ops:
            if o['ev'] is not None and o['ev'][0] not in seen:
                seen.add(o['ev'][0])
                semids.append(o['ev'][0])
        sems = {}
        for k, sid in enumerate(semids):
            sems[sid] = self.stack.enter_context(nc.semaphore("s%d" % k))
        self.n_sems = len(semids)
        per_eng = {e: [] for e in ['pe', 'act', 'dve', 'pool', 'sp']}
        for i, o in enumerate(ops):
            per_eng[o['eng']].append(i)

        def run_engine(ename, eobj):
            waited = {}
            for i in per_eng[ename]:
                o = ops[i]
                need = {}
                for d in o['deps']:
                    sid, v = ops[d]['ev']
                    if sid[0] == 'eng':
                        key = ('eng', sid[1])
                        gv = sid[2] * CH + v
                    else:
                        key = sid
                        gv = v
                    if waited.get(key, 0) >= gv:
                        continue
                    if need.get(key, (None, 0))[1] < gv:
                        need[key] = (sid, gv)
                for key, (sid, gv) in need.items():
                    if sid[0] == 'eng':
                        eobj.wait_ge(sems[sid], gv - sid[2] * CH)
                    else:
                        eobj.wait_ge(sems[sid], gv)
                    waited[key] = gv
                ins = o['fn'](eobj)
                if o['ev'] is not None:
                    ins.then_inc(sems[o['ev'][0]], 16 if o['dma'] else 1)
            if ename == 'sp':
                for i in final_wait_ops:
                    sid, v = ops[i]['ev']
                    eobj.wait_ge(sems[sid], v)

        with nc.Block() as block:
            @block.tensor
            def _(e):
                run_engine('pe', e)

            @block.scalar
            def _(e):
                run_engine('act', e)

            @block.vector
            def _(e):
                run_engine('dve', e)

            @block.gpsimd
            def _(e):
                run_engine('pool', e)

            @block.sync
            def _(e):
                run_engine('sp', e)
        self.stack.close()


D = 1024
T = 4096
TB = 128
NBLK = T // TB
INC = 11424
EPS = 1e-6
GN_EPS = 64 * 1e-5
LWS = -float(np.exp(-0.5))

O_Z, O_X, O_B, O_C, O_DT = 0, 2048, 4096, 4608, 5120
O_R, O_K, O_V, O_G, O_WA = 5152, 6176, 7200, 8224, 9248
O_GS, O_GR = 9376, 10400

PC = {}
_o = 0
for _n, _w in [('pre', 8), ('post', 8), ('cw', 96), ('cb', 24), ('dsk', 16), ('sng', 16), ('mu', 33),
               ('w0', 8), ('a0', 8), ('kk', 8), ('ka', 8), ('rk', 8), ('gng', 8), ('gnb', 8), ('bg', 16),
               ('dtb', 1), ('alog', 1)]:
    PC[_n] = _o
    _o += _w
NPAR = _o
DC = {}
_o = 0
for _n, _w in [('omm', 33), ('omka', 8), ('aneg', 1)]:
    DC[_n] = _o
    _o += _w
NDER = _o

CC = {}
_o = 0
for _n, _w in [('ident', 128), ('incl', 128), ('strict', 128), ('ones', 128), ('blk', 128), ('m4', 128),
               ('low', 64), ('rmask', 128)]:
    CC[_n] = _o
    _o += _w
NCST = _o


def make_consts():
    c = np.zeros((128, NCST), np.float32)
    i = np.arange(128)
    c[:, CC['ident']:CC['ident'] + 128] = np.eye(128)
    c[:, CC['incl']:CC['incl'] + 128] = (i[:, None] <= i[None, :])
    c[:, CC['strict']:CC['strict'] + 128] = (i[:, None] > i[None, :])
    c[:, CC['ones']:CC['ones'] + 128] = 1.0
    c[:, CC['blk']:CC['blk'] + 128] = ((i[:, None] // 64) == (i[None, :] // 64))
    s = i[:, None] % 64
    t = i[None, :] % 64
    u = i[None, :] // 64
    c[:, CC['m4']:CC['m4'] + 128] = np.where(u == 0, s < t, s <= t)
    j = np.arange(64)
    c[:64, CC['low']:CC['low'] + 64] = (j[None, :] < j[:, None])
    c[64:, CC['low']:CC['low'] + 64] = (j[None, :] < j[:, None])
    c[:, CC['rmask']:CC['rmask'] + 128] = (np.arange(128)[None, :] % 64 != 0)
    return c


def build(nblk=NBLK, dbg=None, stage=9):
    nc = bass.Bass("TRN2", target_bir_lowering=False)
    Tn = nblk * TB
    x_d = nc.dram_tensor("x", [Tn, D], F32, kind="ExternalInput").ap()
    win_d = nc.dram_tensor("w_in", [D, INC], F32, kind="ExternalInput").ap()
    par_d = nc.dram_tensor("par", [128, NPAR], F32, kind="ExternalInput").ap()
    cst_d = nc.dram_tensor("cst", [128, NCST], F32, kind="ExternalInput").ap()
    w2a2_d = nc.dram_tensor("w2a2", [128, D], F32, kind="ExternalInput").ap()
    wssm_d = nc.dram_tensor("w_ssm", [2048, D], F32, kind="ExternalInput").ap()
    wrw_d = nc.dram_tensor("w_rwkv", [D, D], F32, kind="ExternalInput").ap()
    wout_d = nc.dram_tensor("w_out", [D, D], F32, kind="ExternalInput").ap()
    out_d = nc.dram_tensor("out", [Tn, D], F32, kind="ExternalOutput").ap()
    NCHK = 90
    win_bf = nc.dram_tensor("win_bf", [NCHK, 128, 8, 128], BF16, kind="Internal").ap()
    wssm_bf = nc.dram_tensor("wssm_bf", [8, 128, 16, 128], BF16, kind="Internal").ap()
    wrw_bf = nc.dram_tensor("wrw_bf", [8, 128, 8, 128], BF16, kind="Internal").ap()
    wout_bf = nc.dram_tensor("wout_bf", [8, 128, 8, 128], BF16, kind="Internal").ap()
    dbg_d = None
    if dbg is not None:
        dbg_d = nc.dram_tensor("dbg", [nblk, 128, dbg], F32, kind="ExternalOutput").ap()

    P = Prog(nc)
    sb = P.sb

    def act(out, in_, func, r, w, **kw):
        P.add('act', lambda e: e.activation(out=out, in_=in_, func=func, **kw), r, w)

    def tt(eng, out, in0, in1, op, r, w):
        P.add(eng, lambda e: e.tensor_tensor(out=out, in0=in0, in1=in1, op=op), r, w)

    def ts(eng, out, in0, s1, s2, op0, op1, r, w):
        if op1 is None:
            P.add(eng, lambda e: e.tensor_scalar(out=out, in0=in0, scalar1=s1, scalar2=None, op0=op0), r, w)
        else:
            P.add(eng, lambda e: e.tensor_scalar(out=out, in0=in0, scalar1=s1, scalar2=s2, op0=op0, op1=op1), r, w)

    def stt(out, in0, scalar, in1, op0, op1, r, w):
        P.add('dve', lambda e: e.scalar_tensor_tensor(out=out, in0=in0, scalar=scalar, in1=in1, op0=op0, op1=op1), r, w)

    def mm(out, lhsT, rhs, start, stop, r, w):
        P.add('pe', lambda e: e.matmul(out, lhsT=lhsT, rhs=rhs, start=start, stop=stop), r, w)

    def tr(out, in_, idn, r, w):
        P.add('pe', lambda e: e.transpose(out=out, in_=in_, identity=idn), r, w)

    def cp(eng, out, in_, r, w):
        if eng == 'act':
            P.add(eng, lambda e: e.activation(out=out, in_=in_, func=AF.Copy), r, w)
        else:
            P.add(eng, lambda e: e.tensor_copy(out=out, in_=in_), r, w)

    def recip(out, in_, r, w):
        P.add('dve', lambda e: e.reciprocal(out=out, in_=in_), r, w)

    def ld(out, in_, key, reads=()):
        return P.dma('sp', lambda e: e.dma_start(out=out, in_=in_), reads=list(reads), writes=[key], semkey=key)

    par = sb("par", [128, NPAR])
    der = sb("der", [128, NDER])
    cst = sb("cst", [128, NCST])
    w2a2 = sb("w2a2", [128, D])
    ld(par[:], par_d, 'par')
    ld(cst[:], cst_d, 'cst')
    ld(w2a2[:], w2a2_d, 'w2a2')

    def pcol(n, i=0, p0=0, p1=128):
        return par[p0:p1, PC[n] + i:PC[n] + i + 1]

    def dcol(n, i=0, p0=0, p1=128):
        return der[p0:p1, DC[n] + i:DC[n] + i + 1]

    def cm(n, w=128, p0=0, p1=128):
        return cst[p0:p1, CC[n]:CC[n] + w]

    ts('dve', der[:, DC['omm']:DC['omm'] + 33], par[:, PC['mu']:PC['mu'] + 33], -1.0, 1.0, ALU.mult, ALU.add, ['par'], ['der'])
    ts('dve', der[:, DC['omka']:DC['omka'] + 8], par[:, PC['ka']:PC['ka'] + 8], -1.0, 1.0, ALU.mult, ALU.add, ['par'], ['der'])
    act(dcol('aneg', 0, 0, 32), pcol('alog', 0, 0, 32), AF.Exp, ['par'], ['der'])
    ts('dve', dcol('aneg', 0, 0, 32), dcol('aneg', 0, 0, 32), -1.0, None, ALU.mult, None, ['der'], ['der'])

    PB = [P.ps("pb%d" % b, [128, 512]) for b in range(8)]

    def bk(b, qs=None, halves=(0, 1)):
        return ['B%dh%d' % (b, h) for h in halves]

    def pq(b, q, p0=0, p1=128, w=128):
        return PB[b][p0:p1, q * 128:q * 128 + w]

    hist_c = sb("hist_c", [128, 24, 3])
    hist_r = sb("hist_r", [128, 33, 1])
    Sst = sb("Sst", [128, 2048])
    SR = sb("SR", [128, 8, 64])
    P.add('pool', lambda e: e.memset(hist_c[:], 0.0), [], ['hc%d' % c for c in range(24)])
    P.add('pool', lambda e: e.memset(hist_r[:], 0.0), [], ['hr%d' % c for c in range(33)])
    P.add('pool', lambda e: e.memset(Sst[:], 0.0), [], ['S%d' % g for g in range(4)])
    P.add('pool', lambda e: e.memset(SR[:], 0.0), [], ['SR%d_%d' % (c, h) for c in range(8) for h in range(2)])

    arena = sb("arena", [128, 20480])
    _ao = [0]

    def carve(n, pat=None, **kw):
        v = arena[:, _ao[0]:_ao[0] + n]
        _ao[0] += n
        assert _ao[0] <= 20480
        return v.rearrange(pat, **kw) if pat else v

    xt = [sb("xt%d" % i, [128, D]) for i in range(2)]
    xn = sb("xn", [128, D])
    ssq = sb("ssq", [128, 1])
    rstd = sb("rstd", [128, 1])
    hT = sb("hT", [128, 8, TB], BF16)
    NW = 6
    wt = [sb("wt%d" % i, [128, 8, 128], BF16) for i in range(NW)]
    _ao[0] = 0
    xbc = carve(3072, "p (c t) -> p c t", c=24)
    zs = carve(2048, "p (c t) -> p c t", c=16)
    raw = [sb("raw%d" % i, [128, 3 + TB]) for i in range(8)]
    acc = [sb("acc%d" % i, [128, TB]) for i in range(4)]
    dtT = sb("dtT", [32, TB])
    aT = sb("aT", [32, TB])
    dta = sb("dta", [128, 64])
    Xdt = carve(2048)
    Xds = carve(2048)
    rhsa_f = carve(4096)
    rhsa = rhsa_f.bitcast(BF16)[:, 0:4096].rearrange("p (h l) -> p h l", h=32)
    dstok = sb("dstok", [128, 32])
    dec = [carve(512, "p (h l) -> p h l", h=4) for i in range(2)]
    mix = [carve(512, "p (h l) -> p h l", h=4) for i in range(2)]
    Eg = [carve(512, "p (h l) -> p h l", h=4) for i in range(2)]
    Cdec = [carve(512, "p (h l) -> p h l", h=4) for i in range(2)]
    t1 = [sb("t1_%d" % i, [128, TB]) for i in range(2)]
    yz = carve(2048, "p (c t) -> p c t", c=16)
    Btok = carve(512, "p (g n) -> p g n", g=4)
    scm = carve(512, "p (g n) -> p g n", g=4)
    sqb = [sb("sqb%d" % i, [128, TB], BF16) for i in range(2)]
    rsb = sb("rsb", [128, TB])
    yn = sb("yn", [128, 16, TB], BF16)
    rawr = [sb("rawr%d" % i, [128, 1 + TB]) for i in range(8)]
    tmpr = [sb("tmpr%d" % i, [128, TB]) for i in range(4)]
    _ao[0] = 0
    sh = carve(33 * 128, "p (c t) -> p c t", c=33)
    lw = carve(1024, "p (c t) -> p c t", c=8)
    av = carve(1024, "p (c t) -> p c t", c=8)
    kkb = carve(1024, "p (c t) -> p c t", c=8)
    kp = carve(1024, "p (c t) -> p c t", c=8)
    bon = carve(1024, "p (c t) -> p c t", c=8)
    EP = carve(1024, "p (c t) -> p c t", c=8)
    NTMP = 10
    tmpb = [[sb("tm%d_%d" % (k, i), [128, TB]) for i in range(2)] for k in range(NTMP)]
    RA = carve(2048, "p (c j t) -> p c j t", c=8, j=2)
    BKt = carve(3072, "p (c j t) -> p c j t", c=8, j=2)
    BKh = carve(3072, "p (c j t) -> p c j t", c=8, j=2)
    VV = sb("VV", [128, 8, 2, 192])
    NAM = 4
    BKtok = [sb("BKtok%d" % i, [128, 64]) for i in range(NAM)]
    UV = [sb("UV%d" % i, [128, 64]) for i in range(NAM)]
    AM = [sb("AM%d" % i, [128, 128]) for i in range(NAM)]
    Pm = [[sb("Pm%d_%d" % (i, k), [128, 64]) for k in range(2)] for i in range(NAM)]
    Ptm = [[sb("Ptm%d_%d" % (i, k), [128, 64]) for k in range(2)] for i in range(NAM)]
    Rt = [[sb("Rt%d_%d" % (i, k), [128, 64]) for k in range(2)] for i in range(NAM)]
    X1s = [sb("X1s%d" % i, [128, 64]) for i in range(NAM)]
    ybuf = carve(1024, "p (c t) -> p c t", c=8)
    yr = sb("yr", [128, 8, TB], BF16)
    _ao[0] = 13312
    wob = sb("wob", [128, 4, 8, 128], BF16)
    mT = sb("mT", [128, 8, TB], BF16)
    oT = carve(1024, "p (c t) -> p c t", c=8)
    on = carve(1024, "p (c t) -> p c t", c=8)
    G = carve(2048, "p (c t) -> p c t", c=16)
    wrwS = sb("wrwS", [128, 8, 8, 128], BF16)
    bar = sb("bar", [128, 1])
    K_SSD = (['xbc%d' % c for c in range(24)] + ['zs%d' % c for c in range(16)] + ['Xdt%d' % c for c in range(4)]
             + ['Xds%d' % c for c in range(4)] + ['rhsa', 'Btok', 'scm'] + ['yz%d' % c for c in range(16)]
             + ['%s%d' % (n, i) for n in ('dec', 'mix', 'Eg', 'Cdec') for i in range(2)])
    K_RW = (['sh%d' % c for c in range(33)] + ['%s%d' % (n, c) for n in ('lw', 'av', 'kkb', 'kp', 'bon', 'EP', 'RA', 'BKt', 'BKh', 'ybuf')
                                                for c in range(8)])
    K_OUT = (['%s%d' % (n, c) for n in ('oT', 'on') for c in range(8)] + ['G%d' % c for c in range(16)])

    K_SSD_E = ['xbc%d' % c for c in range(24)] + ['zs%d' % c for c in range(16)] + ['rhsa']
    K_SSD_R = [k for k in K_SSD if k not in K_SSD_E]

    def barrier(old, new):
        P.add('pool', lambda e: e.memset(bar[:], 0.0), [], ['bar'] + old + new)

    P.add('pool', lambda e: e.memset(VV[:], 0.0), [], ['VV%d' % c for c in range(8)])

    ident = cm('ident')
    cstb = sb("cstb", [128, 384], BF16)
    cp('dve', cstb[:, 0:128], cm('strict'), ['cst'], ['cstb'])
    cp('dve', cstb[:, 128:256], cm('ones'), ['cst'], ['cstb'])
    cp('dve', cstb[:, 256:384], cm('blk'), ['cst'], ['cstb'])
    cnt = {'raw': 0, 'acc': 0, 'rawr': 0, 'wo': 0, 'am': 0, 'tmpr': 0}
    final_ops = []

    def roundrobin(gens):
        gens = list(gens)
        while gens:
            nxt = []
            for g in gens:
                try:
                    next(g)
                    nxt.append(g)
                except StopIteration:
                    pass
            gens = nxt

    def consumer_z(c):
        def f(pp, pk):
            act(zs[:, c, :], pp, AF.Silu, pk, ['zs%d' % c])
            yield
        return f

    def consumer_xbc(c):
        def f(pp, pk):
            ri = cnt['raw'] % 8
            cnt['raw'] += 1
            r_ = raw[ri]
            rk_ = 'raw%d' % ri
            act(r_[:, 3:3 + TB], pp, AF.Copy, pk, [rk_])
            cp('pool', r_[:, 0:3], hist_c[:, c, :], ['hc%d' % c], [rk_])
            yield
            ai = cnt['acc'] % 4
            cnt['acc'] += 1
            a_ = acc[ai]
            ak_ = 'acc%d' % ai
            wc = PC['cw'] + c * 4
            ts('dve', a_[:], r_[:, 0:TB], par[:, wc:wc + 1], pcol('cb', c), ALU.mult, ALU.add, [rk_, 'par'], [ak_])
            yield
            for k in range(1, 4):
                stt(a_[:], r_[:, k:k + TB], par[:, wc + k:wc + k + 1], a_[:], ALU.mult, ALU.add, [rk_, ak_, 'par'], [ak_])
                yield
            cp('pool', hist_c[:, c, :], r_[:, TB:TB + 3], [rk_], ['hc%d' % c])
            act(xbc[:, c, :], a_[:], AF.Silu, [ak_], ['xbc%d' % c])
            yield
        return f

    def consumer_dt(pp, pk):
        act(dtT[:], pp, AF.Exp, pk + ['par'], ['dtT'], bias=pcol('dtb', 0, 0, 32))
        act(dtT[:], dtT[:], AF.Ln, ['dtT'], ['dtT'], bias=1.0)
        ts('dve', aT[:], dtT[:], dcol('aneg', 0, 0, 32), None, ALU.mult, None, ['dtT', 'der'], ['aT'])
        id32 = cst[0:32, CC['ident']:CC['ident'] + 32]
        tr(PB[2][:, 0:32], dtT[:], id32, ['dtT', 'cst'], bk(2, [0]))
        tr(PB[2][:, 32:64], aT[:], id32, ['aT', 'cst'], bk(2, [0]))
        cp('dve', dta[:], PB[2][:, 0:64], bk(2, [0]), ['dta'])
        yield
        mm(PB[3][:, 0:32], cm('strict'), dta[:, 32:64], True, True, ['cst', 'dta'], bk(3, [0]))
        act(dstok[:], PB[3][:, 0:32], AF.Exp, bk(3, [0]), ['dstok'])
        tt('pool', rhsa[:], dta[:, 32:64].unsqueeze(2).to_broadcast([128, 32, 128]),
           cm('incl').unsqueeze(1).to_broadcast([128, 32, 128]), ALU.mult, ['dta', 'cst'], ['rhsa'])
        yield

    def consumer_rw(idx):
        def f(pp, pk):
            ri = cnt['rawr'] % 8
            cnt['rawr'] += 1
            r_ = rawr[ri]
            rk_ = 'rawr%d' % ri
            act(r_[:, 1:1 + TB], pp, AF.Copy, pk, [rk_])
            cp('pool', r_[:, 0:1], hist_r[:, idx, :], ['hr%d' % idx], [rk_])
            yield
            ti_ = cnt['tmpr'] % 4
            cnt['tmpr'] += 1
            t_ = tmpr[ti_]
            tk_ = 'tmpr%d' % ti_
            ts('dve', t_[:], r_[:, 1:1 + TB], dcol('omm', idx), None, ALU.mult, None, [rk_, 'der'], [tk_])
            yield
            stt(sh[:, idx, :], r_[:, 0:TB], pcol('mu', idx), t_[:], ALU.mult, ALU.add, [rk_, tk_, 'par'], ['sh%d' % idx])
            cp('pool', hist_r[:, idx, :], r_[:, TB:TB + 1], [rk_], ['hr%d' % idx])
            yield
        return f

    def consumer_gate(i):
        def f(pp, pk):
            act(G[:, i, :], pp, AF.Sigmoid, pk + ['par'], ['G%d' % i], bias=pcol('bg', i))
            yield
        return f

    glist = []
    glist.append((O_DT, 32, consumer_dt))
    for c in range(16):
        glist.append((O_X + c * 128, 128, consumer_xbc(c)))
    for g in range(4):
        glist.append((O_B + g * 128, 128, consumer_xbc(16 + g)))
    for g in range(4):
        glist.append((O_C + g * 128, 128, consumer_xbc(20 + g)))
    for c in range(16):
        glist.append((O_Z + c * 128, 128, consumer_z(c)))
    n_ssm_groups = len(glist)
    glist.append((O_WA, 128, consumer_rw(32)))
    for c in range(8):
        for i, o in enumerate([O_R, O_K, O_V, O_G]):
            glist.append((o + c * 128, 128, consumer_rw(i * 8 + c)))
    n_rw_groups = len(glist)
    for i in range(16):
        glist.append((O_GS + i * 128, 128, consumer_gate(i)))
    NG = len(glist)
    if stage <= 1:
        g_end = n_ssm_groups
    elif stage == 2:
        g_end = n_rw_groups
    else:
        g_end = NG
    sched = [(b, g) for b in range(nblk) for g in range(g_end)]
    pos = {bg: i for i, bg in enumerate(sched)}

    def issue_w(si):
        if si >= len(sched):
            return
        col0, M, _ = glist[sched[si][1]]
        s = si % NW
        ch = chunk_of(col0)
        if M == 128:
            ld(wt[s][:, :, :], win_bf[ch], 'wt%d' % s, reads=['wbf'])
        else:
            ld(wt[s][:, :, 0:M], win_bf[ch, :, :, 0:M], 'wt%d' % s, reads=['wbf'])

    def chunk_of(col0):
        if col0 < O_DT:
            return col0 // 128
        if col0 == O_DT:
            return 40
        return 41 + (col0 - O_R) // 128

    GL = 4
    INTERLEAVE = False

    def run_groups(blk, g0, g1):
        for _ in run_groups_gen(blk, g0, g1):
            pass

    def run_groups_gen(blk, g0, g1):
        g = g0
        pending = []
        while g < g1:
            gens = []
            for gg in range(g, min(g + GL, g1)):
                si = pos[(blk, gg)]
                issue_w(si + NW - 1)
                col0, M, cons = glist[gg]
                s = si % NW
                q = [0, 1, 4, 5][si % 4]
                pp = PB[q][0:M, 0:TB]
                pk = bk(q)
                for kc in range(8):
                    mm(pp, wt[s][:, kc, 0:M], hT[:, kc, :], kc == 0, kc == 7, ['wt%d' % s, 'hT%d' % kc], pk)
                gens.append(cons(pp, pk))
            live = []
            for gn in gens:
                try:
                    next(gn)
                    live.append(gn)
                except StopIteration:
                    pass
            roundrobin(pending)
            pending = live
            g += GL
            yield
        roundrobin(pending)
        yield

    NB = 4
    stg = [arena[:, i * 2048:(i + 1) * 2048] for i in range(NB)]
    stb = [arena[:, 8192 + i * 1024:8192 + (i + 1) * 1024].bitcast(BF16) for i in range(NB)]
    wst_keys = []
    slabs = []

    def convert(src, nrc, segs, dst):
        for rc in range(nrc):
            for (c0, w, ch0) in segs:
                slabs.append((src, rc, c0, w, ch0, dst))

    segs_in = [(0, 2048, 0), (2048, 2048, 16), (4096, 1024, 32), (O_DT, 32, 40), (O_R, 2048, 41), (O_R + 2048, 2048, 57),
               (O_R + 4096, 2048, 73), (O_R + 6144, 128, 89)]
    convert(win_d, 8, segs_in, win_bf)
    convert(wssm_d, 16, [(0, 1024, 0)], wssm_bf)
    convert(wrw_d, 8, [(0, 1024, 0)], wrw_bf)
    convert(wout_d, 8, [(0, 1024, 0)], wout_bf)

    def slab_ld(n):
        src, rc, c0, w, ch0, dst = slabs[n]
        i = n % NB
        P.dma('sp', lambda e: e.dma_start(out=stg[i][:, 0:w], in_=src[rc * 128:(rc + 1) * 128, c0:c0 + w]),
              writes=['stg%d' % i], semkey='stg%d' % i)

    def slab_cast_st(n):
        src, rc, c0, w, ch0, dst = slabs[n]
        i = n % NB
        eng = ['act', 'dve', 'pool'][n % 3]
        cp(eng, stb[i][:, 0:w], stg[i][:, 0:w], ['stg%d' % i], ['stb%d' % i])
        if w >= 128:
            nch = w // 128
            d_ap = dst[ch0:ch0 + nch, :, rc, :].rearrange("c p m -> p c m")
            s_ap = stb[i][:, 0:w].rearrange("p (c m) -> p c m", m=128)
        else:
            d_ap = dst[ch0, :, rc, 0:w]
            s_ap = stb[i][:, 0:w]
        k = 'wst%d' % n
        wst_keys.append(k)
        P.dma('sp', lambda e: e.dma_start(out=d_ap, in_=s_ap), reads=['stb%d' % i], writes=[k], semkey='wost%d' % i)

    for n in range(min(NB, len(slabs))):
        slab_ld(n)
    for n in range(len(slabs)):
        slab_cast_st(n)
        if n + NB < len(slabs):
            slab_ld(n + NB)
    P.add('pool', lambda e: e.memset(bar[:], 0.0), wst_keys, ['wbf', 'bar'])
    barrier(['stg%d' % i for i in range(NB)] + ['stb%d' % i for i in range(NB)], K_SSD)
    ld(wrwS[:], wrw_bf.rearrange("o p c m -> p o c m"), 'wrwS', reads=['wbf'])

    for si in range(NW - 1):
        issue_w(si)

    def dump(blk, off, ap, keys, width):
        i = P.dma('sp', lambda e: e.dma_start(out=dbg_d[blk, 0:ap.shape[0], off:off + width], in_=ap), reads=keys,
                  semkey='dbg%d_%d' % (blk, off))
        final_ops.append(i)

    def ssd_norm_gen(g):
        bn = [2, 3, 6, 7][g]
        sq = t1[g // 2][:].bitcast(BF16)[:, (g % 2) * TB:(g % 2 + 1) * TB]
        sqk = 't1_%d_%d' % (g // 2, g % 2)
        rs = tmpb[8 + g // 2][g % 2]
        rsk = 'tm%d_%d' % (8 + g // 2, g % 2)
        for q in range(4):
            c = g * 4 + q
            tt('pool', yz[:, c, :], yz[:, c, :], zs[:, c, :], ALU.mult, ['yz%d' % c, 'zs%d' % c], ['yz%d' % c])
            yield
        for q in range(4):
            c = g * 4 + q
            act(sq, yz[:, c, :], AF.Square, ['yz%d' % c], [sqk])
            mm(pq(bn, 0), cstb[:, 128:256], sq, q == 0, q == 3, [sqk, 'cstb'], bk(bn))
            yield
        act(rs[:], pq(bn, 0), AF.Ln, bk(bn), [rsk], scale=1.0 / 512, bias=EPS)
        yield
        act(rs[:], rs[:], AF.Exp, [rsk], [rsk], scale=-0.5)
        yield
        for q in range(4):
            c = g * 4 + q
            stt(yn[:, c, :], yz[:, c, :], pcol('sng', c), rs[:], ALU.mult, ALU.mult, ['yz%d' % c, rsk, 'par'], ['yn%d' % c])
            yield

    def ssd_core(blk):
        for c4 in range(4):
            bx = 2 if c4 % 2 == 0 else 6
            for q in range(4):
                c = c4 * 4 + q
                tr(pq(bx, q), xbc[:, c, :], ident, ['xbc%d' % c, 'cst'], bk(bx))
            tt('dve', Xdt[:, c4 * 512:(c4 + 1) * 512].rearrange("p (h d) -> p h d", h=8),
               PB[bx][:, :].rearrange("p (h d) -> p h d", h=8),
               dta[:, c4 * 8:(c4 + 1) * 8].unsqueeze(2).to_broadcast([128, 8, 64]), ALU.mult, bk(bx) + ['dta'], ['Xdt%d' % c4])
        for g in range(4):
            tr(pq(7, g), xbc[:, 16 + g, :], ident, ['xbc%d' % (16 + g), 'cst'], bk(7))
        act(Btok[:].rearrange("p g n -> p (g n)"), PB[7][:, :], AF.Copy, bk(7), ['Btok'])
        for c4 in range(4):
            tt('dve', Xds[:, c4 * 512:(c4 + 1) * 512].rearrange("p (h d) -> p h d", h=8),
               Xdt[:, c4 * 512:(c4 + 1) * 512].rearrange("p (h d) -> p h d", h=8),
               dstok[:, c4 * 8:(c4 + 1) * 8].unsqueeze(2).to_broadcast([128, 8, 64]), ALU.mult, ['Xdt%d' % c4, 'dstok'], ['Xds%d' % c4])
        for g in range(4):
            mm(pq(3, g), xbc[:, 16 + g, :], xbc[:, 20 + g, :], True, True, ['xbc%d' % (16 + g), 'xbc%d' % (20 + g)], bk(3, [g]))
        tt('dve', scm[:], PB[3][:, :].rearrange("p (g l) -> p g l", g=4), cm('incl').unsqueeze(1).to_broadcast([128, 4, 128]),
           ALU.mult, bk(3) + ['cst'], ['scm'])
        halves = [(g, hf) for g in range(4) for hf in range(2)]

        def stage1(i):
            g, hf = halves[i]
            u = i % 2
            h0 = g * 8 + hf * 4
            bs, be = (4, 5) if u == 0 else (0, 1)
            rv = rhsa[:, h0:h0 + 4, :].rearrange("p h l -> p (h l)")
            mm(PB[bs][:, :], cstb[:, 0:128], rv, True, True, ['cstb', 'rhsa'], bk(bs))
            mm(PB[be][:, :], cstb[:, 128:256], rv, True, True, ['cstb', 'rhsa'], bk(be))
            act(dec[u][:].rearrange("p h l -> p (h l)"), PB[bs][:, :], AF.Exp, bk(bs), ['dec%d' % u])
            act(Eg[u][:].rearrange("p h l -> p (h l)"), PB[be][:, :], AF.Exp, bk(be), ['Eg%d' % u])
            tt('dve', mix[u][:], dec[u][:], scm[:, g, :].unsqueeze(1).to_broadcast([128, 4, 128]), ALU.mult,
               ['dec%d' % u, 'scm'], ['mix%d' % u])
            tt('pool', Cdec[u][:], Eg[u][:], xbc[:, 20 + g, :].unsqueeze(1).to_broadcast([128, 4, 128]), ALU.mult,
               ['Eg%d' % u, 'xbc%d' % (20 + g)], ['Cdec%d' % u])

        def stage2(i):
            g, hf = halves[i]
            u = i % 2
            h0 = g * 8 + hf * 4
            by = 6 if g % 2 == 0 else 2
            for j in range(4):
                h = h0 + j
                c = h // 2
                half = h % 2
                q = c % 4
                yo = pq(by, q, half * 64, (half + 1) * 64)
                mm(yo, Xdt[:, h * 64:(h + 1) * 64], mix[u][:, j, :], True, False, ['Xdt%d' % (h // 8), 'mix%d' % u], bk(by))
                mm(yo, Sst[:, h * 64:(h + 1) * 64], Cdec[u][:, j, :], False, True, ['S%d' % g, 'Cdec%d' % u], bk(by))
            mm(PB[7][:, 0:256], Btok[:, g, :], Xds[:, h0 * 64:(h0 + 4) * 64], True, True, ['Btok', 'Xds%d' % g], bk(7))
            sv = Sst[:, h0 * 64:(h0 + 4) * 64]
            tt('dve', sv.rearrange("p (h d) -> p h d", h=4), sv.rearrange("p (h d) -> p h d", h=4),
               Eg[u][:, :, 127:128].to_broadcast([128, 4, 64]), ALU.mult, ['S%d' % g, 'Eg%d' % u], ['S%d' % g])
            tt('dve', sv, sv, PB[7][:, 0:256], ALU.add, ['S%d' % g] + bk(7), ['S%d' % g])

        def stage3(g):
            by = 6 if g % 2 == 0 else 2
            for q in range(4):
                c = g * 4 + q
                stt(yz[:, c, :], xbc[:, c, :], pcol('dsk', c), pq(by, q), ALU.mult, ALU.add, ['xbc%d' % c, 'par'] + bk(by), ['yz%d' % c])

        stage1(0)
        pend3 = None
        for i in range(8):
            if i + 1 < 8:
                stage1(i + 1)
            stage2(i)
            if pend3 is not None:
                stage3(pend3)
                pend3 = None
            if halves[i][1] == 1:
                pend3 = halves[i][0]
        stage3(pend3)

    def front_p1():
        act(sh[0:64, 32, :], sh[0:64, 32, :], AF.Tanh, ['sh32'], ['sh32'])
        for c in range(8):
            T_ = [tmpb[k][c % 2] for k in range(NTMP)]
            TK = ['tm%d_%d' % (k, c % 2) for k in range(NTMP)]
            bw, ba = (2, 3) if c % 2 == 0 else (6, 7)
            mm(pq(bw, 0), w2a2[0:64, c * 128:(c + 1) * 128], sh[0:64, 32, :], True, True, ['w2a2', 'sh32'], bk(bw))
            mm(pq(ba, 0), w2a2[64:128, c * 128:(c + 1) * 128], sh[64:128, 32, :], True, True, ['w2a2', 'sh32'], bk(ba))
            act(T_[0][:], pq(bw, 0), AF.Sigmoid, bk(bw) + ['par'], [TK[0]], bias=pcol('w0', c))
            ts('dve', lw[:, c, :], T_[0][:], LWS, None, ALU.mult, None, [TK[0]], ['lw%d' % c])
            act(av[:, c, :], pq(ba, 0), AF.Sigmoid, bk(ba) + ['par'], ['av%d' % c], bias=pcol('a0', c))

    def front_silu():
        for c in range(8):
            act(sh[:, 24 + c, :], sh[:, 24 + c, :], AF.Silu, ['sh%d' % (24 + c)], ['sh%d' % (24 + c)])

    def front_c(c):
        if True:
            T_ = [tmpb[k][c % 2] for k in range(NTMP)]
            TK = ['tm%d_%d' % (k, c % 2) for k in range(NTMP)]
            r_, k_, v_, g_ = sh[:, c, :], sh[:, 8 + c, :], sh[:, 16 + c, :], sh[:, 24 + c, :]
            rk_, kk_, vk_ = 'sh%d' % c, 'sh%d' % (8 + c), 'sh%d' % (16 + c)
            ts('dve', T_[1][:], k_, pcol('kk', c), None, ALU.mult, None, [kk_, 'par'], [TK[1]])
            yield
            T2b = T_[2][:].bitcast(BF16)[:, 0:TB]
            act(T2b, T_[1][:], AF.Square, [TK[1]], [TK[2]])
            yield
            mm(pq(6, c % 2), cstb[:, 256:384], T2b, True, True, ['cstb', TK[2]], bk(6))
            yield
            ts('dve', T_[2][:], pq(6, c % 2), 1e-24, None, ALU.max, None, bk(6), [TK[2]])
            yield
            act(T_[2][:], T_[2][:], AF.Ln, [TK[2]], [TK[2]])
            yield
            act(T_[2][:], T_[2][:], AF.Exp, [TK[2]], [TK[2]], scale=-0.5)
            yield
            tt('dve', kkb[:, c, :], T_[1][:], T_[2][:], ALU.mult, [TK[1], TK[2]], ['kkb%d' % c])
            yield
            ts('dve', T_[3][:], av[:, c, :], pcol('ka', c), dcol('omka', c), ALU.mult, ALU.add, ['av%d' % c, 'par', 'der'], [TK[3]])
            yield
            tt('dve', kp[:, c, :], k_, T_[3][:], ALU.mult, [kk_, TK[3]], ['kp%d' % c])
            yield
            stt(T_[4][:], r_, pcol('rk', c), kp[:, c, :], ALU.mult, ALU.mult, [rk_, 'par', 'kp%d' % c], [TK[4]])
            yield
            mm(pq(7, c % 2), cm('blk'), T_[4][:], True, True, ['cst', TK[4]], bk(7))
            yield
            tt('dve', bon[:, c, :], pq(7, c % 2), v_, ALU.mult, bk(7) + [vk_], ['bon%d' % c])
            yield
            P.add('dve', lambda e, o=T_[5][:], m=cm('rmask'), l=lw[:, c, :]: e.tensor_tensor_scan(out=o, data0=m, data1=l, initial=0.0,
                                                                                                 op0=ALU.mult, op1=ALU.add),
                  ['cst', 'lw%d' % c], [TK[5]])
            yield
            cum = T_[5]
            act(EP[:, c, :], cum[:], AF.Exp, [TK[5]], ['EP%d' % c])
            yield
            act(T_[6][:], cum[:], AF.Exp, [TK[5]], [TK[6]], scale=-1.0)
            yield
            tt('dve', T_[7][:], cum[:], lw[:, c, :], ALU.subtract, [TK[5], 'lw%d' % c], [TK[7]])
            yield
            act(T_[7][:], T_[7][:], AF.Exp, [TK[7]], [TK[7]])
            yield
            c3 = cum[:].rearrange("p (j t) -> p j t", j=2)
            tt('dve', T_[8][:].rearrange("p (j t) -> p j t", j=2), c3[:, :, 63:64].to_broadcast([128, 2, 64]), c3, ALU.subtract,
               [TK[5]], [TK[8]])
            yield
            act(T_[8][:], T_[8][:], AF.Exp, [TK[8]], [TK[8]])
            yield

            def v3(ap):
                return ap.rearrange("p (j t) -> p j t", j=2)
            tt('dve', RA[:, c, :, 64:128], v3(r_), v3(EP[:, c, :]), ALU.mult, [rk_, 'EP%d' % c], ['RA%d' % c])
            yield
            stt(RA[:, c, :, 0:64], v3(kkb[:, c, :]), -1.0, v3(T_[7][:]), ALU.mult, ALU.mult, ['kkb%d' % c, TK[7]], ['RA%d' % c])
            yield
            tt('dve', T_[9][:], kkb[:, c, :], av[:, c, :], ALU.mult, ['kkb%d' % c, 'av%d' % c], [TK[9]])
            yield
            tt('dve', BKt[:, c, :, 0:64], v3(T_[9][:]), v3(T_[6][:]), ALU.mult, [TK[9], TK[6]], ['BKt%d' % c])
            yield
            tt('pool', BKt[:, c, :, 128:192], v3(T_[9][:]), v3(T_[6][:]), ALU.mult, [TK[9], TK[6]], ['BKt%d' % c])
            yield
            tt('dve', BKt[:, c, :, 64:128], v3(kp[:, c, :]), v3(T_[6][:]), ALU.mult, ['kp%d' % c, TK[6]], ['BKt%d' % c])
            yield
            tt('pool', BKh[:, c, :, 0:64], v3(T_[9][:]), v3(T_[8][:]), ALU.mult, [TK[9], TK[8]], ['BKh%d' % c])
            yield
            tt('pool', BKh[:, c, :, 128:192], v3(T_[9][:]), v3(T_[8][:]), ALU.mult, [TK[9], TK[8]], ['BKh%d' % c])
            yield
            tt('pool', BKh[:, c, :, 64:128], v3(kp[:, c, :]), v3(T_[8][:]), ALU.mult, ['kp%d' % c, TK[8]], ['BKh%d' % c])
            yield
            cp('pool', VV[:, c, :, 64:128], v3(v_), [vk_], ['VV%d' % c])
            yield

    def rwkv_core(blk):
        for c2 in range(4):
            for hh in range(2):
                pr = hh * 64
                po = 64 - pr
                ph = po // 64
                off = 0 if hh == 1 else 64
                sl = slice(po, po + 64)
                idh = cst[pr:pr + 64, CC['ident'] + pr:CC['ident'] + pr + 64]
                ido = cst[po:po + 64, CC['ident'] + po:CC['ident'] + po + 64]
                RES = {}

                def inv_gen(cc, j):
                    c = 2 * c2 + cc
                    ui = cc * 2 + j
                    am = AM[ui]
                    amk = 'AM%d' % ui
                    bkt, uvh = BKtok[ui], UV[ui]
                    bktk, uvk_v = 'BKtok%d' % ui, 'UVv%d' % ui
                    cs = ui * 64

                    def reg(bn):
                        return PB[bn][po:po + 64, cs:cs + 64], bk(bn, halves=[ph])
                    tr(PB[0][:, ui * 128:ui * 128 + 64], BKh[pr:pr + 64, c, j, off:off + 128], idh, ['BKh%d' % c, 'cst'], bk(0))
                    yield
                    tr(PB[0][:, ui * 128 + 64:ui * 128 + 128], VV[pr:pr + 64, c, j, off:off + 128], idh, ['VV%d' % c, 'cst'], bk(0))
                    yield
                    mm(pq(1, ui), BKt[pr:pr + 64, c, j, off:off + 128], RA[pr:pr + 64, c, j, :], True, True, ['BKt%d' % c, 'RA%d' % c], bk(1))
                    yield
                    r0, k0 = reg(6)
                    mm(r0, RA[pr:pr + 64, c, j, 0:64], BKt[pr:pr + 64, c, j, 0:64], True, True, ['BKt%d' % c, 'RA%d' % c], k0)
                    yield
                    cp('act', bkt[:], PB[0][:, ui * 128:ui * 128 + 64], bk(0), [bktk])
                    yield
                    cp('act', uvh[pr:pr + 64, :], PB[0][pr:pr + 64, ui * 128 + 64:ui * 128 + 128], bk(0), [uvk_v])
                    yield
                    tt('dve', am[:], pq(1, ui), cm('m4'), ALU.mult, bk(1) + ['cst'], [amk])
                    yield
                    p0, pt0, rt = Pm[ui], Ptm[ui], Rt[ui]
                    pk0 = ['Pm%d_%d' % (ui, k) for k in range(2)]
                    ptk = ['Ptm%d_%d' % (ui, k) for k in range(2)]
                    rtk = ['Rt%d_%d' % (ui, k) for k in range(2)]
                    tt('dve', p0[0][sl, :], r0, cm('low', 64, po, po + 64), ALU.mult, k0 + ['cst'], [pk0[0]])
                    yield
                    def bfv(t_):
                        return t_[:].bitcast(BF16)[sl, 0:64]
                    tt('pool', bfv(rt[0]), am[sl, 0:64], ido, ALU.add, [amk, 'cst'], [rtk[0]])
                    yield
                    Pcur, Pck = p0[0][sl, :], pk0[0]
                    Ptcur, Ptck = am[sl, 0:64], amk
                    Rcur, Rck = bfv(rt[0]), rtk[0]
                    rP, kP = reg(2)
                    rPt, kPt = reg(3)
                    rR, kR = reg(4)
                    for lvl in range(1, 6):
                        nb = lvl % 2
                        mm(rP, Ptcur, Pcur, True, True, [Ptck, Pck], kP)
                        yield
                        if lvl < 5:
                            mm(rPt, Pcur, Ptcur, True, True, [Ptck, Pck], kPt)
                            yield
                        cp('act', bfv(p0[nb]), rP, kP, [pk0[nb]])
                        yield
                        if lvl < 5:
                            cp('act', bfv(pt0[nb]), rPt, kPt, [ptk[nb]])
                            yield
                        mm(rR, bfv(p0[nb]), Rcur, True, True, [pk0[nb], Rck], kR)
                        yield
                        rout = rt[nb][sl, :] if lvl == 5 else bfv(rt[nb])
                        tt('dve', rout, Rcur, rR, ALU.add, [Rck] + kR, [rtk[nb]])
                        yield
                        Pcur, Pck = bfv(p0[nb]), pk0[nb]
                        if lvl < 5:
                            Ptcur, Ptck = bfv(pt0[nb]), ptk[nb]
                        Rcur, Rck = rout, rtk[nb]
                    RES[(cc, j)] = (Rcur, Rck)

                def seq_gen(cc, j):
                    c = 2 * c2 + cc
                    ui = cc * 2 + j
                    am = AM[ui]
                    amk = 'AM%d' % ui
                    bkt, uvh, xs = BKtok[ui], UV[ui], X1s[ui]
                    bktk, uvk_v, uvk_u, xsk = 'BKtok%d' % ui, 'UVv%d' % ui, 'UVu%d' % ui, 'X1s%d' % ui
                    Rcur, Rck = RES[(cc, j)]
                    srk = 'SR%d_%d' % (c, hh)
                    srv = SR[pr:pr + 64, c, :]
                    cs = ui * 64
                    rX, kX = PB[5][po:po + 64, cs:cs + 64], bk(5, halves=[ph])
                    rU, kU = PB[6][po:po + 64, cs:cs + 64], bk(6, halves=[ph])
                    yk = bk(7, halves=[hh])
                    mm(rX, RA[pr:pr + 64, c, j, 0:64], srv, True, False, ['RA%d' % c, srk], kX)
                    mm(rX, am[pr:pr + 64, 0:64], uvh[pr:pr + 64, :], False, True, [amk, uvk_v], kX)
                    yield
                    cp('act', xs[sl, :], rX, kX, [xsk])
                    yield
                    mm(rU, Rcur, xs[sl, :], True, True, [Rck, xsk], kU)
                    yield
                    cp('dve', uvh[sl, :], rU, kU, [uvk_u])
                    yield
                    uvk = [uvk_v, uvk_u]
                    mm(PB[7][pr:pr + 64, cs:cs + 64], srv, RA[pr:pr + 64, c, j, 64:128], True, True, [srk, 'RA%d' % c], yk)
                    yield
                    mm(PB[7][pr:pr + 64, 256 + cs:256 + cs + 64], uvh[:, :], am[:, 64:128], True, True, uvk + [amk], yk)
                    yield
                    so = PB[5][pr:pr + 64, cs:cs + 64]
                    sok = bk(5, halves=[hh])
                    mm(so, bkt[:, :], uvh[:, :], True, True, [bktk] + uvk, sok)
                    yield
                    ts('pool', srv, srv, EP[pr:pr + 64, c, j * 64 + 63:j * 64 + 64], None, ALU.mult, None, [srk, 'EP%d' % c], [srk])
                    yield
                    tt('dve', srv, srv, so, ALU.add, [srk] + sok, [srk])
                    yield

                roundrobin([inv_gen(cc, j) for cc in range(2) for j in range(2)])
                for j in range(2):
                    roundrobin([seq_gen(cc, j) for cc in range(2)])
                hs = slice(pr, pr + 64)
                for cc in range(2):
                    c = 2 * c2 + cc
                    cp('act', ybuf[hs, c, :], PB[7][hs, cc * 128:cc * 128 + 128], bk(7, halves=[hh]), ['ybuf%d' % c])
                    tt('dve', ybuf[hs, c, :], ybuf[hs, c, :], PB[7][hs, 256 + cc * 128:256 + cc * 128 + 128], ALU.add,
                       ['ybuf%d' % c] + bk(7, halves=[hh]), ['ybuf%d' % c])

    def gn_gen(c):
        T0, T1 = tmpb[c][0], tmpb[c][1]
        K0, K1 = 'tm%d_0' % c, 'tm%d_1' % c
        bm, qm = 2 + c // 4, c % 4
        bv = 6 + c // 4
        mm(pq(bm, qm), cm('blk'), ybuf[:, c, :], True, True, ['cst', 'ybuf%d' % c], bk(bm))
        yield
        stt(T0[:], pq(bm, qm), -1.0 / 64, ybuf[:, c, :], ALU.mult, ALU.add, bk(bm) + ['ybuf%d' % c], [K0])
        yield
        T1b = T1[:].bitcast(BF16)[:, 0:TB]
        act(T1b, T0[:], AF.Square, [K0], [K1])
        yield
        mm(pq(bv, qm), cstb[:, 256:384], T1b, True, True, ['cstb', K1], bk(bv))
        yield
        act(T1[:], pq(bv, qm), AF.Ln, bk(bv), [K1], scale=1.0 / 64, bias=GN_EPS)
        yield
        act(T1[:], T1[:], AF.Exp, [K1], [K1], scale=-0.5)
        yield
        tt('dve', T0[:], T0[:], T1[:], ALU.mult, [K0, K1], [K0])
        yield
        ts('dve', T0[:], T0[:], pcol('gng', c), pcol('gnb', c), ALU.mult, ALU.add, [K0, 'par'], [K0])
        yield
        tt('dve', T0[:], T0[:], bon[:, c, :], ALU.add, [K0, 'bon%d' % c], [K0])
        yield
        tt('dve', yr[:, c, :], T0[:], sh[:, 24 + c, :], ALU.mult, [K0, 'sh%d' % (24 + c)], ['yr%d' % c])
        yield

    WT = []
    for o_ in range(8):
        WT.append(wssm_bf[o_, :, 0:8, :])
        WT.append(wssm_bf[o_, :, 8:16, :])
    for eo_ in range(8):
        WT.append(wout_bf[eo_])

    def wo_issue(t):
        if t >= 24:
            return
        ld(wob[:, t % 4], WT[t], 'wo%d' % (t % 4), reads=['wbf'])

    def out_prefetch():
        for t in range(3):
            wo_issue(t)

    def out_gen(blk, xs_, xk):
        for o in range(8):
            for hf in range(2):
                t = 2 * o + hf
                wo_issue(t + 3)
                for c8 in range(8):
                    c = hf * 8 + c8
                    mm(pq(2 + o % 2, 0), wob[:, t % 4, c8, :], yn[:, c, :], c == 0, c == 15, ['wo%d' % (t % 4), 'yn%d' % c], bk(2 + o % 2))
            tt('dve', tmpb[3][o % 2][:], pq(2 + o % 2, 0), G[:, o, :], ALU.mult, bk(2 + o % 2) + ['G%d' % o], ['tm3_%d' % (o % 2)])
            for c in range(8):
                mm(pq(6 + o % 2, 0), wrwS[:, o, c, :], yr[:, c, :], c == 0, c == 7, ['wrwS', 'yr%d' % c], bk(6 + o % 2))
            T_ = tmpb[0][o % 2]
            tk = 'tm0_%d' % (o % 2)
            tt('dve', T_[:], pq(6 + o % 2, 0), G[:, 8 + o, :], ALU.mult, bk(6 + o % 2) + ['G%d' % (8 + o)], [tk])
            tt('dve', mT[:, o, :], tmpb[3][o % 2][:], T_[:], ALU.add, ['tm3_%d' % (o % 2), tk], ['mT%d' % o])
            yield
        for eo in range(8):
            t = 16 + eo
            wo_issue(t + 3)
            for o in range(8):
                mm(pq(2 + eo % 2, 0), wob[:, t % 4, o, :], mT[:, o, :], o == 0, o == 7, ['wo%d' % (t % 4), 'mT%d' % o], bk(2 + eo % 2))
            cp('act', oT[:, eo, :], pq(2 + eo % 2, 0), bk(2 + eo % 2), ['oT%d' % eo])
            T_ = tmpb[1][eo % 2]
            tk = 'tm1_%d' % (eo % 2)
            Tb = T_[:].bitcast(BF16)[:, 0:TB]
            act(Tb, oT[:, eo, :], AF.Square, ['oT%d' % eo], [tk])
            mm(pq(6, 0), cstb[:, 128:256], Tb, eo == 0, eo == 7, ['cstb', tk], bk(6))
            yield
        T_ = tmpb[2][0]
        tk = 'tm2_0'
        act(T_[:], pq(6, 0), AF.Ln, bk(6), [tk], scale=1.0 / D, bias=EPS)
        act(T_[:], T_[:], AF.Exp, [tk], [tk], scale=-0.5)
        for eo in range(8):
            stt(on[:, eo, :], oT[:, eo, :], pcol('post', eo), T_[:], ALU.mult, ALU.mult, ['oT%d' % eo, 'par', tk], ['on%d' % eo])
        yield
        for hb in range(2):
            for q_ in range(4):
                eo = hb * 4 + q_
                tr(pq(7, q_), on[:, eo, :], ident, ['on%d' % eo, 'cst'], bk(7))
            fin = xn[:, hb * 512:(hb + 1) * 512]
            tt('dve', fin, PB[7][:, :], xs_[:, hb * 512:(hb + 1) * 512], ALU.add, bk(7) + [xk], ['xn'])
            i = P.dma('sp', lambda e, hb=hb, fin=fin: e.dma_start(out=out_d[blk * TB:(blk + 1) * TB, hb * 512:(hb + 1) * 512], in_=fin),
                      reads=['xn'], semkey='fin%d' % hb)
            final_ops.append(i)
            yield

    def ld_x(blk):
        ld(xt[blk % 2][:], x_d[blk * TB:(blk + 1) * TB, :], 'xt%d' % (blk % 2))

    def stageA_gen(blk, do_ld=True):
        xs_ = xt[blk % 2]
        xk = 'xt%d' % (blk % 2)
        if do_ld:
            ld_x(blk)
        act(xn[:], xs_[:], AF.Square, [xk], ['ssq', 'xn'], accum_out=ssq[:])
        act(rstd[:], ssq[:], AF.Ln, ['ssq'], ['rstd'], scale=1.0 / D, bias=EPS)
        act(rstd[:], rstd[:], AF.Exp, ['rstd'], ['rstd'], scale=-0.5)
        ts('dve', xn[:], xs_[:], rstd[:], None, ALU.mult, None, [xk, 'rstd'], ['xn'])
        yield
        for c in range(8):
            b_, q_ = c // 4, c % 4
            tr(pq(b_, q_), xn[:, c * 128:(c + 1) * 128], ident, ['xn', 'cst'], bk(b_))
        for c in range(8):
            b_, q_ = c // 4, c % 4
            act(hT[:, c, :], pq(b_, q_), AF.Copy, bk(b_) + ['par'], ['hT%d' % c], scale=pcol('pre', c))
        yield

    def front_gen(blk, do_a=True):
        if do_a:
            for _ in stageA_gen(blk):
                yield
        if blk > 0:
            barrier(K_RW, K_SSD_E)
        for _ in run_groups_gen(blk, 0, n_ssm_groups):
            yield

    for _ in front_gen(0):
        pass
    for blk in range(nblk):
        xs_ = xt[blk % 2]
        xk = 'xt%d' % (blk % 2)
        if blk > 0:
            barrier(K_OUT + K_RW, K_SSD_R)
        ssd_core(blk)
        roundrobin([ssd_norm_gen(g_) for g_ in range(4)])
        barrier(K_SSD, K_RW)
        bi = 0
        nfc = 0
        for _ in run_groups_gen(blk, n_ssm_groups, n_rw_groups):
            if bi == 1:
                front_p1()
            elif bi >= 2 and nfc < 8 and 4 * (nfc + 1) + 4 < 4 * (bi - 1):
                roundrobin([front_c(nfc), front_c(nfc + 1)])
                nfc += 2
            bi += 1
        while nfc < 8:
            roundrobin([front_c(nfc), front_c(nfc + 1)])
            nfc += 2
        front_silu()
        out_prefetch()
        if blk + 1 < nblk:
            ld_x(blk + 1)
        rwkv_core(blk)
        barrier(['BKt%d' % c_ for c_ in range(8)] + ['BKh%d' % c_ for c_ in range(8)], K_OUT)
        roundrobin([gn_gen(c_) for c_ in range(8)])
        run_groups(blk, n_rw_groups, NG)
        if INTERLEAVE:
            gens = [out_gen(blk, xs_, xk)]
            if blk + 1 < nblk:
                gens.append(front_gen(blk + 1))
            roundrobin(gens)
        else:
            gens = [out_gen(blk, xs_, xk)]
            if blk + 1 < nblk:
                gens.append(stageA_gen(blk + 1, do_ld=False))
            roundrobin(gens)
            if blk + 1 < nblk:
                roundrobin([front_gen(blk + 1, do_a=False)])

    P.emit(final_wait_ops=final_ops)
    return nc, P


def pack_params(inp):
    p = np.zeros((128, NPAR), np.float32)

    def cmaj(v):
        v = np.asarray(v, np.float32).reshape(-1, 128)
        return v.T

    def put(name, arr):
        p[:arr.shape[0], PC[name]:PC[name] + arr.shape[1]] = arr

    put('pre', cmaj(inp['pre_gain'][0]))
    put('post', cmaj(inp['post_gain'][0]))
    cw = np.asarray(inp['conv_w'][0], np.float32)
    cwp = np.zeros((128, 24, 4), np.float32)
    for k in range(4):
        cwp[:, :, k] = cmaj(cw[k])
    put('cw', cwp.reshape(128, 96))
    put('cb', cmaj(inp['conv_b'][0]))
    put('dsk', cmaj(np.repeat(np.asarray(inp['d_skip'][0], np.float32), 64)))
    put('sng', cmaj(inp['ssm_norm_gain'][0]))
    put('mu', cmaj(inp['rwkv_mu'][0]))
    put('w0', cmaj(inp['decay_w0'][0]))
    put('a0', cmaj(inp['iclr_a0'][0]))
    put('kk', cmaj(inp['k_k'][0]))
    put('ka', cmaj(inp['k_a'][0]))
    put('rk', cmaj(np.asarray(inp['r_k'][0], np.float32).reshape(-1)))
    put('gng', cmaj(inp['gn_gain'][0]))
    put('gnb', cmaj(inp['gn_bias'][0]))
    put('bg', cmaj(inp['b_gate'][0]))
    p[:32, PC['dtb']] = np.asarray(inp['dt_bias'][0], np.float32)
    p[:32, PC['alog']] = np.asarray(inp['a_log'][0], np.float32)
    return p


_CACHE = {}


def make_in_maps(inp, nblk=NBLK, ncores=4):
    par = pack_params(inp)
    cst = make_consts()
    w2a2 = np.concatenate([np.asarray(inp['decay_w2'][0], np.float32), np.asarray(inp['iclr_a2'][0], np.float32)], axis=0)
    shared = {
        "w_in": np.ascontiguousarray(np.asarray(inp['w_in'][0], np.float32)),
        "par": par, "cst": cst, "w2a2": np.ascontiguousarray(w2a2),
        "w_ssm": np.ascontiguousarray(np.asarray(inp['w_branch_ssm'][0], np.float32)),
        "w_rwkv": np.ascontiguousarray(np.asarray(inp['w_branch_rwkv'][0], np.float32)),
        "w_out": np.ascontiguousarray(np.asarray(inp['w_out'][0], np.float32)),
    }
    x = np.asarray(inp['x'], np.float32)
    maps = []
    for b in range(ncores):
        m = dict(shared)
        m["x"] = np.ascontiguousarray(x[b, :nblk * TB])
        maps.append(m)
    return maps


def kernel(**inputs):
    if 'nc' not in _CACHE:
        _CACHE['nc'] = build()[0]
    nc = _CACHE['nc']
    maps = make_in_maps(inputs)
    res = run_bass_kernel_spmd(nc, maps, core_ids=list(range(4)))
    out = np.stack([np.asarray(r["out"], np.float32) for r in res.results], axis=0)
    return out
```

```python
import contextlib
import numpy as np
import concourse.bass as bass
import concourse.mybir as mybir
from concourse.alu_op_type import AluOpType as ALU
from concourse.bass_utils import run_bass_kernel_spmd

F32 = mybir.dt.float32
BF16 = mybir.dt.bfloat16
AF = mybir.ActivationFunctionType

CH = 30000


class Prog:
    def __init__(self, nc):
        self.nc = nc
        self.ops = []
        self.last_w = {}
        self.readers = {}
        self.stack = contextlib.ExitStack()

    def sb(self, name, shape, dtype=F32):
        return self.stack.enter_context(self.nc.sbuf_tensor("s_" + name, list(shape), dtype))

    def ps(self, name, shape, dtype=F32):
        return self.stack.enter_context(self.nc.psum_tensor("p_" + name, list(shape), dtype))

    def add(self, eng, fn, reads=(), writes=(), dma=False, semkey=None):
        i = len(self.ops)
        deps = set()
        for k in reads:
            if k in self.last_w:
                deps.add(self.last_w[k])
        for k in writes:
            if k in self.last_w:
                deps.add(self.last_w[k])
            for r in self.readers.get(k, ()):
                deps.add(r)
        deps.discard(i)
        self.ops.append(dict(eng=eng, fn=fn, deps=deps, dma=dma, semkey=semkey))
        for k in reads:
            self.readers.setdefault(k, []).append(i)
        for k in writes:
            self.last_w[k] = i
            self.readers[k] = []
        return i

    def dma(self, eng, fn, reads=(), writes=(), semkey=None):
        return self.add(eng, fn, reads, writes, dma=True, semkey=semkey)

    def emit(self, final_wait_ops=()):
        nc = self.nc
        ops = self.ops
        n = len(ops)
        waited_on = [False] * n
        for i, o in enumerate(ops):
            nd = set()
            for d in o['deps']:
                od = ops[d]
                if (not od['dma']) and (not o['dma']) and od['eng'] == 'pe' and o['eng'] == 'pe':
                    continue
                nd.add(d)
            o['deps'] = nd
            for d in nd:
                waited_on[d] = True
        for i in final_wait_ops:
            waited_on[i] = True
        cnt = {}
        for i, o in enumerate(ops):
            if o['dma']:
                sid = ('dma', o['semkey'])
                cnt[sid] = cnt.get(sid, 0) + 16
                o['ev'] = (sid, cnt[sid])
            elif waited_on[i]:
                e = o['eng']
                m = cnt.get(('m', e), 0)
                cnt[('m', e)] = m + 1
                o['ev'] = (('eng', e, m // CH), (m % CH) + 1)
            else:
                o['ev'] = None
        semids = []
        seen = set()
        for o in ops:
            if o['ev'] is not None and o['ev'][0] not in seen:
                seen.add(o['ev'][0])
                semids.append(o['ev'][0])
        sems = {}
        for k, sid in enumerate(semids):
            sems[sid] = self.stack.enter_context(nc.semaphore("s%d" % k))
        self.n_sems = len(semids)
        per_eng = {e: [] for e in ['pe', 'act', 'dve', 'pool', 'sp']}
        for i, o in enumerate(ops):
            per_eng[o['eng']].append(i)

        def run_engine(ename, eobj):
            waited = {}
            for i in per_eng[ename]:
                o = ops[i]
                need = {}
                for d in o['deps']:
                    sid, v = ops[d]['ev']
                    if sid[0] == 'eng':
                        key = ('eng', sid[1])
                        gv = sid[2] * CH + v
                    else:
                        key = sid
                        gv = v
                    if waited.get(key, 0) >= gv:
                        continue
                    if need.get(key, (None, 0))[1] < gv:
                        need[key] = (sid, gv)
                for key, (sid, gv) in need.items():
                    if sid[0] == 'eng':
                        eobj.wait_ge(sems[sid], gv - sid[2] * CH)
                    else:
                        eobj.wait_ge(sems[sid], gv)
                    waited[key] = gv
                ins = o['fn'](eobj)
                if o['ev'] is not None:
                    ins.then_inc(sems[o['ev'][0]], 16 if o['dma'] else 1)
            if ename == 'sp':
                for i in final_wait_ops:
                    sid, v = ops[i]['ev']
                    eobj.wait_ge(sems[sid], v)

        with nc.Block() as block:
            @block.tensor
            def _(e):
                run_engine('pe', e)

            @block.scalar
            def _(e):
                run_engine('act', e)

            @block.vector
            def _(e):
                run_engine('dve', e)

            @block.gpsimd
            def _(e):
                run_engine('pool', e)

            @block.sync
            def _(e):
                run_engine('sp', e)
        self.stack.close()


D = 1024
T = 4096
TB = 128
NBLK = T // TB
INC = 11424
EPS = 1e-6
GN_EPS = 64 * 1e-5
LWS = -float(np.exp(-0.5))

O_Z, O_X, O_B, O_C, O_DT = 0, 2048, 4096, 4608, 5120
O_R, O_K, O_V, O_G, O_WA = 5152, 6176, 7200, 8224, 9248
O_GS, O_GR = 9376, 10400

PC = {}
_o = 0
for _n, _w in [('pre', 8), ('post', 8), ('cw', 96), ('cb', 24), ('dsk', 16), ('sng', 16), ('mu', 33),
               ('w0', 8), ('a0', 8), ('kk', 8), ('ka', 8), ('rk', 8), ('gng', 8), ('gnb', 8), ('bg', 16),
               ('dtb', 1), ('alog', 1)]:
    PC[_n] = _o
    _o += _w
NPAR = _o
DC = {}
_o = 0
for _n, _w in [('omm', 33), ('omka', 8), ('aneg', 1)]:
    DC[_n] = _o
    _o += _w
NDER = _o

CC = {}
_o = 0
for _n, _w in [('ident', 128), ('incl', 128), ('strict', 128), ('ones', 128), ('blk', 128), ('m4', 128),
               ('low', 64), ('rmask', 128)]:
    CC[_n] = _o
    _o += _w
NCST = _o


def make_consts():
    c = np.zeros((128, NCST), np.float32)
    i = np.arange(128)
    c[:, CC['ident']:CC['ident'] + 128] = np.eye(128)
    c[:, CC['incl']:CC['incl'] + 128] = (i[:, None] <= i[None, :])
    c[:, CC['strict']:CC['strict'] + 128] = (i[:, None] > i[None, :])
    c[:, CC['ones']:CC['ones'] + 128] = 1.0
    c[:, CC['blk']:CC['blk'] + 128] = ((i[:, None] // 64) == (i[None, :] // 64))
    s = i[:, None] % 64
    t = i[None, :] % 64
    u = i[None, :] // 64
    c[:, CC['m4']:CC['m4'] + 128] = np.where(u == 0, s < t, s <= t)
    j = np.arange(64)
    c[:64, CC['low']:CC['low'] + 64] = (j[None, :] < j[:, None])
    c[64:, CC['low']:CC['low'] + 64] = (j[None, :] < j[:, None])
    c[:, CC['rmask']:CC['rmask'] + 128] = (np.arange(128)[None, :] % 64 != 0)
    return c


def build(nblk=NBLK, dbg=None, stage=9):
    nc = bass.Bass("TRN2", target_bir_lowering=False)
    Tn = nblk * TB
    x_d = nc.dram_tensor("x", [Tn, D], F32, kind="ExternalInput").ap()
    win_d = nc.dram_tensor("w_in", [D, INC], F32, kind="ExternalInput").ap()
    par_d = nc.dram_tensor("par", [128, NPAR], F32, kind="ExternalInput").ap()
    cst_d = nc.dram_tensor("cst", [128, NCST], F32, kind="ExternalInput").ap()
    w2a2_d = nc.dram_tensor("w2a2", [128, D], F32, kind="ExternalInput").ap()
    wssm_d = nc.dram_tensor("w_ssm", [2048, D], F32, kind="ExternalInput").ap()
    wrw_d = nc.dram_tensor("w_rwkv", [D, D], F32, kind="ExternalInput").ap()
    wout_d = nc.dram_tensor("w_out", [D, D], F32, kind="ExternalInput").ap()
    out_d = nc.dram_tensor("out", [Tn, D], F32, kind="ExternalOutput").ap()
    NCHK = 90
    win_bf = nc.dram_tensor("win_bf", [NCHK, 128, 8, 128], BF16, kind="Internal").ap()
    wssm_bf = nc.dram_tensor("wssm_bf", [8, 128, 16, 128], BF16, kind="Internal").ap()
    wrw_bf = nc.dram_tensor("wrw_bf", [8, 128, 8, 128], BF16, kind="Internal").ap()
    wout_bf = nc.dram_tensor("wout_bf", [8, 128, 8, 128], BF16, kind="Internal").ap()
    dbg_d = None
    if dbg is not None:
        dbg_d = nc.dram_tensor("dbg", [nblk, 128, dbg], F32, kind="ExternalOutput").ap()

    P = Prog(nc)
    sb = P.sb

    def act(out, in_, func, r, w, **kw):
        P.add('act', lambda e: e.activation(out=out, in_=in_, func=func, **kw), r, w)

    def tt(eng, out, in0, in1, op, r, w):
        P.add(eng, lambda e: e.tensor_tensor(out=out, in0=in0, in1=in1, op=op), r, w)

    def ts(eng, out, in0, s1, s2, op0, op1, r, w):
        if op1 is None:
            P.add(eng, lambda e: e.tensor_scalar(out=out, in0=in0, scalar1=s1, scalar2=None, op0=op0), r, w)
        else:
            P.add(eng, lambda e: e.tensor_scalar(out=out, in0=in0, scalar1=s1, scalar2=s2, op0=op0, op1=op1), r, w)

    def stt(out, in0, scalar, in1, op0, op1, r, w):
        P.add('dve', lambda e: e.scalar_tensor_tensor(out=out, in0=in0, scalar=scalar, in1=in1, op0=op0, op1=op1), r, w)

    def mm(out, lhsT, rhs, start, stop, r, w):
        P.add('pe', lambda e: e.matmul(out, lhsT=lhsT, rhs=rhs, start=start, stop=stop), r, w)

    def tr(out, in_, idn, r, w):
        P.add('pe', lambda e: e.transpose(out=out, in_=in_, identity=idn), r, w)

    def cp(eng, out, in_, r, w):
        if eng == 'act':
            P.add(eng, lambda e: e.activation(out=out, in_=in_, func=AF.Copy), r, w)
        else:
            P.add(eng, lambda e: e.tensor_copy(out=out, in_=in_), r, w)

    def recip(out, in_, r, w):
        P.add('dve', lambda e: e.reciprocal(out=out, in_=in_), r, w)

    def ld(out, in_, key, reads=()):
        return P.dma('sp', lambda e: e.dma_start(out=out, in_=in_), reads=list(reads), writes=[key], semkey=key)

    par = sb("par", [128, NPAR])
    der = sb("der", [128, NDER])
    cst = sb("cst", [128, NCST])
    w2a2 = sb("w2a2", [128, D])
    ld(par[:], par_d, 'par')
    ld(cst[:], cst_d, 'cst')
    ld(w2a2[:], w2a2_d, 'w2a2')

    def pcol(n, i=0, p0=0, p1=128):
        return par[p0:p1, PC[n] + i:PC[n] + i + 1]

    def dcol(n, i=0, p0=0, p1=128):
        return der[p0:p1, DC[n] + i:DC[n] + i + 1]

    def cm(n, w=128, p0=0, p1=128):
        return cst[p0:p1, CC[n]:CC[n] + w]

    ts('dve', der[:, DC['omm']:DC['omm'] + 33], par[:, PC['mu']:PC['mu'] + 33], -1.0, 1.0, ALU.mult, ALU.add, ['par'], ['der'])
    ts('dve', der[:, DC['omka']:DC['omka'] + 8], par[:, PC['ka']:PC['ka'] + 8], -1.0, 1.0, ALU.mult, ALU.add, ['par'], ['der'])
    act(dcol('aneg', 0, 0, 32), pcol('alog', 0, 0, 32), AF.Exp, ['par'], ['der'])
    ts('dve', dcol('aneg', 0, 0, 32), dcol('aneg', 0, 0, 32), -1.0, None, ALU.mult, None, ['der'], ['der'])

    PB = [P.ps("pb%d" % b, [128, 512]) for b in range(8)]

    def bk(b, qs=None, halves=(0, 1)):
        return ['B%dh%d' % (b, h) for h in halves]

    def pq(b, q, p0=0, p1=128, w=128):
        return PB[b][p0:p1, q * 128:q * 128 + w]

    hist_c = sb("hist_c", [128, 24, 3])
    hist_r = sb("hist_r", [128, 33, 1])
    Sst = sb("Sst", [128, 2048])
    SR = sb("SR", [128, 8, 64])
    P.add('pool', lambda e: e.memset(hist_c[:], 0.0), [], ['hc%d' % c for c in range(24)])
    P.add('pool', lambda e: e.memset(hist_r[:], 0.0), [], ['hr%d' % c for c in range(33)])
    P.add('pool', lambda e: e.memset(Sst[:], 0.0), [], ['S%d' % g for g in range(4)])
    P.add('pool', lambda e: e.memset(SR[:], 0.0), [], ['SR%d_%d' % (c, h) for c in range(8) for h in range(2)])

    arena = sb("arena", [128, 20480])
    _ao = [0]

    def carve(n, pat=None, **kw):
        v = arena[:, _ao[0]:_ao[0] + n]
        _ao[0] += n
        assert _ao[0] <= 20480
        return v.rearrange(pat, **kw) if pat else v

    xt = [sb("xt%d" % i, [128, D]) for i in range(2)]
    xn = sb("xn", [128, D])
    ssq = sb("ssq", [128, 1])
    rstd = sb("rstd", [128, 1])
    hT = sb("hT", [128, 8, TB], BF16)
    NW = 6
    wt = [sb("wt%d" % i, [128, 8, 128], BF16) for i in range(NW)]
    _ao[0] = 0
    xbc = carve(3072, "p (c t) -> p c t", c=24)
    zs = carve(2048, "p (c t) -> p c t", c=16)
    raw = [sb("raw%d" % i, [128, 3 + TB]) for i in range(8)]
    acc = [sb("acc%d" % i, [128, TB]) for i in range(4)]
    dtT = sb("dtT", [32, TB])
    aT = sb("aT", [32, TB])
    dta = sb("dta", [128, 64])
    Xdt = carve(2048)
    Xds = carve(2048)
    rhsa_f = carve(4096)
    rhsa = rhsa_f.bitcast(BF16)[:, 0:4096].rearrange("p (h l) -> p h l", h=32)
    dstok = sb("dstok", [128, 32])
    dec = [carve(512, "p (h l) -> p h l", h=4) for i in range(2)]
    mix = [carve(512, "p (h l) -> p h l", h=4) for i in range(2)]
    Eg = [carve(512, "p (h l) -> p h l", h=4) for i in range(2)]
    Cdec = [carve(512, "p (h l) -> p h l", h=4) for i in range(2)]
    t1 = [sb("t1_%d" % i, [128, TB]) for i in range(2)]
    yz = carve(2048, "p (c t) -> p c t", c=16)
    Btok = carve(512, "p (g n) -> p g n", g=4)
    scm = carve(512, "p (g n) -> p g n", g=4)
    sqb = [sb("sqb%d" % i, [128, TB], BF16) for i in range(2)]
    rsb = sb("rsb", [128, TB])
    yn = sb("yn", [128, 16, TB], BF16)
    rawr = [sb("rawr%d" % i, [128, 1 + TB]) for i in range(8)]
    tmpr = [sb("tmpr%d" % i, [128, TB]) for i in range(4)]
    _ao[0] = 0
    sh = carve(33 * 128, "p (c t) -> p c t", c=33)
    lw = carve(1024, "p (c t) -> p c t", c=8)
    av = carve(1024, "p (c t) -> p c t", c=8)
    kkb = carve(1024, "p (c t) -> p c t", c=8)
    kp = carve(1024, "p (c t) -> p c t", c=8)
    bon = carve(1024, "p (c t) -> p c t", c=8)
    EP = carve(1024, "p (c t) -> p c t", c=8)
    NTMP = 10
    tmpb = [[sb("tm%d_%d" % (k, i), [128, TB]) for i in range(2)] for k in range(NTMP)]
    RA = carve(2048, "p (c j t) -> p c j t", c=8, j=2)
    BKt = carve(3072, "p (c j t) -> p c j t", c=8, j=2)
    BKh = carve(3072, "p (c j t) -> p c j t", c=8, j=2)
    VV = sb("VV", [128, 8, 2, 192])
    NAM = 4
    BKtok = [sb("BKtok%d" % i, [128, 64]) for i in range(NAM)]
    UV = [sb("UV%d" % i, [128, 64]) for i in range(NAM)]
    AM = [sb("AM%d" % i, [128, 128]) for i in range(NAM)]
    Pm = [[sb("Pm%d_%d" % (i, k), [128, 64]) for k in range(2)] for i in range(NAM)]
    Ptm = [[sb("Ptm%d_%d" % (i, k), [128, 64]) for k in range(2)] for i in range(NAM)]
    Rt = [[sb("Rt%d_%d" % (i, k), [128, 64]) for k in range(2)] for i in range(NAM)]
    X1s = [sb("X1s%d" % i, [128, 64]) for i in range(NAM)]
    ybuf = carve(1024, "p (c t) -> p c t", c=8)
    yr = sb("yr", [128, 8, TB], BF16)
    _ao[0] = 13312
    wob = sb("wob", [128, 4, 8, 128], BF16)
    mT = sb("mT", [128, 8, TB], BF16)
    oT = carve(1024, "p (c t) -> p c t", c=8)
    on = carve(1024, "p (c t) -> p c t", c=8)
    G = carve(2048, "p (c t) -> p c t", c=16)
    wrwS = sb("wrwS", [128, 8, 8, 128], BF16)
    bar = sb("bar", [128, 1])
    K_SSD = (['xbc%d' % c for c in range(24)] + ['zs%d' % c for c in range(16)] + ['Xdt%d' % c for c in range(4)]
             + ['Xds%d' % c for c in range(4)] + ['rhsa', 'Btok', 'scm'] + ['yz%d' % c for c in range(16)]
             + ['%s%d' % (n, i) for n in ('dec', 'mix', 'Eg', 'Cdec') for i in range(2)])
    K_RW = (['sh%d' % c for c in range(33)] + ['%s%d' % (n, c) for n in ('lw', 'av', 'kkb', 'kp', 'bon', 'EP', 'RA', 'BKt', 'BKh', 'ybuf')
                                                for c in range(8)])
    K_OUT = (['%s%d' % (n, c) for n in ('oT', 'on') for c in range(8)] + ['G%d' % c for c in range(16)])

    K_SSD_E = ['xbc%d' % c for c in range(24)] + ['zs%d' % c for c in range(16)] + ['rhsa']
    K_SSD_R = [k for k in K_SSD if k not in K_SSD_E]

    def barrier(old, new):
        P.add('pool', lambda e: e.memset(bar[:], 0.0), [], ['bar'] + old + new)

    P.add('pool', lambda e: e.memset(VV[:], 0.0), [], ['VV%d' % c for c in range(8)])

    ident = cm('ident')
    cstb = sb("cstb", [128, 384], BF16)
    cp('dve', cstb[:, 0:128], cm('strict'), ['cst'], ['cstb'])
    cp('dve', cstb[:, 128:256], cm('ones'), ['cst'], ['cstb'])
    cp('dve', cstb[:, 256:384], cm('blk'), ['cst'], ['cstb'])
    cnt = {'raw': 0, 'acc': 0, 'rawr': 0, 'wo': 0, 'am': 0, 'tmpr': 0}
    final_ops = []

    def roundrobin(gens):
        gens = list(gens)
        while gens:
            nxt = []
            for g in gens:
                try:
                    next(g)
                    nxt.append(g)
                except StopIteration:
                    pass
            gens = nxt

    def consumer_z(c):
        def f(pp, pk):
            act(zs[:, c, :], pp, AF.Silu, pk, ['zs%d' % c])
            yield
        return f

    def consumer_xbc(c):
        def f(pp, pk):
            ri = cnt['raw'] % 8
            cnt['raw'] += 1
            r_ = raw[ri]
            rk_ = 'raw%d' % ri
            act(r_[:, 3:3 + TB], pp, AF.Copy, pk, [rk_])
            cp('pool', r_[:, 0:3], hist_c[:, c, :], ['hc%d' % c], [rk_])
            yield
            ai = cnt['acc'] % 4
            cnt['acc'] += 1
            a_ = acc[ai]
            ak_ = 'acc%d' % ai
            wc = PC['cw'] + c * 4
            ts('dve', a_[:], r_[:, 0:TB], par[:, wc:wc + 1], pcol('cb', c), ALU.mult, ALU.add, [rk_, 'par'], [ak_])
            yield
            for k in range(1, 4):
                stt(a_[:], r_[:, k:k + TB], par[:, wc + k:wc + k + 1], a_[:], ALU.mult, ALU.add, [rk_, ak_, 'par'], [ak_])
                yield
            cp('pool', hist_c[:, c, :], r_[:, TB:TB + 3], [rk_], ['hc%d' % c])
            act(xbc[:, c, :], a_[:], AF.Silu, [ak_], ['xbc%d' % c])
            yield
        return f

    def consumer_dt(pp, pk):
        act(dtT[:], pp, AF.Exp, pk + ['par'], ['dtT'], bias=pcol('dtb', 0, 0, 32))
        act(dtT[:], dtT[:], AF.Ln, ['dtT'], ['dtT'], bias=1.0)
        ts('dve', aT[:], dtT[:], dcol('aneg', 0, 0, 32), None, ALU.mult, None, ['dtT', 'der'], ['aT'])
        id32 = cst[0:32, CC['ident']:CC['ident'] + 32]
        tr(PB[2][:, 0:32], dtT[:], id32, ['dtT', 'cst'], bk(2, [0]))
        tr(PB[2][:, 32:64], aT[:], id32, ['aT', 'cst'], bk(2, [0]))
        cp('dve', dta[:], PB[2][:, 0:64], bk(2, [0]), ['dta'])
        yield
        mm(PB[3][:, 0:32], cm('strict'), dta[:, 32:64], True, True, ['cst', 'dta'], bk(3, [0]))
        act(dstok[:], PB[3][:, 0:32], AF.Exp, bk(3, [0]), ['dstok'])
        tt('pool', rhsa[:], dta[:, 32:64].unsqueeze(2).to_broadcast([128, 32, 128]),
           cm('incl').unsqueeze(1).to_broadcast([128, 32, 128]), ALU.mult, ['dta', 'cst'], ['rhsa'])
        yield

    def consumer_rw(idx):
        def f(pp, pk):
            ri = cnt['rawr'] % 8
            cnt['rawr'] += 1
            r_ = rawr[ri]
            rk_ = 'rawr%d' % ri
            act(r_[:, 1:1 + TB], pp, AF.Copy, pk, [rk_])
            cp('pool', r_[:, 0:1], hist_r[:, idx, :], ['hr%d' % idx], [rk_])
            yield
            ti_ = cnt['tmpr'] % 4
            cnt['tmpr'] += 1
            t_ = tmpr[ti_]
            tk_ = 'tmpr%d' % ti_
            ts('dve', t_[:], r_[:, 1:1 + TB], dcol('omm', idx), None, ALU.mult, None, [rk_, 'der'], [tk_])
            yield
            stt(sh[:, idx, :], r_[:, 0:TB], pcol('mu', idx), t_[:], ALU.mult, ALU.add, [rk_, tk_, 'par'], ['sh%d' % idx])
            cp('pool', hist_r[:, idx, :], r_[:, TB:TB + 1], [rk_], ['hr%d' % idx])
            yield
        return f

    def consumer_gate(i):
        def f(pp, pk):
            act(G[:, i, :], pp, AF.Sigmoid, pk + ['par'], ['G%d' % i], bias=pcol('bg', i))
            yield
        return f

    glist = []
    glist.append((O_DT, 32, consumer_dt))
    for c in range(16):
        glist.append((O_X + c * 128, 128, consumer_xbc(c)))
    for g in range(4):
        glist.append((O_B + g * 128, 128, consumer_xbc(16 + g)))
    for g in range(4):
        glist.append((O_C + g * 128, 128, consumer_xbc(20 + g)))
    for c in range(16):
        glist.append((O_Z + c * 128, 128, consumer_z(c)))
    n_ssm_groups = len(glist)
    glist.append((O_WA, 128, consumer_rw(32)))
    for c in range(8):
        for i, o in enumerate([O_R, O_K, O_V, O_G]):
            glist.append((o + c * 128, 128, consumer_rw(i * 8 + c)))
    n_rw_groups = len(glist)
    for i in range(16):
        glist.append((O_GS + i * 128, 128, consumer_gate(i)))
    NG = len(glist)
    if stage <= 1:
        g_end = n_ssm_groups
    elif stage == 2:
        g_end = n_rw_groups
    else:
        g_end = NG
    sched = [(b, g) for b in range(nblk) for g in range(g_end)]
    pos = {bg: i for i, bg in enumerate(sched)}

    def issue_w(si):
        if si >= len(sched):
            return
        col0, M, _ = glist[sched[si][1]]
        s = si % NW
        ch = chunk_of(col0)
        if M == 128:
            ld(wt[s][:, :, :], win_bf[ch], 'wt%d' % s, reads=['wbf'])
        else:
            ld(wt[s][:, :, 0:M], win_bf[ch, :, :, 0:M], 'wt%d' % s, reads=['wbf'])

    def chunk_of(col0):
        if col0 < O_DT:
            return col0 // 128
        if col0 == O_DT:
            return 40
        return 41 + (col0 - O_R) // 128

    GL = 4
    INTERLEAVE = False

    def run_groups(blk, g0, g1):
        for _ in run_groups_gen(blk, g0, g1):
            pass

    def run_groups_gen(blk, g0, g1):
        g = g0
        pending = []
        while g < g1:
            gens = []
            for gg in range(g, min(g + GL, g1)):
                si = pos[(blk, gg)]
                issue_w(si + NW - 1)
                col0, M, cons = glist[gg]
                s = si % NW
                q = [0, 1, 4, 5][si % 4]
                pp = PB[q][0:M, 0:TB]
                pk = bk(q)
                for kc in range(8):
                    mm(pp, wt[s][:, kc, 0:M], hT[:, kc, :], kc == 0, kc == 7, ['wt%d' % s, 'hT%d' % kc], pk)
                gens.append(cons(pp, pk))
            live = []
            for gn in gens:
                try:
                    next(gn)
                    live.append(gn)
                except StopIteration:
                    pass
            roundrobin(pending)
            pending = live
            g += GL
            yield
        roundrobin(pending)
        yield

    NB = 4
    stg = [arena[:, i * 2048:(i + 1) * 2048] for i in range(NB)]
    stb = [arena[:, 8192 + i * 1024:8192 + (i + 1) * 1024].bitcast(BF16) for i in range(NB)]
    wst_keys = []
    slabs = []

    def convert(src, nrc, segs, dst):
        for rc in range(nrc):
            for (c0, w, ch0) in segs:
                slabs.append((src, rc, c0, w, ch0, dst))

    segs_in = [(0, 2048, 0), (2048, 2048, 16), (4096, 1024, 32), (O_DT, 32, 40), (O_R, 2048, 41), (O_R + 2048, 2048, 57),
               (O_R + 4096, 2048, 73), (O_R + 6144, 128, 89)]
    convert(win_d, 8, segs_in, win_bf)
    convert(wssm_d, 16, [(0, 1024, 0)], wssm_bf)
    convert(wrw_d, 8, [(0, 1024, 0)], wrw_bf)
    convert(wout_d, 8, [(0, 1024, 0)], wout_bf)

    def slab_ld(n):
        src, rc, c0, w, ch0, dst = slabs[n]
        i = n % NB
        P.dma('sp', lambda e: e.dma_start(out=stg[i][:, 0:w], in_=src[rc * 128:(rc + 1) * 128, c0:c0 + w]),
              writes=['stg%d' % i], semkey='stg%d' % i)

    def slab_cast_st(n):
        src, rc, c0, w, ch0, dst = slabs[n]
        i = n % NB
        eng = ['act', 'dve', 'pool'][n % 3]
        cp(eng, stb[i][:, 0:w], stg[i][:, 0:w], ['stg%d' % i], ['stb%d' % i])
        if w >= 128:
            nch = w // 128
            d_ap = dst[ch0:ch0 + nch, :, rc, :].rearrange("c p m -> p c m")
            s_ap = stb[i][:, 0:w].rearrange("p (c m) -> p c m", m=128)
        else:
            d_ap = dst[ch0, :, rc, 0:w]
            s_ap = stb[i][:, 0:w]
        k = 'wst%d' % n
        wst_keys.append(k)
        P.dma('sp', lambda e: e.dma_start(out=d_ap, in_=s_ap), reads=['stb%d' % i], writes=[k], semkey='wost%d' % i)

    for n in range(min(NB, len(slabs))):
        slab_ld(n)
    for n in range(len(slabs)):
        slab_cast_st(n)
        if n + NB < len(slabs):
            slab_ld(n + NB)
    P.add('pool', lambda e: e.memset(bar[:], 0.0), wst_keys, ['wbf', 'bar'])
    barrier(['stg%d' % i for i in range(NB)] + ['stb%d' % i for i in range(NB)], K_SSD)
    ld(wrwS[:], wrw_bf.rearrange("o p c m -> p o c m"), 'wrwS', reads=['wbf'])

    for si in range(NW - 1):
        issue_w(si)

    def dump(blk, off, ap, keys, width):
        i = P.dma('sp', lambda e: e.dma_start(out=dbg_d[blk, 0:ap.shape[0], off:off + width], in_=ap), reads=keys,
                  semkey='dbg%d_%d' % (blk, off))
        final_ops.append(i)

    def ssd_norm_gen(g):
        bn = [2, 3, 6, 7][g]
        sq = t1[g // 2][:].bitcast(BF16)[:, (g % 2) * TB:(g % 2 + 1) * TB]
        sqk = 't1_%d_%d' % (g // 2, g % 2)
        rs = tmpb[8 + g // 2][g % 2]
        rsk = 'tm%d_%d' % (8 + g // 2, g % 2)
        for q in range(4):
            c = g * 4 + q
            tt('pool', yz[:, c, :], yz[:, c, :], zs[:, c, :], ALU.mult, ['yz%d' % c, 'zs%d' % c], ['yz%d' % c])
            yield
        for q in range(4):
            c = g * 4 + q
            act(sq, yz[:, c, :], AF.Square, ['yz%d' % c], [sqk])
            mm(pq(bn, 0), cstb[:, 128:256], sq, q == 0, q == 3, [sqk, 'cstb'], bk(bn))
            yield
        act(rs[:], pq(bn, 0), AF.Ln, bk(bn), [rsk], scale=1.0 / 512, bias=EPS)
        yield
        act(rs[:], rs[:], AF.Exp, [rsk], [rsk], scale=-0.5)
        yield
        for q in range(4):
            c = g * 4 + q
            stt(yn[:, c, :], yz[:, c, :], pcol('sng', c), rs[:], ALU.mult, ALU.mult, ['yz%d' % c, rsk, 'par'], ['yn%d' % c])
            yield

    def ssd_core(blk):
        for c4 in range(4):
            bx = 2 if c4 % 2 == 0 else 6
            for q in range(4):
                c = c4 * 4 + q
                tr(pq(bx, q), xbc[:, c, :], ident, ['xbc%d' % c, 'cst'], bk(bx))
            tt('dve', Xdt[:, c4 * 512:(c4 + 1) * 512].rearrange("p (h d) -> p h d", h=8),
               PB[bx][:, :].rearrange("p (h d) -> p h d", h=8),
               dta[:, c4 * 8:(c4 + 1) * 8].unsqueeze(2).to_broadcast([128, 8, 64]), ALU.mult, bk(bx) + ['dta'], ['Xdt%d' % c4])
        for g in range(4):
            tr(pq(7, g), xbc[:, 16 + g, :], ident, ['xbc%d' % (16 + g), 'cst'], bk(7))
        act(Btok[:].rearrange("p g n -> p (g n)"), PB[7][:, :], AF.Copy, bk(7), ['Btok'])
        for c4 in range(4):
            tt('dve', Xds[:, c4 * 512:(c4 + 1) * 512].rearrange("p (h d) -> p h d", h=8),
               Xdt[:, c4 * 512:(c4 + 1) * 512].rearrange("p (h d) -> p h d", h=8),
               dstok[:, c4 * 8:(c4 + 1) * 8].unsqueeze(2).to_broadcast([128, 8, 64]), ALU.mult, ['Xdt%d' % c4, 'dstok'], ['Xds%d' % c4])
        for g in range(4):
            mm(pq(3, g), xbc[:, 16 + g, :], xbc[:, 20 + g, :], True, True, ['xbc%d' % (16 + g), 'xbc%d' % (20 + g)], bk(3, [g]))
        tt('dve', scm[:], PB[3][:, :].rearrange("p (g l) -> p g l", g=4), cm('incl').unsqueeze(1).to_broadcast([128, 4, 128]),
           ALU.mult, bk(3) + ['cst'], ['scm'])
        halves = [(g, hf) for g in range(4) for hf in range(2)]

        def stage1(i):
            g, hf = halves[i]
            u = i % 2
            h0 = g * 8 + hf * 4
            bs, be = (4, 5) if u == 0 else (0, 1)
            rv = rhsa[:, h0:h0 + 4, :].rearrange("p h l -> p (h l)")
            mm(PB[bs][:, :], cstb[:, 0:128], rv, True, True, ['cstb', 'rhsa'], bk(bs))
            mm(PB[be][:, :], cstb[:, 128:256], rv, True, True, ['cstb', 'rhsa'], bk(be))
            act(dec[u][:].rearrange("p h l -> p (h l)"), PB[bs][:, :], AF.Exp, bk(bs), ['dec%d' % u])
            act(Eg[u][:].rearrange("p h l -> p (h l)"), PB[be][:, :], AF.Exp, bk(be), ['Eg%d' % u])
            tt('dve', mix[u][:], dec[u][:], scm[:, g, :].unsqueeze(1).to_broadcast([128, 4, 128]), ALU.mult,
               ['dec%d' % u, 'scm'], ['mix%d' % u])
            tt('pool', Cdec[u][:], Eg[u][:], xbc[:, 20 + g, :].unsqueeze(1).to_broadcast([128, 4, 128]), ALU.mult,
               ['Eg%d' % u, 'xbc%d' % (20 + g)], ['Cdec%d' % u])

        def stage2(i):
            g, hf = halves[i]
            u = i % 2
            h0 = g * 8 + hf * 4
            by = 6 if g % 2 == 0 else 2
            for j in range(4):
                h = h0 + j
                c = h // 2
                half = h % 2
                q = c % 4
                yo = pq(by, q, half * 64, (half + 1) * 64)
                mm(yo, Xdt[:, h * 64:(h + 1) * 64], mix[u][:, j, :], True, False, ['Xdt%d' % (h // 8), 'mix%d' % u], bk(by))
                mm(yo, Sst[:, h * 64:(h + 1) * 64], Cdec[u][:, j, :], False, True, ['S%d' % g, 'Cdec%d' % u], bk(by))
            mm(PB[7][:, 0:256], Btok[:, g, :], Xds[:, h0 * 64:(h0 + 4) * 64], True, True, ['Btok', 'Xds%d' % g], bk(7))
            sv = Sst[:, h0 * 64:(h0 + 4) * 64]
            tt('dve', sv.rearrange("p (h d) -> p h d", h=4), sv.rearrange("p (h d) -> p h d", h=4),
               Eg[u][:, :, 127:128].to_broadcast([128, 4, 64]), ALU.mult, ['S%d' % g, 'Eg%d' % u], ['S%d' % g])
            tt('dve', sv, sv, PB[7][:, 0:256], ALU.add, ['S%d' % g] + bk(7), ['S%d' % g])

        def stage3(g):
            by = 6 if g % 2 == 0 else 2
            for q in range(4):
                c = g * 4 + q
                stt(yz[:, c, :], xbc[:, c, :], pcol('dsk', c), pq(by, q), ALU.mult, ALU.add, ['xbc%d' % c, 'par'] + bk(by), ['yz%d' % c])

        stage1(0)
        pend3 = None
        for i in range(8):
            if i + 1 < 8:
                stage1(i + 1)
            stage2(i)
            if pend3 is not None:
                stage3(pend3)
                pend3 = None
            if halves[i][1] == 1:
                pend3 = halves[i][0]
        stage3(pend3)

    def front_p1():
        act(sh[0:64, 32, :], sh[0:64, 32, :], AF.Tanh, ['sh32'], ['sh32'])
        for c in range(8):
            T_ = [tmpb[k][c % 2] for k in range(NTMP)]
            TK = ['tm%d_%d' % (k, c % 2) for k in range(NTMP)]
            bw, ba = (2, 3) if c % 2 == 0 else (6, 7)
            mm(pq(bw, 0), w2a2[0:64, c * 128:(c + 1) * 128], sh[0:64, 32, :], True, True, ['w2a2', 'sh32'], bk(bw))
            mm(pq(ba, 0), w2a2[64:128, c * 128:(c + 1) * 128], sh[64:128, 32, :], True, True, ['w2a2', 'sh32'], bk(ba))
            act(T_[0][:], pq(bw, 0), AF.Sigmoid, bk(bw) + ['par'], [TK[0]], bias=pcol('w0', c))
            ts('dve', lw[:, c, :], T_[0][:], LWS, None, ALU.mult, None, [TK[0]], ['lw%d' % c])
            act(av[:, c, :], pq(ba, 0), AF.Sigmoid, bk(ba) + ['par'], ['av%d' % c], bias=pcol('a0', c))

    def front_silu():
        for c in range(8):
            act(sh[:, 24 + c, :], sh[:, 24 + c, :], AF.Silu, ['sh%d' % (24 + c)], ['sh%d' % (24 + c)])

    def front_c(c):
        if True:
            T_ = [tmpb[k][c % 2] for k in range(NTMP)]
            TK = ['tm%d_%d' % (k, c % 2) for k in range(NTMP)]
            r_, k_, v_, g_ = sh[:, c, :], sh[:, 8 + c, :], sh[:, 16 + c, :], sh[:, 24 + c, :]
            rk_, kk_, vk_ = 'sh%d' % c, 'sh%d' % (8 + c), 'sh%d' % (16 + c)
            ts('dve', T_[1][:], k_, pcol('kk', c), None, ALU.mult, None, [kk_, 'par'], [TK[1]])
            yield
            T2b = T_[2][:].bitcast(BF16)[:, 0:TB]
            act(T2b, T_[1][:], AF.Square, [TK[1]], [TK[2]])
            yield
            mm(pq(6, c % 2), cstb[:, 256:384], T2b, True, True, ['cstb', TK[2]], bk(6))
            yield
            ts('dve', T_[2][:], pq(6, c % 2), 1e-24, None, ALU.max, None, bk(6), [TK[2]])
            yield
            act(T_[2][:], T_[2][:], AF.Ln, [TK[2]], [TK[2]])
            yield
            act(T_[2][:], T_[2][:], AF.Exp, [TK[2]], [TK[2]], scale=-0.5)
            yield
            tt('dve', kkb[:, c, :], T_[1][:], T_[2][:], ALU.mult, [TK[1], TK[2]], ['kkb%d' % c])
            yield
            ts('dve', T_[3][:], av[:, c, :], pcol('ka', c), dcol('omka', c), ALU.mult, ALU.add, ['av%d' % c, 'par', 'der'], [TK[3]])
            yield
            tt('dve', kp[:, c, :], k_, T_[3][:], ALU.mult, [kk_, TK[3]], ['kp%d' % c])
            yield
            stt(T_[4][:], r_, pcol('rk', c), kp[:, c, :], ALU.mult, ALU.mult, [rk_, 'par', 'kp%d' % c], [TK[4]])
            yield
            mm(pq(7, c % 2), cm('blk'), T_[4][:], True, True, ['cst', TK[4]], bk(7))
            yield
            tt('dve', bon[:, c, :], pq(7, c % 2), v_, ALU.mult, bk(7) + [vk_], ['bon%d' % c])
            yield
            P.add('dve', lambda e, o=T_[5][:], m=cm('rmask'), l=lw[:, c, :]: e.tensor_tensor_scan(out=o, data0=m, data1=l, initial=0.0,
                                                                                                 op0=ALU.mult, op1=ALU.add),
                  ['cst', 'lw%d' % c], [TK[5]])
            yield
            cum = T_[5]
            act(EP[:, c, :], cum[:], AF.Exp, [TK[5]], ['EP%d' % c])
            yield
            act(T_[6][:], cum[:], AF.Exp, [TK[5]], [TK[6]], scale=-1.0)
            yield
            tt('dve', T_[7][:], cum[:], lw[:, c, :], ALU.subtract, [TK[5], 'lw%d' % c], [TK[7]])
            yield
            act(T_[7][:], T_[7][:], AF.Exp, [TK[7]], [TK[7]])
            yield
            c3 = cum[:].rearrange("p (j t) -> p j t", j=2)
            tt('dve', T_[8][:].rearrange("p (j t) -> p j t", j=2), c3[:, :, 63:64].to_broadcast([128, 2, 64]), c3, ALU.subtract,
               [TK[5]], [TK[8]])
            yield
            act(T_[8][:], T_[8][:], AF.Exp, [TK[8]], [TK[8]])
            yield

            def v3(ap):
                return ap.rearrange("p (j t) -> p j t", j=2)
            tt('dve', RA[:, c, :, 64:128], v3(r_), v3(EP[:, c, :]), ALU.mult, [rk_, 'EP%d' % c], ['RA%d' % c])
            yield
            stt(RA[:, c, :, 0:64], v3(kkb[:, c, :]), -1.0, v3(T_[7][:]), ALU.mult, ALU.mult, ['kkb%d' % c, TK[7]], ['RA%d' % c])
            yield
            tt('dve', T_[9][:], kkb[:, c, :], av[:, c, :], ALU.mult, ['kkb%d' % c, 'av%d' % c], [TK[9]])
            yield
            tt('dve', BKt[:, c, :, 0:64], v3(T_[9][:]), v3(T_[6][:]), ALU.mult, [TK[9], TK[6]], ['BKt%d' % c])
            yield
            tt('pool', BKt[:, c, :, 128:192], v3(T_[9][:]), v3(T_[6][:]), ALU.mult, [TK[9], TK[6]], ['BKt%d' % c])
            yield
            tt('dve', BKt[:, c, :, 64:128], v3(kp[:, c, :]), v3(T_[6][:]), ALU.mult, ['kp%d' % c, TK[6]], ['BKt%d' % c])
            yield
            tt('pool', BKh[:, c, :, 0:64], v3(T_[9][:]), v3(T_[8][:]), ALU.mult, [TK[9], TK[8]], ['BKh%d' % c])
            yield
            tt('pool', BKh[:, c, :, 128:192], v3(T_[9][:]), v3(T_[8][:]), ALU.mult, [TK[9], TK[8]], ['BKh%d' % c])
            yield
            tt('pool', BKh[:, c, :, 64:128], v3(kp[:, c, :]), v3(T_[8][:]), ALU.mult, ['kp%d' % c, TK[8]], ['BKh%d' % c])
            yield
            cp('pool', VV[:, c, :, 64:128], v3(v_), [vk_], ['VV%d' % c])
            yield

    def rwkv_core(blk):
        for c2 in range(4):
            for hh in range(2):
                pr = hh * 64
                po = 64 - pr
                ph = po // 64
                off = 0 if hh == 1 else 64
                sl = slice(po, po + 64)
                idh = cst[pr:pr + 64, CC['ident'] + pr:CC['ident'] + pr + 64]
                ido = cst[po:po + 64, CC['ident'] + po:CC['ident'] + po + 64]
                RES = {}

                def inv_gen(cc, j):
                    c = 2 * c2 + cc
                    ui = cc * 2 + j
                    am = AM[ui]
                    amk = 'AM%d' % ui
                    bkt, uvh = BKtok[ui], UV[ui]
                    bktk, uvk_v = 'BKtok%d' % ui, 'UVv%d' % ui
                    cs = ui * 64

                    def reg(bn):
                        return PB[bn][po:po + 64, cs:cs + 64], bk(bn, halves=[ph])
                    tr(PB[0][:, ui * 128:ui * 128 + 64], BKh[pr:pr + 64, c, j, off:off + 128], idh, ['BKh%d' % c, 'cst'], bk(0))
                    yield
                    tr(PB[0][:, ui * 128 + 64:ui * 128 + 128], VV[pr:pr + 64, c, j, off:off + 128], idh, ['VV%d' % c, 'cst'], bk(0))
                    yield
                    mm(pq(1, ui), BKt[pr:pr + 64, c, j, off:off + 128], RA[pr:pr + 64, c, j, :], True, True, ['BKt%d' % c, 'RA%d' % c], bk(1))
                    yield
                    r0, k0 = reg(6)
                    mm(r0, RA[pr:pr + 64, c, j, 0:64], BKt[pr:pr + 64, c, j, 0:64], True, True, ['BKt%d' % c, 'RA%d' % c], k0)
                    yield
                    cp('act', bkt[:], PB[0][:, ui * 128:ui * 128 + 64], bk(0), [bktk])
                    yield
                    cp('act', uvh[pr:pr + 64, :], PB[0][pr:pr + 64, ui * 128 + 64:ui * 128 + 128], bk(0), [uvk_v])
                    yield
                    tt('dve', am[:], pq(1, ui), cm('m4'), ALU.mult, bk(1) + ['cst'], [amk])
                    yield
                    p0, pt0, rt = Pm[ui], Ptm[ui], Rt[ui]
                    pk0 = ['Pm%d_%d' % (ui, k) for k in range(2)]
                    ptk = ['Ptm%d_%d' % (ui, k) for k in range(2)]
                    rtk = ['Rt%d_%d' % (ui, k) for k in range(2)]
                    tt('dve', p0[0][sl, :], r0, cm('low', 64, po, po + 64), ALU.mult, k0 + ['cst'], [pk0[0]])
                    yield
                    def bfv(t_):
                        return t_[:].bitcast(BF16)[sl, 0:64]
                    tt('pool', bfv(rt[0]), am[sl, 0:64], ido, ALU.add, [amk, 'cst'], [rtk[0]])
                    yield
                    Pcur, Pck = p0[0][sl, :], pk0[0]
                    Ptcur, Ptck = am[sl, 0:64], amk
                    Rcur, Rck = bfv(rt[0]), rtk[0]
                    rP, kP = reg(2)
                    rPt, kPt = reg(3)
                    rR, kR = reg(4)
                    for lvl in range(1, 6):
                        nb = lvl % 2
                        mm(rP, Ptcur, Pcur, True, True, [Ptck, Pck], kP)
                        yield
                        if lvl < 5:
                            mm(rPt, Pcur, Ptcur, True, True, [Ptck, Pck], kPt)
                            yield
                        cp('act', bfv(p0[nb]), rP, kP, [pk0[nb]])
                        yield
                        if lvl < 5:
                            cp('act', bfv(pt0[nb]), rPt, kPt, [ptk[nb]])
                            yield
                        mm(rR, bfv(p0[nb]), Rcur, True, True, [pk0[nb], Rck], kR)
                        yield
                        rout = rt[nb][sl, :] if lvl == 5 else bfv(rt[nb])
                        tt('dve', rout, Rcur, rR, ALU.add, [Rck] + kR, [rtk[nb]])
                        yield
                        Pcur, Pck = bfv(p0[nb]), pk0[nb]
                        if lvl < 5:
                            Ptcur, Ptck = bfv(pt0[nb]), ptk[nb]
                        Rcur, Rck = rout, rtk[nb]
                    RES[(cc, j)] = (Rcur, Rck)

                def seq_gen(cc, j):
                    c = 2 * c2 + cc
                    ui = cc * 2 + j
                    am = AM[ui]
                    amk = 'AM%d' % ui
                    bkt, uvh, xs = BKtok[ui], UV[ui], X1s[ui]
                    bktk, uvk_v, uvk_u, xsk = 'BKtok%d' % ui, 'UVv%d' % ui, 'UVu%d' % ui, 'X1s%d' % ui
                    Rcur, Rck = RES[(cc, j)]
                    srk = 'SR%d_%d' % (c, hh)
                    srv = SR[pr:pr + 64, c, :]
                    cs = ui * 64
                    rX, kX = PB[5][po:po + 64, cs:cs + 64], bk(5, halves=[ph])
                    rU, kU = PB[6][po:po + 64, cs:cs + 64], bk(6, halves=[ph])
                    yk = bk(7, halves=[hh])
                    mm(rX, RA[pr:pr + 64, c, j, 0:64], srv, True, False, ['RA%d' % c, srk], kX)
                    mm(rX, am[pr:pr + 64, 0:64], uvh[pr:pr + 64, :], False, True, [amk, uvk_v], kX)
                    yield
                    cp('act', xs[sl, :], rX, kX, [xsk])
                    yield
                    mm(rU, Rcur, xs[sl, :], True, True, [Rck, xsk], kU)
                    yield
                    cp('dve', uvh[sl, :], rU, kU, [uvk_u])
                    yield
                    uvk = [uvk_v, uvk_u]
                    mm(PB[7][pr:pr + 64, cs:cs + 64], srv, RA[pr:pr + 64, c, j, 64:128], True, True, [srk, 'RA%d' % c], yk)
                    yield
                    mm(PB[7][pr:pr + 64, 256 + cs:256 + cs + 64], uvh[:, :], am[:, 64:128], True, True, uvk + [amk], yk)
                    yield
                    so = PB[5][pr:pr + 64, cs:cs + 64]
                    sok = bk(5, halves=[hh])
                    mm(so, bkt[:, :], uvh[:, :], True, True, [bktk] + uvk, sok)
                    yield
                    ts('pool', srv, srv, EP[pr:pr + 64, c, j * 64 + 63:j * 64 + 64], None, ALU.mult, None, [srk, 'EP%d' % c], [srk])
                    yield
                    tt('dve', srv, srv, so, ALU.add, [srk] + sok, [srk])
                    yield

                roundrobin([inv_gen(cc, j) for cc in range(2) for j in range(2)])
                for j in range(2):
                    roundrobin([seq_gen(cc, j) for cc in range(2)])
                hs = slice(pr, pr + 64)
                for cc in range(2):
                    c = 2 * c2 + cc
                    cp('act', ybuf[hs, c, :], PB[7][hs, cc * 128:cc * 128 + 128], bk(7, halves=[hh]), ['ybuf%d' % c])
                    tt('dve', ybuf[hs, c, :], ybuf[hs, c, :], PB[7][hs, 256 + cc * 128:256 + cc * 128 + 128], ALU.add,
                       ['ybuf%d' % c] + bk(7, halves=[hh]), ['ybuf%d' % c])

    def gn_gen(c):
        T0, T1 = tmpb[c][0], tmpb[c][1]
        K0, K1 = 'tm%d_0' % c, 'tm%d_1' % c
        bm, qm = 2 + c // 4, c % 4
        bv = 6 + c // 4
        mm(pq(bm, qm), cm('blk'), ybuf[:, c, :], True, True, ['cst', 'ybuf%d' % c], bk(bm))
        yield
        stt(T0[:], pq(bm, qm), -1.0 / 64, ybuf[:, c, :], ALU.mult, ALU.add, bk(bm) + ['ybuf%d' % c], [K0])
        yield
        T1b = T1[:].bitcast(BF16)[:, 0:TB]
        act(T1b, T0[:], AF.Square, [K0], [K1])
        yield
        mm(pq(bv, qm), cstb[:, 256:384], T1b, True, True, ['cstb', K1], bk(bv))
        yield
        act(T1[:], pq(bv, qm), AF.Ln, bk(bv), [K1], scale=1.0 / 64, bias=GN_EPS)
        yield
        act(T1[:], T1[:], AF.Exp, [K1], [K1], scale=-0.5)
        yield
        tt('dve', T0[:], T0[:], T1[:], ALU.mult, [K0, K1], [K0])
        yield
        ts('dve', T0[:], T0[:], pcol('gng', c), pcol('gnb', c), ALU.mult, ALU.add, [K0, 'par'], [K0])
        yield
        tt('dve', T0[:], T0[:], bon[:, c, :], ALU.add, [K0, 'bon%d' % c], [K0])
        yield
        tt('dve', yr[:, c, :], T0[:], sh[:, 24 + c, :], ALU.mult, [K0, 'sh%d' % (24 + c)], ['yr%d' % c])
        yield

    WT = []
    for o_ in range(8):
        WT.append(wssm_bf[o_, :, 0:8, :])
        WT.append(wssm_bf[o_, :, 8:16, :])
    for eo_ in range(8):
        WT.append(wout_bf[eo_])

    def wo_issue(t):
        if t >= 24:
            return
        ld(wob[:, t % 4], WT[t], 'wo%d' % (t % 4), reads=['wbf'])

    def out_prefetch():
        for t in range(3):
            wo_issue(t)

    def out_gen(blk, xs_, xk):
        for o in range(8):
            for hf in range(2):
                t = 2 * o + hf
                wo_issue(t + 3)
                for c8 in range(8):
                    c = hf * 8 + c8
                    mm(pq(2 + o % 2, 0), wob[:, t % 4, c8, :], yn[:, c, :], c == 0, c == 15, ['wo%d' % (t % 4), 'yn%d' % c], bk(2 + o % 2))
            tt('dve', tmpb[3][o % 2][:], pq(2 + o % 2, 0), G[:, o, :], ALU.mult, bk(2 + o % 2) + ['G%d' % o], ['tm3_%d' % (o % 2)])
            for c in range(8):
                mm(pq(6 + o % 2, 0), wrwS[:, o, c, :], yr[:, c, :], c == 0, c == 7, ['wrwS', 'yr%d' % c], bk(6 + o % 2))
            T_ = tmpb[0][o % 2]
            tk = 'tm0_%d' % (o % 2)
            tt('dve', T_[:], pq(6 + o % 2, 0), G[:, 8 + o, :], ALU.mult, bk(6 + o % 2) + ['G%d' % (8 + o)], [tk])
            tt('dve', mT[:, o, :], tmpb[3][o % 2][:], T_[:], ALU.add, ['tm3_%d' % (o % 2), tk], ['mT%d' % o])
            yield
        for eo in range(8):
            t = 16 + eo
            wo_issue(t + 3)
            for o in range(8):
                mm(pq(2 + eo % 2, 0), wob[:, t % 4, o, :], mT[:, o, :], o == 0, o == 7, ['wo%d' % (t % 4), 'mT%d' % o], bk(2 + eo % 2))
            cp('act', oT[:, eo, :], pq(2 + eo % 2, 0), bk(2 + eo % 2), ['oT%d' % eo])
            T_ = tmpb[1][eo % 2]
            tk = 'tm1_%d' % (eo % 2)
            Tb = T_[:].bitcast(BF16)[:, 0:TB]
            act(Tb, oT[:, eo, :], AF.Square, ['oT%d' % eo], [tk])
            mm(pq(6, 0), cstb[:, 128:256], Tb, eo == 0, eo == 7, ['cstb', tk], bk(6))
            yield
        T_ = tmpb[2][0]
        tk = 'tm2_0'
        act(T_[:], pq(6, 0), AF.Ln, bk(6), [tk], scale=1.0 / D, bias=EPS)
        act(T_[:], T_[:], AF.Exp, [tk], [tk], scale=-0.5)
        for eo in range(8):
            stt(on[:, eo, :], oT[:, eo, :], pcol('post', eo), T_[:], ALU.mult, ALU.mult, ['oT%d' % eo, 'par', tk], ['on%d' % eo])
        yield
        for hb in range(2):
            for q_ in range(4):
                eo = hb * 4 + q_
                tr(pq(7, q_), on[:, eo, :], ident, ['on%d' % eo, 'cst'], bk(7))
            fin = xn[:, hb * 512:(hb + 1) * 512]
            tt('dve', fin, PB[7][:, :], xs_[:, hb * 512:(hb + 1) * 512], ALU.add, bk(7) + [xk], ['xn'])
            i = P.dma('sp', lambda e, hb=hb, fin=fin: e.dma_start(out=out_d[blk * TB:(blk + 1) * TB, hb * 512:(hb + 1) * 512], in_=fin),
                      reads=['xn'], semkey='fin%d' % hb)
            final_ops.append(i)
            yield

    def ld_x(blk):
        ld(xt[blk % 2][:], x_d[blk * TB:(blk + 1) * TB, :], 'xt%d' % (blk % 2))

    def stageA_gen(blk, do_ld=True):
        xs_ = xt[blk % 2]
        xk = 'xt%d' % (blk % 2)
        if do_ld:
            ld_x(blk)
        act(xn[:], xs_[:], AF.Square, [xk], ['ssq', 'xn'], accum_out=ssq[:])
        act(rstd[:], ssq[:], AF.Ln, ['ssq'], ['rstd'], scale=1.0 / D, bias=EPS)
        act(rstd[:], rstd[:], AF.Exp, ['rstd'], ['rstd'], scale=-0.5)
        ts('dve', xn[:], xs_[:], rstd[:], None, ALU.mult, None, [xk, 'rstd'], ['xn'])
        yield
        for c in range(8):
            b_, q_ = c // 4, c % 4
            tr(pq(b_, q_), xn[:, c * 128:(c + 1) * 128], ident, ['xn', 'cst'], bk(b_))
        for c in range(8):
            b_, q_ = c // 4, c % 4
            act(hT[:, c, :], pq(b_, q_), AF.Copy, bk(b_) + ['par'], ['hT%d' % c], scale=pcol('pre', c))
        yield

    def front_gen(blk, do_a=True):
        if do_a:
            for _ in stageA_gen(blk):
                yield
        if blk > 0:
            barrier(K_RW, K_SSD_E)
        for _ in run_groups_gen(blk, 0, n_ssm_groups):
            yield

    for _ in front_gen(0):
        pass
    for blk in range(nblk):
        xs_ = xt[blk % 2]
        xk = 'xt%d' % (blk % 2)
        if blk > 0:
            barrier(K_OUT + K_RW, K_SSD_R)
        ssd_core(blk)
        roundrobin([ssd_norm_gen(g_) for g_ in range(4)])
        barrier(K_SSD, K_RW)
        bi = 0
        nfc = 0
        for _ in run_groups_gen(blk, n_ssm_groups, n_rw_groups):
            if bi == 1:
                front_p1()
            elif bi >= 2 and nfc < 8 and nfc <= bi - 3:
                roundrobin([front_c(nfc), front_c(nfc + 1)])
                nfc += 2
            bi += 1
        while nfc < 8:
            roundrobin([front_c(nfc), front_c(nfc + 1)])
            nfc += 2
        front_silu()
        out_prefetch()
        if blk + 1 < nblk:
            ld_x(blk + 1)
        rwkv_core(blk)
        barrier(['BKt%d' % c_ for c_ in range(8)] + ['BKh%d' % c_ for c_ in range(8)], K_OUT)
        roundrobin([gn_gen(c_) for c_ in range(8)])
        run_groups(blk, n_rw_groups, NG)
        if INTERLEAVE:
            gens = [out_gen(blk, xs_, xk)]
            if blk + 1 < nblk:
                gens.append(front_gen(blk + 1))
            roundrobin(gens)
        else:
            gens = [out_gen(blk, xs_, xk)]
            if blk + 1 < nblk:
                gens.append(stageA_gen(blk + 1, do_ld=False))
            roundrobin(gens)
            if blk + 1 < nblk:
                roundrobin([front_gen(blk + 1, do_a=False)])

    P.emit(final_wait_ops=final_ops)
    return nc, P


def pack_params(inp):
    p = np.zeros((128, NPAR), np.float32)

    def cmaj(v):
        v = np.asarray(v, np.float32).reshape(-1, 128)
        return v.T

    def put(name, arr):
        p[:arr.shape[0], PC[name]:PC[name] + arr.shape[1]] = arr

    put('pre', cmaj(inp['pre_gain'][0]))
    put('post', cmaj(inp['post_gain'][0]))
    cw = np.asarray(inp['conv_w'][0], np.float32)
    cwp = np.zeros((128, 24, 4), np.float32)
    for k in range(4):
        cwp[:, :, k] = cmaj(cw[k])
    put('cw', cwp.reshape(128, 96))
    put('cb', cmaj(inp['conv_b'][0]))
    put('dsk', cmaj(np.repeat(np.asarray(inp['d_skip'][0], np.float32), 64)))
    put('sng', cmaj(inp['ssm_norm_gain'][0]))
    put('mu', cmaj(inp['rwkv_mu'][0]))
    put('w0', cmaj(inp['decay_w0'][0]))
    put('a0', cmaj(inp['iclr_a0'][0]))
    put('kk', cmaj(inp['k_k'][0]))
    put('ka', cmaj(inp['k_a'][0]))
    put('rk', cmaj(np.asarray(inp['r_k'][0], np.float32).reshape(-1)))
    put('gng', cmaj(inp['gn_gain'][0]))
    put('gnb', cmaj(inp['gn_bias'][0]))
    put('bg', cmaj(inp['b_gate'][0]))
    p[:32, PC['dtb']] = np.asarray(inp['dt_bias'][0], np.float32)
    p[:32, PC['alog']] = np.asarray(inp['a_log'][0], np.float32)
    return p


_CACHE = {}


def make_in_maps(inp, nblk=NBLK, ncores=4):
    par = pack_params(inp)
    cst = make_consts()
    w2a2 = np.concatenate([np.asarray(inp['decay_w2'][0], np.float32), np.asarray(inp['iclr_a2'][0], np.float32)], axis=0)
    shared = {
        "w_in": np.ascontiguousarray(np.asarray(inp['w_in'][0], np.float32)),
        "par": par, "cst": cst, "w2a2": np.ascontiguousarray(w2a2),
        "w_ssm": np.ascontiguousarray(np.asarray(inp['w_branch_ssm'][0], np.float32)),
        "w_rwkv": np.ascontiguousarray(np.asarray(inp['w_branch_rwkv'][0], np.float32)),
        "w_out": np.ascontiguousarray(np.asarray(inp['w_out'][0], np.float32)),
    }
    x = np.asarray(inp['x'], np.float32)
    maps = []
    for b in range(ncores):
        m = dict(shared)
        m["x"] = np.ascontiguousarray(x[b, :nblk * TB])
        maps.append(m)
    return maps


def kernel(**inputs):
    if 'nc' not in _CACHE:
        _CACHE['nc'] = build()[0]
    nc = _CACHE['nc']
    maps = make_in_maps(inputs)
    res = run_bass_kernel_spmd(nc, maps, core_ids=list(range(4)))
    out = np.stack([np.asarray(r["out"], np.float32) for r in res.results], axis=0)
    return out
```
